# Optimizing a Trainium2 kernel written in Bass

```python
import jax, jax.numpy as jnp
from jax import lax
import numpy as np


D_MODEL = 1024
BATCH = 8
SEQ = 8192
DEPTH = 1
DEC_BATCH = 8
DEC_SEQ = 2048
PAST_LEN = 128

GRID_W = 64
PLE_DIM = 256
POOL_WIDTH = 1024
POOL_GROUPS = 4
POOL_GROUP_WIDTH = POOL_WIDTH // POOL_GROUPS
POOL_WINDOWS = (2, 4, 8, 16)
N_Q_HEADS = 16
N_KV_HEADS = 4
HEADS_PER_KV = N_Q_HEADS // N_KV_HEADS
HEAD_DIM = 64
Q_WIDTH = N_Q_HEADS * HEAD_DIM
KV_WIDTH = N_KV_HEADS * HEAD_DIM
ROPE_AXIS_DIM = HEAD_DIM // 2
ROPE_BASE = 10000.0
Q_BLOCK = 128
EPS = 1e-6
IN_SPLITS = (POOL_WIDTH, POOL_WIDTH, Q_WIDTH, KV_WIDTH, KV_WIDTH, Q_WIDTH, D_MODEL, D_MODEL)
IN_WIDTH = sum(IN_SPLITS)
SPLIT_POINTS = tuple(int(v) for v in np.cumsum(IN_SPLITS)[:-1])

kernel_name = 'hybrid_pool_gqa_encoder'


def rms_norm(x, g):
    xf = x.astype(jnp.float32)
    y = xf * lax.rsqrt(jnp.mean(xf * xf, axis=-1, keepdims=True) + EPS)
    return (y * g.astype(jnp.float32)).astype(x.dtype)


def rope_1d(x, pos):
    dim = x.shape[-1]
    freqs = ROPE_BASE ** (-jnp.arange(0, dim, 2, dtype=jnp.float32) / dim)
    ang = pos[:, None] * freqs[None, :]
    cos = jnp.cos(ang)[None, :, None, :]
    sin = jnp.sin(ang)[None, :, None, :]
    xf = x.astype(jnp.float32)
    x1, x2 = xf[..., : dim // 2], xf[..., dim // 2:]
    out = jnp.concatenate([x1 * cos - x2 * sin, x2 * cos + x1 * sin], axis=-1)
    return out.astype(x.dtype)


def axial_rope(x, pos_row, pos_col):
    return jnp.concatenate([rope_1d(x[..., :ROPE_AXIS_DIM], pos_row),
                            rope_1d(x[..., ROPE_AXIS_DIM:], pos_col)], axis=-1)


def pool_mixer(u, pool_w, pool_scale):
    B, T, _ = u.shape
    uf = u.astype(jnp.float32)
    c = jnp.concatenate([jnp.zeros((B, 1, POOL_WIDTH), jnp.float32), jnp.cumsum(uf, axis=1)], axis=1)
    t = jnp.arange(T)
    groups = []
    for gi, w in enumerate(POOL_WINDOWS):
        sl = slice(gi * POOL_GROUP_WIDTH, (gi + 1) * POOL_GROUP_WIDTH)
        lo = jnp.clip(t - w // 2, 0, T)
        hi = jnp.clip(t + w - w // 2, 0, T)
        cnt = (hi - lo).astype(jnp.float32)
        cg = c[:, :, sl]
        mean = (cg[:, hi] - cg[:, lo]) / cnt[None, :, None]
        groups.append(mean - uf[..., sl])
    pooled = jnp.stack(groups, axis=2).astype(u.dtype)
    mixed = jnp.einsum('btgc,gcd->btgd', pooled, pool_w).reshape(B, T, POOL_WIDTH)
    return mixed * pool_scale


def block_attention(q, k, v):
    B, T, _, _ = q.shape
    n_blk = T // Q_BLOCK
    scale = HEAD_DIM ** -0.5
    qb = q.reshape(B, n_blk, Q_BLOCK, N_KV_HEADS, HEADS_PER_KV, HEAD_DIM)
    qb = jnp.moveaxis(qb, 1, 0)

    def one_block(qi):
        s = jnp.einsum('bqkgd,bskd->bkgqs', qi, k, preferred_element_type=jnp.float32) * scale
        p = jax.nn.softmax(s, axis=-1)
        return jnp.einsum('bkgqs,bskd->bqkgd', p.astype(v.dtype), v)

    o = lax.map(one_block, qb)
    return jnp.moveaxis(o, 0, 1).reshape(B, T, Q_WIDTH)


def encoder_layer(x, p_i, norm_pre, w_in, pool_w, pool_scale, w_branch_a, q_norm, k_norm,
                  w_branch_b, w_out, norm_post, ple_norm, w_ple_gate, w_ple_in):
    B, T, _ = x.shape
    rows = T // GRID_W
    pos_row = jnp.repeat(jnp.arange(rows, dtype=jnp.float32), GRID_W)
    pos_col = jnp.tile(jnp.arange(GRID_W, dtype=jnp.float32), rows)

    h = rms_norm(x, norm_pre)
    u = h @ w_in
    ua, za, q, k, v, zb, ma, mb = jnp.split(u, SPLIT_POINTS, axis=-1)

    a = pool_mixer(ua, pool_w, pool_scale) * jax.nn.silu(za)
    a = a @ w_branch_a

    q = rms_norm(q.reshape(B, T, N_Q_HEADS, HEAD_DIM), q_norm)
    k = rms_norm(k.reshape(B, T, N_KV_HEADS, HEAD_DIM), k_norm)
    v = v.reshape(B, T, N_KV_HEADS, HEAD_DIM)
    q = axial_rope(q, pos_row, pos_col)
    k = axial_rope(k, pos_row, pos_col)
    b = block_attention(q, k, v) * jax.nn.silu(zb)
    b = b @ w_branch_b

    merged = jax.nn.sigmoid(ma) * a + jax.nn.sigmoid(mb) * b
    x = x + rms_norm(merged @ w_out, norm_post)

    gate = jax.nn.sigmoid(rms_norm(x, ple_norm) @ w_ple_gate)
    return x + gate * (p_i @ w_ple_in)


def trunk(x, p, norm_pre, w_in, pool_w, pool_scale, w_branch_a, q_norm, k_norm,
          w_branch_b, w_out, norm_post, ple_norm, w_ple_gate, w_ple_in):
    for i in range(DEPTH):
        x = encoder_layer(x, p[i], norm_pre[i], w_in[i], pool_w[i], pool_scale[i], w_branch_a[i],
                          q_norm[i], k_norm[i], w_branch_b[i], w_out[i], norm_post[i],
                          ple_norm[i], w_ple_gate[i], w_ple_in[i])
    return x


def setup_inputs(seed: int = 0) -> dict:
    key = jax.random.key(seed)
    ks = jax.random.split(key, 20)
    f32 = jnp.float32

    def nrm(k, shape, scale=1.0):
        return jax.random.normal(k, shape, f32) * scale

    return {
        'x_prompt': nrm(ks[0], (BATCH, SEQ, D_MODEL)),
        'x_sample': nrm(ks[1], (DEC_BATCH, DEC_SEQ, D_MODEL)),
        'p_prompt': nrm(ks[2], (DEPTH, BATCH, SEQ, PLE_DIM)),
        'p_sample': nrm(ks[3], (DEPTH, DEC_BATCH, DEC_SEQ, PLE_DIM)),
        'norm_pre': 1.0 + nrm(ks[4], (DEPTH, D_MODEL), 0.05),
        'w_in': nrm(ks[5], (DEPTH, D_MODEL, IN_WIDTH), D_MODEL ** -0.5),
        'pool_w': nrm(ks[6], (DEPTH, POOL_GROUPS, POOL_GROUP_WIDTH, POOL_GROUP_WIDTH), POOL_GROUP_WIDTH ** -0.5),
        'pool_scale': 1.0 + nrm(ks[7], (DEPTH, POOL_WIDTH), 0.05),
        'w_branch_a': nrm(ks[8], (DEPTH, POOL_WIDTH, D_MODEL), POOL_WIDTH ** -0.5),
        'q_norm': 1.0 + nrm(ks[9], (DEPTH, HEAD_DIM), 0.05),
        'k_norm': 1.0 + nrm(ks[10], (DEPTH, HEAD_DIM), 0.05),
        'w_branch_b': nrm(ks[11], (DEPTH, Q_WIDTH, D_MODEL), Q_WIDTH ** -0.5),
        'w_out': nrm(ks[12], (DEPTH, D_MODEL, D_MODEL), D_MODEL ** -0.5),
        'norm_post': 1.0 + nrm(ks[13], (DEPTH, D_MODEL), 0.05),
        'ple_norm': 1.0 + nrm(ks[14], (DEPTH, D_MODEL), 0.05),
        'w_ple_gate': nrm(ks[15], (DEPTH, D_MODEL, D_MODEL), D_MODEL ** -0.5),
        'w_ple_in': nrm(ks[16], (DEPTH, PLE_DIM, D_MODEL), PLE_DIM ** -0.5),
    }


def reference(x_prompt, x_sample, p_prompt, p_sample, norm_pre, w_in, pool_w, pool_scale, w_branch_a,
              q_norm, k_norm, w_branch_b, w_out, norm_post, ple_norm, w_ple_gate, w_ple_in):
    y_prompt = trunk(x_prompt, p_prompt, norm_pre, w_in, pool_w, pool_scale, w_branch_a, q_norm, k_norm,
                     w_branch_b, w_out, norm_post, ple_norm, w_ple_gate, w_ple_in)
    y_sample = trunk(x_sample, p_sample, norm_pre, w_in, pool_w, pool_scale, w_branch_a, q_norm, k_norm,
                     w_branch_b, w_out, norm_post, ple_norm, w_ple_gate, w_ple_in)
    return (y_prompt, y_sample)
```

```python
import numpy as np
from contextlib import ExitStack
import concourse.bass as bass
import concourse.mybir as mybir
from concourse.bass_utils import run_bass_kernel_spmd

F32 = mybir.dt.float32
BF16 = mybir.dt.bfloat16
AF = mybir.ActivationFunctionType
ALU = mybir.AluOpType

D = 1024
PLE = 256
EPS = 1e-6
ST = 512
NSLOT = 22
(S_UAZA, S_POOL, S_MA, S_WA, S_Q, S_ZB, S_MB, S_WB, S_WO, S_WG, S_WPLE) = (0, 4, 5, 7, 9, 11, 13, 15, 17, 19, 21)
OFF_UA, OFF_ZA, OFF_Q, OFF_K, OFF_V, OFF_ZB, OFF_MA, OFF_MB = 0, 1024, 2048, 3072, 3328, 3584, 4608, 5632
POOL_W = (2, 4, 8, 16)


class _Res:
    __slots__ = ("w", "r")

    def __init__(self):
        self.w = None
        self.r = {}


class _Eng:
    def __init__(self, fw, name, h, is_pe=False):
        self.name = name
        self.h = h
        self.is_pe = is_pe
        self.sem = fw.new_sem("e_" + name)
        self.count = 0
        self.waited = {}


class FW:
    def __init__(self, nc, es):
        self.nc = nc
        self.es = es
        self.sems = {}
        self.pe = _Eng(self, "pe", nc.tensor, is_pe=True)
        self.act = _Eng(self, "act", nc.scalar)
        self.dve = _Eng(self, "dve", nc.vector)
        self.pool = _Eng(self, "pool", nc.gpsimd)
        self.sp = _Eng(self, "sp", nc.sync)
        self.engs = [self.pe, self.act, self.dve, self.pool, self.sp]
        self.res = {}
        self.dma_cnt = {}
        self.phase_toks = {}

    def new_sem(self, name):
        self.sems[name] = self.es.enter_context(self.nc.semaphore(name))
        return name

    def R(self, key):
        r = self.res.get(key)
        if r is None:
            r = _Res()
            self.res[key] = r
        return r

    def _wait(self, eng, tok):
        sk, val = tok
        if eng.waited.get(sk, 0) >= val:
            return
        eng.h.wait_ge(self.sems[sk], val)
        eng.waited[sk] = val

    def _deps(self, reads, writes):
        deps = []
        for k in reads:
            r = self.R(k)
            if r.w is not None:
                deps.append(r.w)
        for k in writes:
            r = self.R(k)
            if r.w is not None:
                deps.append(r.w)
            deps.extend(r.r.items())
        return deps

    def _record(self, tok, reads, writes):
        for k in reads:
            r = self.R(k)
            if r.r.get(tok[0], 0) < tok[1]:
                r.r[tok[0]] = tok[1]
        for k in writes:
            r = self.R(k)
            r.w = tok
            r.r = {}

    def op(self, eng, fns, reads=(), writes=()):
        if callable(fns):
            fns = [fns]
        writes = list(writes) + [k for k in reads if k.startswith("ps")]
        reads = [k for k in reads if not k.startswith("ps")]
        for tok in self._deps(reads, writes):
            if eng.is_pe and tok[0] == eng.sem:
                continue
            self._wait(eng, tok)
        inst = None
        for f in fns:
            inst = f()
        eng.count += 1
        inst.then_inc(self.sems[eng.sem], 1)
        tok = (eng.sem, eng.count)
        self._record(tok, reads, writes)
        return tok

    def dma(self, eng, out, in_, sem, reads=(), writes=(), phase_local=False, **kw):
        if sem not in self.sems:
            self.new_sem(sem)
            self.dma_cnt[sem] = 0
        if phase_local:
            for tok in self.phase_toks.items():
                self._wait(eng, tok)
        for tok in self._deps(reads, writes):
            self._wait(eng, tok)
        self.dma_cnt[sem] += 16
        eng.h.dma_start(out=out, in_=in_, **kw).then_inc(self.sems[sem], 16)
        tok = (sem, self.dma_cnt[sem])
        self._record(tok, reads, writes)
        return tok

    def barrier(self, engs=None, sp=False):
        last = {}
        for r in self.res.values():
            toks = list(r.r.items())
            if r.w is not None:
                toks.append(r.w)
            for sk, v in toks:
                if last.get(sk, 0) < v:
                    last[sk] = v
        if engs is None:
            engs = [self.pe, self.act, self.dve, self.pool] + ([self.sp] if sp else [])
        for e in engs:
            for sk, v in last.items():
                self._wait(e, (sk, v))
        self.phase_toks = dict(last)


def _fm_block(W, cols):
    return np.ascontiguousarray(W[:, cols].reshape(8, 128, 128).transpose(1, 0, 2))


def _fm_slot(W, col_lists):
    return np.stack([_fm_block(W, c) for c in col_lists], axis=1).reshape(128, 4096)


def _tm_slot(W, hf):
    return np.ascontiguousarray(W[:, hf * 512:(hf + 1) * 512].reshape(8, 128, 512).transpose(1, 0, 2)).reshape(128, 4096)


def _head_cols(base, j, i):
    hA, hB = 8 * j + i, 8 * j + 4 + i
    return np.concatenate([base + hA * 64 + np.arange(64), base + hB * 64 + np.arange(64)])


def pack_weights(inp):
    w_in = np.asarray(inp["w_in"][0], np.float32)
    slots = np.zeros((NSLOT, 128, 4096), np.float32)
    ar = np.arange(128)
    for g in range(4):
        slots[S_UAZA + g] = _fm_slot(w_in, [OFF_UA + (2 * g) * 128 + ar, OFF_UA + (2 * g + 1) * 128 + ar,
                                            OFF_ZA + (2 * g) * 128 + ar, OFF_ZA + (2 * g + 1) * 128 + ar])
    pw = np.asarray(inp["pool_w"][0], np.float32).reshape(4, 2, 128, 2, 128)
    slots[S_POOL, :, :2048] = pw.transpose(2, 0, 3, 1, 4).reshape(128, 2048)
    w_a = np.asarray(inp["w_branch_a"][0], np.float32)
    w_b = np.asarray(inp["w_branch_b"][0], np.float32)
    rperm = np.concatenate([_head_cols(0, j, i) for j in range(2) for i in range(4)])
    w_bp = w_b[rperm, :]
    w_o = np.asarray(inp["w_out"][0], np.float32)
    w_g = np.asarray(inp["w_ple_gate"][0], np.float32)
    for h in range(2):
        slots[S_MA + h] = _fm_slot(w_in, [OFF_MA + (4 * h + o) * 128 + ar for o in range(4)])
        slots[S_MB + h] = _fm_slot(w_in, [OFF_MB + (4 * h + o) * 128 + ar for o in range(4)])
        slots[S_WA + h] = _fm_slot(w_a, [(4 * h + o) * 128 + ar for o in range(4)])
        slots[S_WB + h] = _fm_slot(w_bp, [(4 * h + o) * 128 + ar for o in range(4)])
        slots[S_Q + h] = _fm_slot(w_in, [_head_cols(OFF_Q, h, i) for i in range(4)])
        slots[S_ZB + h] = _fm_slot(w_in, [_head_cols(OFF_ZB, h, i) for i in range(4)])
        slots[S_WO + h] = _tm_slot(w_o, h)
        slots[S_WG + h] = _tm_slot(w_g, h)
    wp = np.asarray(inp["w_ple_in"][0], np.float32)
    slots[S_WPLE, :, :2048] = wp.reshape(2, 128, 1024).transpose(1, 0, 2).reshape(128, 2048)
    wkv = np.zeros((128, 4096), np.float32)
    wkv[:, 0:2048] = np.stack([_fm_block(w_in, OFF_K + j * 128 + ar) for j in range(2)], axis=1).reshape(128, 2048)
    wkv[:, 2048:4096] = w_in[:, OFF_V:OFF_V + 256].reshape(8, 128, 256).transpose(1, 0, 2).reshape(128, 2048)
    gains = np.zeros((128, 32), np.float32)
    gains[:, 0:8] = np.asarray(inp["norm_pre"][0]).reshape(8, 128).T
    gains[:, 8:16] = np.asarray(inp["ple_norm"][0]).reshape(8, 128).T
    gains[:, 16:24] = np.asarray(inp["pool_scale"][0]).reshape(8, 128).T
    gains[:, 24] = np.tile(np.asarray(inp["q_norm"][0]), 2)
    gains[:, 25] = np.tile(np.asarray(inp["k_norm"][0]), 2)
    return slots, wkv, gains, np.ascontiguousarray(np.asarray(inp["norm_post"][0], np.float32))


def make_consts(tmax):
    cm = np.zeros((128, 4, 128), np.float32)
    cm[:, 0, :] = np.eye(128)
    for h in range(2):
        cm[h * 64:(h + 1) * 64, 1, h * 64:(h + 1) * 64] = 1.0
    Rm = np.zeros((64, 64), np.float32)
    for a in range(2):
        for i in range(16):
            Rm[a * 32 + i, a * 32 + i + 16] = -1.0
            Rm[a * 32 + i + 16, a * 32 + i] = 1.0
    R2 = np.zeros((128, 128), np.float32)
    R2[0:64, 0:64] = Rm
    R2[64:128, 64:128] = Rm
    cm[:, 2, :] = R2.T
    cm[:, 3, :] = 1.0
    t = np.arange(tmax)
    f = np.arange(128) % 64
    idx = (f % 32) % 16
    freq = (10000.0 ** (-(2.0 * idx) / 32.0)).astype(np.float32)
    pos = np.where((f < 32)[:, None], (t // 64)[None, :], (t % 64)[None, :]).astype(np.float32)
    ang = pos * freq[:, None]
    cs = np.stack([np.cos(ang), np.sin(ang)], axis=1).astype(np.float32)
    edge = np.zeros((128, 2, 4, 8), np.float32)
    for g, w in enumerate(POOL_W):
        for i in range(8):
            edge[:, 0, g, i] = 1.0 / (min(i + w // 2, 1 << 30) - max(i - w // 2, 0))
            edge[:, 1, g, i] = 1.0 / (min(w // 2, 8 - i) + w // 2)
    return cm.reshape(128, 512), np.ascontiguousarray(cs), edge.reshape(128, 64)


class _Stop(Exception):
    pass


def build_program(seqs, tmax, stop=0):
    nc = bass.Bass("TRN2", target_bir_lowering=False)
    nseq = len(seqs)
    x_d = [nc.dram_tensor(f"x{i}", [T, D], F32, kind="ExternalInput").ap() for i, T in enumerate(seqs)]
    p_d = [nc.dram_tensor(f"p{i}", [T, PLE], F32, kind="ExternalInput").ap() for i, T in enumerate(seqs)]
    y_d = [nc.dram_tensor(f"y{i}", [T, D], F32, kind="ExternalOutput").ap() for i, T in enumerate(seqs)]
    wsl_d = nc.dram_tensor("wslots", [NSLOT, 128, 4096], F32, kind="ExternalInput").ap()
    wkv_d = nc.dram_tensor("wkv", [128, 4096], F32, kind="ExternalInput").ap()
    gains_d = nc.dram_tensor("gains", [128, 32], F32, kind="ExternalInput").ap()
    gpost_d = nc.dram_tensor("gpost", [D], F32, kind="ExternalInput").ap()
    cmat_d = nc.dram_tensor("cmat", [128, 512], F32, kind="ExternalInput").ap()
    cs_d = nc.dram_tensor("cs", [128, 2, tmax], F32, kind="ExternalInput").ap()
    edge_d = nc.dram_tensor("edge", [128, 64], F32, kind="ExternalInput").ap()
    wbf_d = nc.dram_tensor("wbf", [NSLOT, 128, 4096], BF16, kind="Internal").ap()
    wkvbf_d = nc.dram_tensor("wkvbf", [128, 4096], BF16, kind="Internal").ap()
    TM = max(seqs)
    NKM = TM // 128

    with ExitStack() as es:
        fw = FW(nc, es)
        pe, act, dve, pool, sp = fw.pe, fw.act, fw.dve, fw.pool, fw.sp
        V, S, G = nc.vector, nc.scalar, nc.gpsimd

        uid = [0]

        def sbuf(stack, name, shape, dt):
            uid[0] += 1
            return stack.enter_context(nc.sbuf_tensor(f"s_{name}_{uid[0]}", shape, dt))

        KT = sbuf(es, "KT", [128, 2, TM], BF16)
        Vt = sbuf(es, "Vt", [128, NKM, 256], BF16)
        xt = sbuf(es, "xt", [128, 4, D], F32)
        hT = sbuf(es, "hT", [128, 8, ST], BF16)
        hTh = sbuf(es, "hTh", [128, 8, 16], BF16)
        QT = sbuf(es, "QT", [128, 8, ST], BF16)
        szb = sbuf(es, "szb", [128, 8, ST], BF16)
        pTt = sbuf(es, "pTt", [128, 2, ST], BF16)
        tA = sbuf(es, "tA", [128, 8, ST], F32)
        cs_t = sbuf(es, "cs_t", [128, 2, ST], F32)
        ring = [sbuf(es, f"ring{i}", [128, 4096], BF16) for i in range(4)]
        cmat = sbuf(es, "cmat", [128, 4, 128], BF16)
        gains = sbuf(es, "gains", [128, 32], F32)
        gpost = sbuf(es, "gpost", [128, D], F32)
        edge = sbuf(es, "edge", [128, 2, 4, 8], F32)
        epsc = sbuf(es, "epsc", [128, 2], F32)
        stat = sbuf(es, "stat", [128, 64], F32)
        ps = es.enter_context(nc.psum_tensor("ps", [128, 8, 512], F32))
        ident = cmat[:, 0, :]
        onesblk = cmat[:, 1, :]
        rmatT = cmat[:, 2, :]
        ones = cmat[:, 3, :]
        pst = ps[:, 7, :].bitcast(BF16)
        pstb = {6: ps[:, 6, :].bitcast(BF16), 7: pst}

        def mm(out, lhsT, rhs, start=True, stop=True):
            return lambda: nc.tensor.matmul(out, lhsT=lhsT, rhs=rhs, start=start, stop=stop)

        def tr(out, in_):
            return lambda: nc.tensor.transpose(out=out, in_=in_, identity=ident if in_.shape[0] == 128 else cmat[0:in_.shape[0], 0, 0:in_.shape[0]])

        dbg = {}

        def dump(name, t, key_list, cond=True):
            if not (stop == -1 and cond) or name in dbg:
                return
            shp = list(t.shape)
            d_ = nc.dram_tensor("dbg_" + name, shp, t.dtype, kind="ExternalOutput").ap()
            dbg[name] = d_
            fw.barrier(sp=True)
            fw.dma(sp, d_, t[:], "d_dbg_" + name, reads=key_list)
            fw.barrier(sp=True)

        bank_rr = [0]

        def next_bank(pool_banks):
            b = pool_banks[bank_rr[0] % len(pool_banks)]
            bank_rr[0] += 1
            return b

        ring_rr = [0]

        def load_slot(slot, half=False):
            r = ring_rr[0] % 4
            ring_rr[0] += 1
            n = 2048 if half else 4096
            fw.dma(sp, ring[r][:, 0:n], wbf_d[slot, :, 0:n], f"d_ring{r}", reads=[f"wbf{slot}_{q}" for q in range(2 if half else 4)], writes=[f"ring{r}"])
            return r

        with ExitStack() as ph:
            stg = sbuf(ph, "c_stg", [128, 512], F32)
            fw.dma(sp, stg[:], cmat_d, "d_cstg", writes=["c_stg"])
            fw.dma(sp, gains[:], gains_d, "d_gains", writes=["gains"])
            fw.dma(sp, gpost[:], gpost_d.partition_broadcast(128), "d_gpost", writes=["gpost"])
            fw.dma(sp, edge[:].rearrange("p a g i -> p (a g i)"), edge_d, "d_edge", writes=["edge"])
            fw.op(dve, lambda: V.tensor_copy(out=cmat[:].rearrange("p a c -> p (a c)"), in_=stg[:]), reads=["c_stg"], writes=["cmat"])
            fw.op(dve, lambda: V.memset(epsc[:, 0:1], EPS), writes=["epsc"])
            fw.op(dve, lambda: V.memset(epsc[:, 1:2], 64.0 * EPS), writes=["epsc"])
            fw.barrier()
        if stop == 1:
            return nc

        GPRE, GPLE = 0, 8
        gain_slots = {}
        for g in range(4):
            gain_slots[S_UAZA + g] = ("fm", GPRE)
        for h_ in range(2):
            for s0 in (S_MA, S_Q, S_ZB, S_MB):
                gain_slots[s0 + h_] = ("fm", GPRE)
            gain_slots[S_WG + h_] = ("tm", GPLE)
        prep_state = {"u": 0, "sf": None, "sbb": None}

        def prep_unit(src_ap, dst_ap, dst_key, layout, gcol):
            u = prep_state["u"]
            prep_state["u"] += 1
            b = u % 2
            sf, sbb = prep_state["sf"][b], prep_state["sbb"][b]
            fw.dma(sp, sf[:], src_ap, f"d_wsf{b}", writes=[sf.name], phase_local=True)
            if layout is None:
                fw.op(act, lambda: S.copy(out=sbb[:], in_=sf[:]), reads=[sf.name], writes=[sbb.name])
            else:
                nk, w_, g0 = layout
                fw.op(dve, lambda: V.tensor_tensor(out=sbb[:].rearrange("p (k c) -> p k c", k=nk), in0=sf[:].rearrange("p (k c) -> p k c", k=nk),
                                                   in1=gains[:, gcol + g0:gcol + g0 + nk].unsqueeze(2).broadcast_to([128, nk, w_]), op=ALU.mult),
                      reads=[sf.name, "gains"], writes=[sbb.name])
            fw.dma(pool, dst_ap, sbb[:], f"d_wsb{b}", reads=[sbb.name], writes=[dst_key])

        prep_units = []
        for q4 in range(4):
            lay = (8, 128, 0) if q4 < 2 else (4, 256, 4 * (q4 - 2))
            prep_units.append(lambda q4=q4, lay=lay: prep_unit(wkv_d[:, q4 * 1024:(q4 + 1) * 1024], wkvbf_d[:, q4 * 1024:(q4 + 1) * 1024], f"wkvbf{q4}", lay, GPRE))
        N_WKV_UNITS = 4
        for s_ in range(NSLOT):
            lay0, gc = gain_slots.get(s_, (None, 0))
            nq = 2 if s_ in (S_POOL, S_WPLE) else 4
            for q4 in range(nq):
                if lay0 == "fm":
                    lay = (8, 128, 0)
                elif lay0 == "tm":
                    lay = (2, 512, 2 * q4)
                else:
                    lay = None
                prep_units.append(lambda s_=s_, q4=q4, lay=lay, gc=gc: prep_unit(wsl_d[s_, :, q4 * 1024:(q4 + 1) * 1024], wbf_d[s_, :, q4 * 1024:(q4 + 1) * 1024],
                                                                               f"wbf{s_}_{q4}", lay, gc))

        def rstd_from(ss_ap, out_ap, scale, eps_col, tmp_ap, rkeys, wkey):
            n_ = ss_ap.shape[0]
            fw.op(act, lambda: S.activation(out=tmp_ap, in_=ss_ap, func=AF.Ln, bias=epsc[0:n_, eps_col:eps_col + 1], scale=scale),
                  reads=rkeys + ["epsc"], writes=[wkey + "_t"])
            fw.op(act, lambda: S.activation(out=out_ap, in_=tmp_ap, func=AF.Exp, scale=-0.5), reads=[wkey + "_t"], writes=[wkey])

        def norm_pre(x_ap, nrows, junk, hn, keys):
            fw.op(act, lambda: S.activation(out=junk[0:nrows, :], in_=x_ap, func=AF.Square, accum_out=stat[0:nrows, 12:13]),
                  reads=keys, writes=["stat12", "junk"])
            rstd_from(stat[0:nrows, 12:13], stat[0:nrows, 14:15], 1.0 / D, 0, stat[0:nrows, 13:14], ["stat12"], "stat14")
            fw.op(dve, lambda: V.tensor_scalar(out=hn[0:nrows, :], in0=x_ap, scalar1=stat[0:nrows, 14:15], scalar2=None, op0=ALU.mult),
                  reads=keys + ["stat14"], writes=[hn.name])

        def norm_tr(nrows, hn, dst_fn):
            fw.op(pe, [tr(pst[:, kc * 128:kc * 128 + nrows], hn[0:nrows, kc * 128:(kc + 1) * 128]) for kc in range(8)],
                  reads=[hn.name, "cmat"], writes=["ps7"])
            dst_fn()

        def qk_stage1(bank, gcol, tmp, c):
            sq, qg = tmp["sq"], tmp["qg"]
            if stop != 322:
                fw.op(act, lambda: S.activation(out=sq[:], in_=ps[:, bank, :], func=AF.Square), reads=[f"ps{bank}"], writes=[sq.name])
            if stop == 321:
                return
            fw.op(act, lambda: S.activation(out=qg[:], in_=ps[:, bank, :], func=AF.Copy, scale=gains[:, gcol:gcol + 1]),
                  reads=[f"ps{bank}", "gains"], writes=[qg.name])

        def qk_stage2(tmp, dst_ap, dst_key, bss, brq):
            sq, qg, t1, t2, rs = tmp["sq"], tmp["qg"], tmp["t1"], tmp["t2"], tmp["rs"]
            fw.op(pe, [mm(ps[:, bss, :], onesblk, sq[:])], reads=[sq.name, "cmat"], writes=[f"ps{bss}"])
            fw.op(pe, [mm(ps[:, brq, :], rmatT, qg[:])], reads=[qg.name, "cmat"], writes=[f"ps{brq}"])
            fw.op(pool, lambda: G.tensor_tensor(out=t1[:], in0=qg[:], in1=cs_t[:, 0, :], op=ALU.mult), reads=[qg.name, "cs_t"], writes=[t1.name])
            fw.op(dve, lambda: V.tensor_tensor(out=t2[:], in0=ps[:, brq, :], in1=cs_t[:, 1, :], op=ALU.mult), reads=[f"ps{brq}", "cs_t"], writes=[t2.name])
            fw.op(act, lambda: S.activation(out=rs[:], in_=ps[:, bss, :], func=AF.Ln, bias=epsc[:, 1:2], scale=1.0),
                  reads=[f"ps{bss}", "epsc"], writes=[rs.name])
            fw.op(dve, lambda: V.tensor_tensor(out=t1[:], in0=t1[:], in1=t2[:], op=ALU.add), reads=[t1.name, t2.name], writes=[t1.name])
            fw.op(act, lambda: S.activation(out=rs[:], in_=rs[:], func=AF.Exp, scale=-0.5), reads=[rs.name], writes=[rs.name])
            fw.op(dve, lambda: V.tensor_tensor(out=dst_ap, in0=t1[:], in1=rs[:], op=ALU.mult), reads=[t1.name, rs.name], writes=[dst_key])

        def alloc_qk_tmp(ph, tag):
            return dict(sq=sbuf(ph, f"sq{tag}", [128, ST], BF16), qg=sbuf(ph, f"qg{tag}", [128, ST], BF16),
                        t1=sbuf(ph, f"t1{tag}", [128, ST], F32), t2=sbuf(ph, f"t2{tag}", [128, ST], F32),
                        rs=sbuf(ph, f"rs{tag}", [128, ST], F32))

        LIN_BANKS = [0, 1, 2, 3]

        def load_x_tile(xd, t0):
            fw.dma(sp, xt[:], xd[t0:t0 + ST, :].rearrange("(s p) d -> p s d", p=128), "d_xt", writes=["xt"])

        def mh_stats(junk):
            for s in range(4):
                fw.op(act, lambda s=s: S.activation(out=junk[:], in_=xt[:, s, :], func=AF.Square, accum_out=stat[:, s:s + 1]),
                      reads=["xt"], writes=[f"st_ss{s}", "junk"])
            fw.op(act, lambda: S.activation(out=stat[:, 4:8], in_=stat[:, 0:4], func=AF.Ln, bias=epsc[:, 0:1], scale=1.0 / D),
                  reads=[f"st_ss{s}" for s in range(4)] + ["epsc"], writes=["st_ln"])
            fw.op(act, lambda: S.activation(out=stat[:, 8:12], in_=stat[:, 4:8], func=AF.Exp, scale=-0.5), reads=["st_ln"], writes=["st_rs"])

        def mh_scale(s, hn):
            fw.op(dve, lambda: V.tensor_scalar(out=hn[:], in0=xt[:, s, :], scalar1=stat[:, 8 + s:9 + s], scalar2=None, op0=ALU.mult),
                  reads=["xt", "st_rs"], writes=[hn.name])

        def mh_tr_copy(s, hn, tb=None):
            if tb is None:
                tb = 6 + (s % 2)
            pv_ = ps[:, tb, :].bitcast(BF16)
            fw.op(pe, [tr(pv_[:, kc * 128:(kc + 1) * 128], hn[:, kc * 128:(kc + 1) * 128]) for kc in range(8)],
                  reads=[hn.name, "cmat"], writes=[f"ps{tb}"])
            fw.op(act, lambda: S.copy(out=hT[:, :, s * 128:(s + 1) * 128], in_=pv_.rearrange("p (k c) -> p k c", k=8)),
                  reads=[f"ps{tb}"], writes=["hT"])

        def make_hT(junk, hn2):
            mh_stats(junk)
            for s in range(4):
                mh_scale(s, hn2[s % 2])
                mh_tr_copy(s, hn2[s % 2])

        for si, T in enumerate(seqs):
            NK = T // 128
            NST = T // ST
            xd, pd, yd = x_d[si], p_d[si], y_d[si]

            with ExitStack() as ph:
                junk = sbuf(ph, "junk", [128, D], BF16)
                hn2 = [sbuf(ph, f"hn{i}", [128, D], BF16) for i in range(2)]
                tmps = [alloc_qk_tmp(ph, f"a{i}") for i in range(2)]
                wkv = sbuf(ph, "wkv", [128, 4096], BF16)
                if si == 0:
                    prep_state["sf"] = [sbuf(ph, f"w_sf{i}", [128, 1024], F32) for i in range(2)]
                    prep_state["sbb"] = [sbuf(ph, f"w_sb{i}", [128, 1024], BF16) for i in range(2)]
                    for pu in prep_units[:N_WKV_UNITS]:
                        pu()
                    rest = prep_units[N_WKV_UNITS:]
                    per_it = -(-len(rest) // NST)
                fw.dma(sp, wkv[:], wkvbf_d, "d_wkv", reads=[f"wkvbf{q}" for q in range(4)], writes=["wkv"], phase_local=True)
                wk = wkv[:, 0:2048].rearrange("p (j k c) -> p j k c", j=2, k=8)
                wv = wkv[:, 2048:4096].rearrange("p (k c) -> p k c", k=8)
                load_x_tile(xd, 0)
                for it in range(NST):
                    t0 = it * ST
                    fw.dma(sp, cs_t[:], cs_d[:, :, t0:t0 + ST], "d_cs", writes=["cs_t"])
                    if stop == 30:
                        fw.barrier(); return nc
                    make_hT(junk, hn2)
                    if it + 1 < NST:
                        load_x_tile(xd, t0 + ST)
                    if stop == 31:
                        fw.barrier(); return nc
                    banks = []
                    for j in range(2):
                        b = next_bank(LIN_BANKS)
                        banks.append(b)
                        fw.op(pe, [mm(ps[:, b, :], wk[:, j, kc, :], hT[:, kc, :], kc == 0, kc == 7) for kc in range(8)],
                              reads=["hT", "wkv"], writes=[f"ps{b}"])
                        if stop == 320:
                            continue
                        qk_stage1(b, 25, tmps[j], j)
                    if stop in (32, 320, 321, 322):
                        fw.barrier(); return nc
                    for s in range(4):
                        b = next_bank(LIN_BANKS)
                        fw.op(pe, [mm(ps[:, b, 0:256], hT[:, kc, s * 128:(s + 1) * 128], wv[:, kc, :], kc == 0, kc == 7) for kc in range(8)],
                              reads=["hT", "wkv"], writes=[f"ps{b}"])
                        kt = it * 4 + s
                        fw.op(dve, lambda b=b, kt=kt: V.tensor_copy(out=Vt[:, kt, :], in_=ps[:, b, 0:256]), reads=[f"ps{b}"], writes=["Vt"])
                    if stop == 33:
                        fw.barrier(); return nc
                    for j in range(2):
                        qk_stage2(tmps[j], KT[:, j, t0:t0 + ST], "KT", 4 + j, 6)
                    if si == 0:
                        for pu in rest[it * per_it:(it + 1) * per_it]:
                            pu()
                fw.barrier()
                dump("KT", KT, ["KT"], si == 0)
                dump("Vt", Vt, ["Vt"], si == 0)
            if stop == 3:
                return nc

            load_x_tile(xd, 0)
            fw.dma(sp, cs_t[:], cs_d[:, :, 0:ST], "d_cs", writes=["cs_t"])
            for it in range(NST):
                t0 = it * ST
                first, last = (it == 0), (it == NST - 1)
                with ExitStack() as ph:
                    junk = sbuf(ph, "junk", [128, D], BF16)
                    hn2 = [sbuf(ph, f"hn{i}", [128, D], BF16) for i in range(2)]
                    hn = hn2[0]
                    xh = sbuf(ph, "xh", [16, D], F32)
                    poolw_t = sbuf(ph, "poolw", [128, 2048], BF16)
                    pa = ExitStack()
                    uext = [sbuf(pa, f"uext{i}", [128, 2, ST + 16], F32) for i in range(2)]
                    pwa = sbuf(pa, "pwa", [128, 2, ST + 16], F32)
                    pwb = sbuf(pa, "pwb", [128, 2, ST + 16], F32)
                    pooled = [sbuf(pa, f"pooled{i}", [128, 2, ST], BF16) for i in range(2)]
                    sza = [sbuf(pa, f"sza{i}", [128, 2, ST], BF16) for i in range(2)]
                    sma = [sbuf(pa, f"sma{i}", [128, ST], BF16) for i in range(4)]
                    etmp = sbuf(pa, "etmp", [128, 2, 8], F32)
                    aT = szb

                    fw.op(dve, lambda: V.memset(xh[:], 0.0), writes=["xh_l", "xh_r"])
                    if not first:
                        fw.dma(sp, xh[0:8, :], xd[t0 - 8:t0, :], "d_xhl", writes=["xh_l"], phase_local=True)
                    if not last:
                        fw.dma(sp, xh[8:16, :], xd[t0 + ST:t0 + ST + 8, :], "d_xhr", writes=["xh_r"], phase_local=True)
                    if it == 0:
                        make_hT(junk, hn2)

                    def dsth():
                        fw.op(act, lambda: S.copy(out=hTh[:], in_=pst.rearrange("p (k c) -> p k c", k=8)[:, :, 0:16]), reads=["ps7"], writes=["hTh"])
                    dump("hT", hT, ["hT"], si == 0 and it == 0)
                    dump("hTh", hTh, ["hTh"], si == 0 and it == 0)

                    fw.dma(sp, poolw_t[:], wbf_d[S_POOL, :, 0:2048], "d_poolw", reads=[f"wbf{S_POOL}_0", f"wbf{S_POOL}_1"], writes=["poolw"], phase_local=True)
                    poolw = poolw_t[:].rearrange("p (g o k c) -> p g o k c", g=4, o=2, k=2)

                    slot_of = {}

                    def proj_a_main(g):
                        r = load_slot(S_UAZA + g)
                        slot_of[g] = r
                        wsl = ring[r][:].rearrange("p (o k c) -> p o k c", o=4, k=8)
                        ue, sz = uext[g % 2], sza[g % 2]
                        for b2 in range(2):
                            bk = next_bank(LIN_BANKS)
                            fw.op(pe, [mm(ps[:, bk, :], wsl[:, b2, kc, :], hT[:, kc, :], kc == 0, kc == 7) for kc in range(8)],
                                  reads=["hT", f"ring{r}"], writes=[f"ps{bk}"])
                            fw.op(act, lambda: S.copy(out=ue[:, b2, 8:8 + ST], in_=ps[:, bk, :]), reads=[f"ps{bk}"], writes=[ue.name])
                        for b2 in range(2):
                            bk = next_bank(LIN_BANKS)
                            fw.op(pe, [mm(ps[:, bk, :], wsl[:, 2 + b2, kc, :], hT[:, kc, :], kc == 0, kc == 7) for kc in range(8)],
                                  reads=["hT", f"ring{r}"], writes=[f"ps{bk}"])
                            fw.op(act, lambda: S.activation(out=sz[:, b2, :], in_=ps[:, bk, :], func=AF.Silu), reads=[f"ps{bk}"], writes=[sz.name])

                    def proj_a_halo(g):
                        r = slot_of[g]
                        wsl = ring[r][:].rearrange("p (o k c) -> p o k c", o=4, k=8)
                        ue = uext[g % 2]
                        for b2 in range(2):
                            bh = next_bank([4, 5])
                            fw.op(pe, [mm(ps[:, bh, 0:16], wsl[:, b2, kc, :], hTh[:, kc, :], kc == 0, kc == 7) for kc in range(8)],
                                  reads=["hTh", f"ring{r}"], writes=[f"ps{bh}"])
                            fw.op(dve, lambda: V.tensor_copy(out=ue[:, b2, 0:8], in_=ps[:, bh, 0:8]), reads=[f"ps{bh}"], writes=[ue.name])
                            fw.op(dve, lambda: V.tensor_copy(out=ue[:, b2, 8 + ST:16 + ST], in_=ps[:, bh, 8:16]), reads=[f"ps{bh}"], writes=[ue.name])

                    def proj_a(g):
                        proj_a_main(g)
                        proj_a_halo(g)

                    def pool_mix(g):
                        w = POOL_W[g]
                        ue, pl, sz = uext[g % 2], pooled[g % 2], sza[g % 2]
                        L = ST + 16
                        cur, step, k_ = ue, 1, 0
                        while step < w:
                            nxt = pwa if k_ % 2 == 0 else pwb
                            e, h = (pool, G) if k_ % 2 == 0 else (dve, V)
                            fw.op(e, lambda: h.tensor_tensor(out=nxt[:, :, 0:L - step], in0=cur[:, :, 0:L - step], in1=cur[:, :, step:L], op=ALU.add),
                                  reads=[cur.name], writes=[nxt.name])
                            L -= step
                            cur = nxt
                            step *= 2
                            k_ += 1
                        off = 8 - w // 2
                        fw.op(dve, lambda: V.scalar_tensor_tensor(out=pl[:], in0=cur[:, :, off:off + ST], scalar=1.0 / w, in1=ue[:, :, 8:8 + ST],
                                                                 op0=ALU.mult, op1=ALU.subtract),
                              reads=[cur.name, ue.name], writes=[pl.name])
                        for (is_edge, a_, j0) in ((first, 0, 0), (last, 1, ST - 8)):
                            if not is_edge:
                                continue
                            fw.op(dve, lambda: V.tensor_tensor(out=etmp[:], in0=cur[:, :, off + j0:off + j0 + 8],
                                                               in1=edge[:, a_, g, :].unsqueeze(1).broadcast_to([128, 2, 8]), op=ALU.mult),
                                  reads=[cur.name, "edge"], writes=["etmp"])
                            fw.op(dve, lambda: V.tensor_tensor(out=pl[:, :, j0:j0 + 8], in0=etmp[:], in1=ue[:, :, 8 + j0:16 + j0], op=ALU.subtract),
                                  reads=["etmp", ue.name, pl.name], writes=[pl.name])
                        for o2 in range(2):
                            bk = next_bank(LIN_BANKS)
                            fw.op(pe, [mm(ps[:, bk, :], poolw[:, g, o2, k2, :], pl[:, k2, :], k2 == 0, k2 == 1) for k2 in range(2)],
                                  reads=[pl.name, "poolw"], writes=[f"ps{bk}"])
                            oc = 2 * g + o2
                            fw.op(dve, lambda: V.scalar_tensor_tensor(out=aT[:, oc, :], in0=ps[:, bk, :], scalar=gains[:, 16 + oc:17 + oc], in1=sz[:, o2, :],
                                                                     op0=ALU.mult, op1=ALU.mult),
                                  reads=[f"ps{bk}", "gains", sz.name], writes=["szb"])

                    def ma_proj(w1, r1, o, oc):
                        sm = sma[oc % 4]
                        bk = next_bank(LIN_BANKS)
                        fw.op(pe, [mm(ps[:, bk, :], w1[:, o, kc, :], hT[:, kc, :], kc == 0, kc == 7) for kc in range(8)],
                              reads=["hT", f"ring{r1}"], writes=[f"ps{bk}"])
                        fw.op(act, lambda: S.activation(out=sm[:], in_=ps[:, bk, :], func=AF.Sigmoid), reads=[f"ps{bk}"], writes=[sm.name])

                    def wa_proj(w2, r2, o, oc):
                        sm = sma[oc % 4]
                        bk2 = next_bank(LIN_BANKS)
                        fw.op(pe, [mm(ps[:, bk2, :], w2[:, o, kc, :], aT[:, kc, :], kc == 0, kc == 7) for kc in range(8)],
                              reads=["szb", f"ring{r2}"], writes=[f"ps{bk2}"])
                        fw.op(dve, lambda: V.tensor_tensor(out=tA[:, oc, :], in0=ps[:, bk2, :], in1=sm[:], op=ALU.mult),
                              reads=[f"ps{bk2}", sm.name], writes=["tA"])

                    proj_a_main(0)
                    norm_pre(xh[:], 16, junk, hn, ["xh_l", "xh_r"])
                    norm_tr(16, hn, dsth)
                    proj_a_halo(0)
                    for g in range(3):
                        proj_a(g + 1)
                        pool_mix(g)
                    r1 = load_slot(S_MA + 0)
                    w1 = ring[r1][:].rearrange("p (o k c) -> p o k c", o=4, k=8)
                    for o in range(4):
                        ma_proj(w1, r1, o, o)
                    pool_mix(3)
                    r2 = load_slot(S_WA + 0)
                    w2 = ring[r2][:].rearrange("p (o k c) -> p o k c", o=4, k=8)
                    for o in range(4):
                        wa_proj(w2, r2, o, o)
                    r1 = load_slot(S_MA + 1)
                    r2 = load_slot(S_WA + 1)
                    w1 = ring[r1][:].rearrange("p (o k c) -> p o k c", o=4, k=8)
                    w2 = ring[r2][:].rearrange("p (o k c) -> p o k c", o=4, k=8)
                    for o in range(4):
                        ma_proj(w1, r1, o, 4 + o)
                        wa_proj(w2, r2, o, 4 + o)
                    dump("tA", tA, ["tA"], si == 0 and it == 0)
                    dump("aT", szb, ["szb"], si == 0 and it == 0)
                    fw.barrier()
                    pa.close()
                    tmps = [alloc_qk_tmp(ph, f"b{i}") for i in range(2)]
                    pend = None
                    cidx = 0
                    for j in range(2):
                        r = load_slot(S_Q + j)
                        wsl = ring[r][:].rearrange("p (o k c) -> p o k c", o=4, k=8)
                        for i in range(4):
                            c = 4 * j + i
                            bk = next_bank(LIN_BANKS)
                            fw.op(pe, [mm(ps[:, bk, :], wsl[:, i, kc, :], hT[:, kc, :], kc == 0, kc == 7) for kc in range(8)],
                                  reads=["hT", f"ring{r}"], writes=[f"ps{bk}"])
                            tm_ = tmps[cidx % 2]
                            if pend is not None:
                                qk_stage2(*pend)
                            qk_stage1(bk, 24, tm_, c)
                            pend = (tm_, QT[:, c, :], f"QT{c}", 4 + (cidx % 2), 6)
                            cidx += 1
                    qk_stage2(*pend)
                    for (s0, dstt, func) in ((S_ZB, szb, AF.Silu),):
                        for j in range(2):
                            r = load_slot(s0 + j)
                            wsl = ring[r][:].rearrange("p (o k c) -> p o k c", o=4, k=8)
                            for i in range(4):
                                c = 4 * j + i
                                bk = next_bank(LIN_BANKS)
                                fw.op(pe, [mm(ps[:, bk, :], wsl[:, i, kc, :], hT[:, kc, :], kc == 0, kc == 7) for kc in range(8)],
                                      reads=["hT", f"ring{r}"], writes=[f"ps{bk}"])
                                fw.op(act, lambda bk=bk, c=c, dstt=dstt, func=func: S.activation(out=dstt[:, c, :], in_=ps[:, bk, :], func=func),
                                      reads=[f"ps{bk}"], writes=["szb"])
                    fw.barrier()
                    dump("QT", QT, [], si == 0 and it == 0)
                    dump("szb", szb, [], si == 0 and it == 0)
                if stop == 4:
                    return nc

                if it + 1 < NST:
                    load_x_tile(xd, t0 + ST)
                    fw.dma(sp, cs_t[:], cs_d[:, :, t0 + ST:t0 + 2 * ST], "d_cs", writes=["cs_t"])
                with ExitStack() as ph:
                    bT = sbuf(ph, "bT", [128, 8, ST], BF16)
                    rinv = sbuf(ph, "rinv", [128, ST], F32)
                    wgt = sbuf(ph, "wgt", [128, ST], F32)
                    mrg = QT
                    smb = szb
                    tB = sbuf(ph, "tB", [128, ST], F32)
                    steps = [(c, kt) for c in range(8) for kt in range(NK)]
                    ptile4 = [sbuf(ph, f"ptile{i}", [128, PLE], F32) for i in range(4)]
                    ptb4 = [sbuf(ph, f"ptb{i}", [128, PLE], BF16) for i in range(4)]
                    for s4 in range(4):
                        fw.dma(sp, ptile4[s4][:], pd[t0 + s4 * 128:t0 + (s4 + 1) * 128, :], f"d_ptile{s4}", writes=[ptile4[s4].name], phase_local=True)
                        fw.op(pool, lambda s4=s4: G.tensor_copy(out=ptb4[s4][:], in_=ptile4[s4][:]), reads=[ptile4[s4].name], writes=[ptb4[s4].name])
                    pa2 = ExitStack()
                    NPT = 6
                    PT = [sbuf(pa2, f"PT{i}", [128, 2, ST], BF16) for i in range(NPT)]
                    PS2 = [sbuf(pa2, f"PS2_{i}", [128, 2, ST], BF16) for i in range(2)]
                    PS4 = [sbuf(pa2, f"PS4_{i}", [128, 2, ST], BF16) for i in range(2)]
                    RSG = 4 if NK % 4 == 0 else (2 if NK % 2 == 0 else 1)

                    def emit_qk(n):
                        c, kt = steps[n]
                        j = c // 4
                        sb_ = 2 * (n % 2)
                        fw.op(pe, [mm(ps[:, sb_, :], KT[0:64, j, kt * 128:(kt + 1) * 128], QT[0:64, c, :]),
                                   mm(ps[:, sb_ + 1, :], KT[64:128, j, kt * 128:(kt + 1) * 128], QT[64:128, c, :])],
                              reads=["KT", f"QT{c}"], writes=[f"ps{sb_}", f"ps{sb_ + 1}"])

                    def emit_rs(c, kt, src):
                        rb = 5 + 2 * (c % 2)
                        st_, sp_ = (kt == RSG - 1), (kt == NK - 1)
                        fw.op(pe, [mm(ps[0:64, rb, :], ones[:, 0:64], src[:, 0, :], st_, sp_),
                                   mm(ps[64:128, rb, :], ones[:, 64:128], src[:, 1, :], st_, sp_)],
                              reads=[src.name, "cmat"], writes=[f"ps{rb}"])
                        if kt == NK - 1:
                            ob = rb - 1
                            fw.op(dve, lambda: V.reciprocal(out=rinv[:], in_=ps[:, rb, :]), reads=[f"ps{rb}"], writes=["rinv"])
                            fw.op(pool, lambda: G.tensor_tensor(out=wgt[:], in0=rinv[:], in1=szb[:, c, :], op=ALU.mult), reads=["rinv", "szb"], writes=["wgt"])

                            def fin():
                                fw.op(dve, lambda: V.tensor_tensor(out=bT[:, c, :], in0=ps[:, ob, :], in1=wgt[:], op=ALU.mult),
                                      reads=[f"ps{ob}", "wgt"], writes=["bT"])
                            deferred.append((cur_n[0] + 3, c, fin))

                    deferred = []
                    cur_n = [0]
                    npair = [0]
                    nquad = [0]
                    emit_qk(0)
                    emit_qk(1)
                    for n, (c, kt) in enumerate(steps):
                        j = c // 4
                        cur_n[0] = n
                        if kt == 0:
                            while any(d[1] <= c - 2 for d in deferred):
                                deferred.sort(key=lambda d: d[0])
                                i_ = next(i for i, d in enumerate(deferred) if d[1] <= c - 2)
                                deferred.pop(i_)[2]()
                        sb_ = 2 * (n % 2)
                        pt = PT[n % NPT]
                        fw.op(act, lambda sb_=sb_, pt=pt: S.activation(out=pt[:], in_=ps[:, sb_:sb_ + 2, :], func=AF.Exp, scale=8.0),
                              reads=[f"ps{sb_}", f"ps{sb_ + 1}"], writes=[pt.name])
                        if n + 2 < len(steps):
                            emit_qk(n + 2)
                        ob = 4 + 2 * (c % 2)
                        st_, sp_ = (kt == 0), (kt == NK - 1)
                        fw.op(pe, [mm(ps[0:64, ob, :], Vt[:, kt, 128 * j:128 * j + 64], pt[:, 0, :], st_, sp_),
                                   mm(ps[64:128, ob, :], Vt[:, kt, 128 * j + 64:128 * j + 128], pt[:, 1, :], st_, sp_)],
                              reads=[pt.name, "Vt"], writes=[f"ps{ob}"])
                        if RSG == 1:
                            deferred.append((n, c, lambda c=c, kt=kt, pt=pt: emit_rs(c, kt, pt)))
                        else:
                            if kt % 2 == 1:
                                p2 = PS2[npair[0] % 2]
                                npair[0] += 1
                                pprev = PT[(n - 1) % NPT]
                                fw.op(dve, lambda p2=p2, pprev=pprev, pt=pt: V.tensor_tensor(out=p2[:], in0=pprev[:], in1=pt[:], op=ALU.add),
                                      reads=[pprev.name, pt.name], writes=[p2.name])
                                if RSG == 2:
                                    deferred.append((n + 4, c, lambda c=c, kt=kt, p2=p2: emit_rs(c, kt, p2)))
                            if RSG == 4 and kt % 4 == 3:
                                p4 = PS4[nquad[0] % 2]
                                nquad[0] += 1
                                fw.op(dve, lambda p4=p4: V.tensor_tensor(out=p4[:], in0=PS2[0][:], in1=PS2[1][:], op=ALU.add),
                                      reads=[PS2[0].name, PS2[1].name], writes=[p4.name])
                                deferred.append((n + 4, c, lambda c=c, kt=kt, p4=p4: emit_rs(c, kt, p4)))
                        deferred.sort(key=lambda d: d[0])
                        while deferred and deferred[0][0] <= n:
                            deferred.pop(0)[2]()
                            deferred.sort(key=lambda d: d[0])
                    def mb_proj(wsl, r, i):
                        bk = next_bank(LIN_BANKS)
                        fw.op(pe, [mm(ps[:, bk, :], wsl[:, i, kc, :], hT[:, kc, :], kc == 0, kc == 7) for kc in range(8)],
                              reads=["hT", f"ring{r}"], writes=[f"ps{bk}"])
                        return bk

                    def mb_act(bk, c):
                        fw.op(act, lambda: S.activation(out=smb[:, c, :], in_=ps[:, bk, :], func=AF.Sigmoid), reads=[f"ps{bk}"], writes=["szb"])

                    r_mb0 = load_slot(S_MB + 0)
                    wsl0 = ring[r_mb0][:].rearrange("p (o k c) -> p o k c", o=4, k=8)
                    mb_banks = [mb_proj(wsl0, r_mb0, i) for i in range(4)]
                    cur_n[0] = len(steps) + 10
                    while deferred:
                        deferred.sort(key=lambda d: d[0])
                        deferred.pop(0)[2]()
                    fw.barrier()
                    pa2.close()
                    dump("bT", bT, ["bT"], si == 0 and it == 0)
                    if stop == 5:
                        fw.barrier()
                        return nc
                    for i in range(4):
                        mb_act(mb_banks[i], i)
                    for s4 in range(4):
                        pb = 0 + (s4 % 2)
                        pv_ = ps[:, pb, :].bitcast(BF16)
                        fw.op(pe, [tr(pv_[:, k2 * 128:(k2 + 1) * 128], ptb4[s4][:, k2 * 128:(k2 + 1) * 128]) for k2 in range(2)],
                              reads=[ptb4[s4].name, "cmat"], writes=[f"ps{pb}"])
                        fw.op(dve, lambda s4=s4, pv_=pv_: V.tensor_copy(out=pTt[:, :, s4 * 128:(s4 + 1) * 128], in_=pv_[:, 0:256].rearrange("p (k c) -> p k c", k=2)),
                              reads=[f"ps{pb}"], writes=["pTt"])
                    r = load_slot(S_MB + 1)
                    wsl = ring[r][:].rearrange("p (o k c) -> p o k c", o=4, k=8)
                    for i in range(4):
                        mb_act(mb_proj(wsl, r, i), 4 + i)
                    for h_ in range(2):
                        r = load_slot(S_WB + h_)
                        wsl = ring[r][:].rearrange("p (o k c) -> p o k c", o=4, k=8)
                        for o in range(4):
                            oc = 4 * h_ + o
                            bk = next_bank(LIN_BANKS)
                            fw.op(pe, [mm(ps[:, bk, :], wsl[:, o, kc, :], bT[:, kc, :], kc == 0, kc == 7) for kc in range(8)],
                                  reads=["bT", f"ring{r}"], writes=[f"ps{bk}"])
                            fw.op(dve, lambda bk=bk, oc=oc: V.tensor_tensor(out=tB[:], in0=ps[:, bk, :], in1=smb[:, oc, :], op=ALU.mult),
                                  reads=[f"ps{bk}", "szb"], writes=["tB"])
                            fw.op(dve, lambda oc=oc: V.tensor_tensor(out=mrg[:, oc, :], in0=tB[:], in1=tA[:, oc, :], op=ALU.add),
                                  reads=["tB", "tA"], writes=[f"QT{oc}"])
                    dump("mrg", QT, [], si == 0 and it == 0)
                    junk = sbuf(ph, "junk2", [128, D], BF16)
                    xres2 = [sbuf(ph, f"xres{i}", [128, D], F32) for i in range(2)]
                    x2n2 = [sbuf(ph, f"x2n{i}", [128, D], BF16) for i in range(2)]
                    x1v = tA[:].rearrange("p a c -> p (a c)").rearrange("p (s d) -> p s d", s=4)
                    rwo = [load_slot(S_WO + hf) for hf in range(2)]

                    def stage_a(s):
                        tok = slice(s * 128, (s + 1) * 128)
                        b = s % 2
                        xres = xres2[b]
                        fw.dma(sp, xres[:], xd[t0 + s * 128:t0 + (s + 1) * 128, :], f"d_xres{b}", writes=[xres.name], phase_local=True)
                        yb = (0, 2, 4)[s % 3]
                        for hf in range(2):
                            wt = ring[rwo[hf]][:].rearrange("p (k c) -> p k c", k=8)
                            fw.op(pe, [mm(ps[:, yb + hf, :], mrg[:, kc, tok], wt[:, kc, :], kc == 0, kc == 7) for kc in range(8)],
                                  reads=[f"QT{c_}" for c_ in range(8)] + [f"ring{rwo[hf]}"], writes=[f"ps{yb + hf}"])
                        fw.op(act, lambda: S.activation(out=junk[:].rearrange("p (a c) -> p a c", a=2), in_=ps[:, yb:yb + 2, :], func=AF.Square,
                                                        accum_out=stat[:, 16 + s:17 + s]),
                              reads=[f"ps{yb}", f"ps{yb + 1}"], writes=[f"t1ss{s}", "junk2"])
                        fw.op(act, lambda: S.activation(out=stat[:, 20 + s:21 + s], in_=stat[:, 16 + s:17 + s], func=AF.Ln, bias=epsc[:, 0:1], scale=1.0 / D),
                              reads=[f"t1ss{s}", "epsc"], writes=[f"t1ln{s}"])
                        fw.op(act, lambda: S.activation(out=stat[:, 24 + s:25 + s], in_=stat[:, 20 + s:21 + s], func=AF.Exp, scale=-0.5),
                              reads=[f"t1ln{s}"], writes=[f"t1rs{s}"])
                        fw.op(dve, lambda: V.scalar_tensor_tensor(out=x1v[:, s, :].rearrange("p (a c) -> p a c", a=2), in0=ps[:, yb:yb + 2, :], scalar=stat[:, 24 + s:25 + s],
                                                                 in1=gpost[:].rearrange("p (a c) -> p a c", a=2), op0=ALU.mult, op1=ALU.mult),
                              reads=[f"ps{yb}", f"ps{yb + 1}", f"t1rs{s}", "gpost"], writes=[f"x1_{s}", "tA"])
                        fw.op(dve, lambda: V.tensor_tensor(out=x1v[:, s, :], in0=x1v[:, s, :], in1=xres[:], op=ALU.add), reads=[f"x1_{s}", xres.name], writes=[f"x1_{s}"])

                    def stage_b(s):
                        tok = slice(s * 128, (s + 1) * 128)
                        b = s % 2
                        x2n = x2n2[b]
                        tb = 6 + b
                        fw.op(act, lambda: S.activation(out=junk[:], in_=x1v[:, s, :], func=AF.Square, accum_out=stat[:, 28 + s:29 + s]),
                              reads=[f"x1_{s}"], writes=[f"t2ss{s}", "junk2"])
                        fw.op(act, lambda: S.activation(out=stat[:, 32 + s:33 + s], in_=stat[:, 28 + s:29 + s], func=AF.Ln, bias=epsc[:, 0:1], scale=1.0 / D),
                              reads=[f"t2ss{s}", "epsc"], writes=[f"t2ln{s}"])
                        fw.op(act, lambda: S.activation(out=stat[:, 36 + s:37 + s], in_=stat[:, 32 + s:33 + s], func=AF.Exp, scale=-0.5),
                              reads=[f"t2ln{s}"], writes=[f"t2rs{s}"])
                        fw.op(dve, lambda: V.tensor_scalar(out=x2n[:], in0=x1v[:, s, :], scalar1=stat[:, 36 + s:37 + s], scalar2=None, op0=ALU.mult),
                              reads=[f"x1_{s}", f"t2rs{s}"], writes=[x2n.name])
                        fw.op(pe, [tr(pstb[tb][:, kc * 128:(kc + 1) * 128], x2n[:, kc * 128:(kc + 1) * 128]) for kc in range(8)],
                              reads=[x2n.name, "cmat"], writes=[f"ps{tb}"])

                    def stage_b2(s):
                        tok = slice(s * 128, (s + 1) * 128)
                        tb = 6 + (s % 2)
                        fw.op(act, lambda: S.copy(out=szb[:, :, tok], in_=pstb[tb].rearrange("p (k c) -> p k c", k=8)), reads=[f"ps{tb}"], writes=["szb"])

                    nxt = it + 1 < NST
                    if nxt:
                        hnb = [sbuf(ph, f"hnb{i}", [128, D], BF16) for i in range(2)]
                        mh_stats(junk)
                        mh_scale(0, hnb[0])
                        mh_scale(1, hnb[1])
                    stage_a(0)
                    stage_a(1)
                    stage_a(2)
                    if nxt:
                        mh_tr_copy(0, hnb[0])
                        mh_tr_copy(1, hnb[1])
                        mh_scale(2, hnb[0])
                        mh_scale(3, hnb[1])
                    stage_b(0)
                    stage_a(3)
                    if nxt:
                        mh_tr_copy(2, hnb[0], 2)
                        mh_tr_copy(3, hnb[1], 3)
                    stage_b(1)
                    stage_b2(0)
                    stage_b(2)
                    stage_b2(1)
                    stage_b(3)
                    stage_b2(2)
                    stage_b2(3)
                    fw.barrier()
                    dump("x1", tA, [], si == 0 and it == 0)
                    dump("x2nT", szb, [], si == 0 and it == 0)
                    dump("pTt", pTt, [], si == 0 and it == 0)
                if stop == 6:
                    return nc
                with ExitStack() as ph:
                    sg2 = [sbuf(ph, f"sg{i}", [128, ST], F32) for i in range(2)]
                    tE2 = [sbuf(ph, f"tE{i}", [128, ST], F32) for i in range(2)]
                    outb = [sbuf(ph, f"outb{i}", [128, D], F32) for i in range(2)]
                    x1v = tA[:].rearrange("p a c -> p (a c)").rearrange("p (s d) -> p s d", s=4)
                    rwp = load_slot(S_WPLE, half=True)
                    wpl = ring[rwp][:, 0:2048].rearrange("p (k c) -> p k c", k=2)
                    rwg = [load_slot(S_WG + hf) for hf in range(2)]
                    for s in range(4):
                        tok = slice(s * 128, (s + 1) * 128)
                        ob_ = outb[s % 2]
                        for hf in range(2):
                            sg, tE = sg2[hf], tE2[hf]
                            wt = ring[rwg[hf]][:].rearrange("p (k c) -> p k c", k=8)
                            gb = next_bank([0, 1])
                            eb = next_bank([2, 3])
                            fw.op(pe, [mm(ps[:, gb, :], szb[:, kc, tok], wt[:, kc, :], kc == 0, kc == 7) for kc in range(8)],
                                  reads=["szb", f"ring{rwg[hf]}"], writes=[f"ps{gb}"])
                            fw.op(pe, [mm(ps[:, eb, :], pTt[:, k2, tok], wpl[:, k2, hf * 512:(hf + 1) * 512], k2 == 0, k2 == 1) for k2 in range(2)],
                                  reads=["pTt", f"ring{rwp}"], writes=[f"ps{eb}"])
                            fw.op(act, lambda gb=gb, sg=sg: S.activation(out=sg[:], in_=ps[:, gb, :], func=AF.Sigmoid), reads=[f"ps{gb}"], writes=[sg.name])
                            fw.op(dve, lambda eb=eb, sg=sg, tE=tE: V.tensor_tensor(out=tE[:], in0=ps[:, eb, :], in1=sg[:], op=ALU.mult), reads=[f"ps{eb}", sg.name], writes=[tE.name])
                            fw.op(pool, lambda s=s, hf=hf, ob_=ob_, tE=tE: G.tensor_tensor(out=ob_[:, hf * 512:(hf + 1) * 512], in0=tE[:], in1=x1v[:, s, hf * 512:(hf + 1) * 512], op=ALU.add),
                                  reads=[tE.name, f"x1_{s}"], writes=[ob_.name])
                        fw.dma(pool, yd[t0 + s * 128:t0 + (s + 1) * 128, :], ob_[:], f"d_{ob_.name}", reads=[ob_.name])
                    fw.barrier()
        fw.barrier(sp=True)
    return nc


_CACHE = {}


def _run(x_list, p_list, inp, n_cores):
    seqs = tuple(int(x.shape[1]) for x in x_list)
    tmax = max(seqs)
    key = (seqs, tmax)
    if key not in _CACHE:
        import os
        _CACHE[key] = build_program(list(seqs), tmax, stop=int(os.environ.get("KSTOP", "0")))
    nc = _CACHE[key]
    slots, wkv, gains, gpost = pack_weights(inp)
    cm, cs, edge = make_consts(tmax)
    in_maps = []
    for c in range(n_cores):
        m = {"wslots": slots, "wkv": wkv, "gains": gains, "gpost": gpost, "cmat": cm, "cs": cs, "edge": edge}
        for i in range(len(seqs)):
            m[f"x{i}"] = np.ascontiguousarray(x_list[i][c], dtype=np.float32)
            m[f"p{i}"] = np.ascontiguousarray(p_list[i][c], dtype=np.float32)
        in_maps.append(m)
    res = run_bass_kernel_spmd(nc, in_maps, core_ids=list(range(n_cores)))
    global LAST_RES
    LAST_RES = res
    outs = []
    for i in range(len(seqs)):
        outs.append(np.stack([np.asarray(res.results[c][f"y{i}"], dtype=np.float32) for c in range(n_cores)], axis=0))
    return outs


def kernel(**inputs):
    xp = np.asarray(inputs["x_prompt"], np.float32)
    xs = np.asarray(inputs["x_sample"], np.float32)
    pp = np.asarray(inputs["p_prompt"], np.float32)[0]
    psm = np.asarray(inputs["p_sample"], np.float32)[0]
    outs = _run([xp, xs], [pp, psm], inputs, 8)
    return (outs[0], outs[1])
```

```python
import numpy as np
from contextlib import ExitStack
import concourse.bass as bass
import concourse.mybir as mybir
from concourse.bass_utils import run_bass_kernel_spmd

F32 = mybir.dt.float32
BF16 = mybir.dt.bfloat16
AF = mybir.ActivationFunctionType
ALU = mybir.AluOpType

D = 1024
PLE = 256
EPS = 1e-6
ST = 512
NSLOT = 22
(S_UAZA, S_POOL, S_MA, S_WA, S_Q, S_ZB, S_MB, S_WB, S_WO, S_WG, S_WPLE) = (0, 4, 5, 7, 9, 11, 13, 15, 17, 19, 21)
OFF_UA, OFF_ZA, OFF_Q, OFF_K, OFF_V, OFF_ZB, OFF_MA, OFF_MB = 0, 1024, 2048, 3072, 3328, 3584, 4608, 5632
POOL_W = (2, 4, 8, 16)


class _Res:
    __slots__ = ("w", "r")

    def __init__(self):
        self.w = None
        self.r = {}


class _Eng:
    def __init__(self, fw, name, h, is_pe=False):
        self.name = name
        self.h = h
        self.is_pe = is_pe
        self.sem = fw.new_sem("e_" + name)
        self.count = 0
        self.waited = {}


class FW:
    def __init__(self, nc, es):
        self.nc = nc
        self.es = es
        self.sems = {}
        self.pe = _Eng(self, "pe", nc.tensor, is_pe=True)
        self.act = _Eng(self, "act", nc.scalar)
        self.dve = _Eng(self, "dve", nc.vector)
        self.pool = _Eng(self, "pool", nc.gpsimd)
        self.sp = _Eng(self, "sp", nc.sync)
        self.engs = [self.pe, self.act, self.dve, self.pool, self.sp]
        self.res = {}
        self.dma_cnt = {}
        self.phase_toks = {}

    def new_sem(self, name):
        self.sems[name] = self.es.enter_context(self.nc.semaphore(name))
        return name

    def R(self, key):
        r = self.res.get(key)
        if r is None:
            r = _Res()
            self.res[key] = r
        return r

    def _wait(self, eng, tok):
        sk, val = tok
        if eng.waited.get(sk, 0) >= val:
            return
        eng.h.wait_ge(self.sems[sk], val)
        eng.waited[sk] = val

    def _deps(self, reads, writes):
        deps = []
        for k in reads:
            r = self.R(k)
            if r.w is not None:
                deps.append(r.w)
        for k in writes:
            r = self.R(k)
            if r.w is not None:
                deps.append(r.w)
            deps.extend(r.r.items())
        return deps

    def _record(self, tok, reads, writes):
        for k in reads:
            r = self.R(k)
            if r.r.get(tok[0], 0) < tok[1]:
                r.r[tok[0]] = tok[1]
        for k in writes:
            r = self.R(k)
            r.w = tok
            r.r = {}

    def op(self, eng, fns, reads=(), writes=()):
        if callable(fns):
            fns = [fns]
        writes = list(writes) + [k for k in reads if k.startswith("ps")]
        reads = [k for k in reads if not k.startswith("ps")]
        for tok in self._deps(reads, writes):
            if eng.is_pe and tok[0] == eng.sem:
                continue
            self._wait(eng, tok)
        inst = None
        for f in fns:
            inst = f()
        eng.count += 1
        inst.then_inc(self.sems[eng.sem], 1)
        tok = (eng.sem, eng.count)
        self._record(tok, reads, writes)
        return tok

    def dma(self, eng, out, in_, sem, reads=(), writes=(), phase_local=False, **kw):
        if sem not in self.sems:
            self.new_sem(sem)
            self.dma_cnt[sem] = 0
        if phase_local:
            for tok in self.phase_toks.items():
                self._wait(eng, tok)
        for tok in self._deps(reads, writes):
            self._wait(eng, tok)
        self.dma_cnt[sem] += 16
        eng.h.dma_start(out=out, in_=in_, **kw).then_inc(self.sems[sem], 16)
        tok = (sem, self.dma_cnt[sem])
        self._record(tok, reads, writes)
        return tok

    def barrier(self, engs=None, sp=False):
        last = {}
        for r in self.res.values():
            toks = list(r.r.items())
            if r.w is not None:
                toks.append(r.w)
            for sk, v in toks:
                if last.get(sk, 0) < v:
                    last[sk] = v
        if engs is None:
            engs = [self.pe, self.act, self.dve, self.pool] + ([self.sp] if sp else [])
        for e in engs:
            for sk, v in last.items():
                self._wait(e, (sk, v))
        self.phase_toks = dict(last)


def _fm_block(W, cols):
    return np.ascontiguousarray(W[:, cols].reshape(8, 128, 128).transpose(1, 0, 2))


def _fm_slot(W, col_lists):
    return np.stack([_fm_block(W, c) for c in col_lists], axis=1).reshape(128, 4096)


def _tm_slot(W, hf):
    return np.ascontiguousarray(W[:, hf * 512:(hf + 1) * 512].reshape(8, 128, 512).transpose(1, 0, 2)).reshape(128, 4096)


def _head_cols(base, j, i):
    hA, hB = 8 * j + i, 8 * j + 4 + i
    return np.concatenate([base + hA * 64 + np.arange(64), base + hB * 64 + np.arange(64)])


def pack_weights(inp):
    w_in = np.asarray(inp["w_in"][0], np.float32)
    slots = np.zeros((NSLOT, 128, 4096), np.float32)
    ar = np.arange(128)
    for g in range(4):
        slots[S_UAZA + g] = _fm_slot(w_in, [OFF_UA + (2 * g) * 128 + ar, OFF_UA + (2 * g + 1) * 128 + ar,
                                            OFF_ZA + (2 * g) * 128 + ar, OFF_ZA + (2 * g + 1) * 128 + ar])
    pw = np.asarray(inp["pool_w"][0], np.float32).reshape(4, 2, 128, 2, 128)
    slots[S_POOL, :, :2048] = pw.transpose(2, 0, 3, 1, 4).reshape(128, 2048)
    w_a = np.asarray(inp["w_branch_a"][0], np.float32)
    w_b = np.asarray(inp["w_branch_b"][0], np.float32)
    rperm = np.concatenate([_head_cols(0, j, i) for j in range(2) for i in range(4)])
    w_bp = w_b[rperm, :]
    w_o = np.asarray(inp["w_out"][0], np.float32)
    w_g = np.asarray(inp["w_ple_gate"][0], np.float32)
    for h in range(2):
        slots[S_MA + h] = _fm_slot(w_in, [OFF_MA + (4 * h + o) * 128 + ar for o in range(4)])
        slots[S_MB + h] = _fm_slot(w_in, [OFF_MB + (4 * h + o) * 128 + ar for o in range(4)])
        slots[S_WA + h] = _fm_slot(w_a, [(4 * h + o) * 128 + ar for o in range(4)])
        slots[S_WB + h] = _fm_slot(w_bp, [(4 * h + o) * 128 + ar for o in range(4)])
        slots[S_Q + h] = _fm_slot(w_in, [_head_cols(OFF_Q, h, i) for i in range(4)])
        slots[S_ZB + h] = _fm_slot(w_in, [_head_cols(OFF_ZB, h, i) for i in range(4)])
        slots[S_WO + h] = _tm_slot(w_o, h)
        slots[S_WG + h] = _tm_slot(w_g, h)
    wp = np.asarray(inp["w_ple_in"][0], np.float32)
    slots[S_WPLE, :, :2048] = wp.reshape(2, 128, 1024).transpose(1, 0, 2).reshape(128, 2048)
    wkv = np.zeros((128, 4096), np.float32)
    wkv[:, 0:2048] = np.stack([_fm_block(w_in, OFF_K + j * 128 + ar) for j in range(2)], axis=1).reshape(128, 2048)
    wkv[:, 2048:4096] = w_in[:, OFF_V:OFF_V + 256].reshape(8, 128, 256).transpose(1, 0, 2).reshape(128, 2048)
    gains = np.zeros((128, 32), np.float32)
    gains[:, 0:8] = np.asarray(inp["norm_pre"][0]).reshape(8, 128).T
    gains[:, 8:16] = np.asarray(inp["ple_norm"][0]).reshape(8, 128).T
    gains[:, 16:24] = np.asarray(inp["pool_scale"][0]).reshape(8, 128).T
    gains[:, 24] = np.tile(np.asarray(inp["q_norm"][0]), 2)
    gains[:, 25] = np.tile(np.asarray(inp["k_norm"][0]), 2)
    return slots, wkv, gains, np.ascontiguousarray(np.asarray(inp["norm_post"][0], np.float32))


def make_consts(tmax):
    cm = np.zeros((128, 4, 128), np.float32)
    cm[:, 0, :] = np.eye(128)
    for h in range(2):
        cm[h * 64:(h + 1) * 64, 1, h * 64:(h + 1) * 64] = 1.0
    Rm = np.zeros((64, 64), np.float32)
    for a in range(2):
        for i in range(16):
            Rm[a * 32 + i, a * 32 + i + 16] = -1.0
            Rm[a * 32 + i + 16, a * 32 + i] = 1.0
    R2 = np.zeros((128, 128), np.float32)
    R2[0:64, 0:64] = Rm
    R2[64:128, 64:128] = Rm
    cm[:, 2, :] = R2.T
    cm[:, 3, :] = 1.0
    t = np.arange(tmax)
    f = np.arange(128) % 64
    idx = (f % 32) % 16
    freq = (10000.0 ** (-(2.0 * idx) / 32.0)).astype(np.float32)
    pos = np.where((f < 32)[:, None], (t // 64)[None, :], (t % 64)[None, :]).astype(np.float32)
    ang = pos * freq[:, None]
    cs = np.stack([np.cos(ang), np.sin(ang)], axis=1).astype(np.float32)
    edge = np.zeros((128, 2, 4, 8), np.float32)
    for g, w in enumerate(POOL_W):
        for i in range(8):
            edge[:, 0, g, i] = 1.0 / (min(i + w // 2, 1 << 30) - max(i - w // 2, 0))
            edge[:, 1, g, i] = 1.0 / (min(w // 2, 8 - i) + w // 2)
    return cm.reshape(128, 512), np.ascontiguousarray(cs), edge.reshape(128, 64)


class _Stop(Exception):
    pass


def build_program(seqs, tmax, stop=0):
    nc = bass.Bass("TRN2", target_bir_lowering=False)
    nseq = len(seqs)
    x_d = [nc.dram_tensor(f"x{i}", [T, D], F32, kind="ExternalInput").ap() for i, T in enumerate(seqs)]
    p_d = [nc.dram_tensor(f"p{i}", [T, PLE], F32, kind="ExternalInput").ap() for i, T in enumerate(seqs)]
    y_d = [nc.dram_tensor(f"y{i}", [T, D], F32, kind="ExternalOutput").ap() for i, T in enumerate(seqs)]
    wsl_d = nc.dram_tensor("wslots", [NSLOT, 128, 4096], F32, kind="ExternalInput").ap()
    wkv_d = nc.dram_tensor("wkv", [128, 4096], F32, kind="ExternalInput").ap()
    gains_d = nc.dram_tensor("gains", [128, 32], F32, kind="ExternalInput").ap()
    gpost_d = nc.dram_tensor("gpost", [D], F32, kind="ExternalInput").ap()
    cmat_d = nc.dram_tensor("cmat", [128, 512], F32, kind="ExternalInput").ap()
    cs_d = nc.dram_tensor("cs", [128, 2, tmax], F32, kind="ExternalInput").ap()
    edge_d = nc.dram_tensor("edge", [128, 64], F32, kind="ExternalInput").ap()
    wbf_d = nc.dram_tensor("wbf", [NSLOT, 128, 4096], BF16, kind="Internal").ap()
    wkvbf_d = nc.dram_tensor("wkvbf", [128, 4096], BF16, kind="Internal").ap()
    TM = max(seqs)
    NKM = TM // 128

    with ExitStack() as es:
        fw = FW(nc, es)
        pe, act, dve, pool, sp = fw.pe, fw.act, fw.dve, fw.pool, fw.sp
        V, S, G = nc.vector, nc.scalar, nc.gpsimd

        uid = [0]

        def sbuf(stack, name, shape, dt):
            uid[0] += 1
            return stack.enter_context(nc.sbuf_tensor(f"s_{name}_{uid[0]}", shape, dt))

        KT = sbuf(es, "KT", [128, 2, TM], BF16)
        Vt = sbuf(es, "Vt", [128, NKM, 256], BF16)
        xt = sbuf(es, "xt", [128, 4, D], F32)
        hT = sbuf(es, "hT", [128, 8, ST], BF16)
        hTh = sbuf(es, "hTh", [128, 8, 16], BF16)
        QT = sbuf(es, "QT", [128, 8, ST], BF16)
        szb = sbuf(es, "szb", [128, 8, ST], BF16)
        pTt = sbuf(es, "pTt", [128, 2, ST], BF16)
        tA = sbuf(es, "tA", [128, 8, ST], F32)
        cs_t = sbuf(es, "cs_t", [128, 2, ST], F32)
        ring = [sbuf(es, f"ring{i}", [128, 4096], BF16) for i in range(3)]
        cmat = sbuf(es, "cmat", [128, 4, 128], BF16)
        gains = sbuf(es, "gains", [128, 32], F32)
        gpost = sbuf(es, "gpost", [128, D], F32)
        edge = sbuf(es, "edge", [128, 2, 4, 8], F32)
        epsc = sbuf(es, "epsc", [128, 2], F32)
        stat = sbuf(es, "stat", [128, 64], F32)
        ps = es.enter_context(nc.psum_tensor("ps", [128, 8, 512], F32))
        ident = cmat[:, 0, :]
        onesblk = cmat[:, 1, :]
        rmatT = cmat[:, 2, :]
        ones = cmat[:, 3, :]
        pst = ps[:, 7, :].bitcast(BF16)
        pstb = {6: ps[:, 6, :].bitcast(BF16), 7: pst}

        def mm(out, lhsT, rhs, start=True, stop=True):
            return lambda: nc.tensor.matmul(out, lhsT=lhsT, rhs=rhs, start=start, stop=stop)

        def tr(out, in_):
            return lambda: nc.tensor.transpose(out=out, in_=in_, identity=ident if in_.shape[0] == 128 else cmat[0:in_.shape[0], 0, 0:in_.shape[0]])

        dbg = {}

        def dump(name, t, key_list, cond=True):
            if not (stop == -1 and cond) or name in dbg:
                return
            shp = list(t.shape)
            d_ = nc.dram_tensor("dbg_" + name, shp, t.dtype, kind="ExternalOutput").ap()
            dbg[name] = d_
            fw.barrier(sp=True)
            fw.dma(sp, d_, t[:], "d_dbg_" + name, reads=key_list)
            fw.barrier(sp=True)

        bank_rr = [0]

        def next_bank(pool_banks):
            b = pool_banks[bank_rr[0] % len(pool_banks)]
            bank_rr[0] += 1
            return b

        ring_rr = [0]

        def load_slot(slot, half=False):
            r = ring_rr[0] % 3
            ring_rr[0] += 1
            n = 2048 if half else 4096
            fw.dma(sp, ring[r][:, 0:n], wbf_d[slot, :, 0:n], f"d_ring{r}", reads=[f"wbf{slot}_{q}" for q in range(2 if half else 4)], writes=[f"ring{r}"])
            return r

        with ExitStack() as ph:
            stg = sbuf(ph, "c_stg", [128, 512], F32)
            fw.dma(sp, stg[:], cmat_d, "d_cstg", writes=["c_stg"])
            fw.dma(sp, gains[:], gains_d, "d_gains", writes=["gains"])
            fw.dma(sp, gpost[:], gpost_d.partition_broadcast(128), "d_gpost", writes=["gpost"])
            fw.dma(sp, edge[:].rearrange("p a g i -> p (a g i)"), edge_d, "d_edge", writes=["edge"])
            fw.op(dve, lambda: V.tensor_copy(out=cmat[:].rearrange("p a c -> p (a c)"), in_=stg[:]), reads=["c_stg"], writes=["cmat"])
            fw.op(dve, lambda: V.memset(epsc[:, 0:1], EPS), writes=["epsc"])
            fw.op(dve, lambda: V.memset(epsc[:, 1:2], 64.0 * EPS), writes=["epsc"])
            fw.barrier()
        if stop == 1:
            return nc

        GPRE, GPLE = 0, 8
        gain_slots = {}
        for g in range(4):
            gain_slots[S_UAZA + g] = ("fm", GPRE)
        for h_ in range(2):
            for s0 in (S_MA, S_Q, S_ZB, S_MB):
                gain_slots[s0 + h_] = ("fm", GPRE)
            gain_slots[S_WG + h_] = ("tm", GPLE)
        prep_state = {"u": 0, "sf": None, "sbb": None}

        def prep_unit(src_ap, dst_ap, dst_key, layout, gcol):
            u = prep_state["u"]
            prep_state["u"] += 1
            b = u % 2
            sf, sbb = prep_state["sf"][b], prep_state["sbb"][b]
            fw.dma(sp, sf[:], src_ap, f"d_wsf{b}", writes=[sf.name], phase_local=True)
            if layout is None:
                fw.op(act, lambda: S.copy(out=sbb[:], in_=sf[:]), reads=[sf.name], writes=[sbb.name])
            else:
                nk, w_, g0 = layout
                fw.op(dve, lambda: V.tensor_tensor(out=sbb[:].rearrange("p (k c) -> p k c", k=nk), in0=sf[:].rearrange("p (k c) -> p k c", k=nk),
                                                   in1=gains[:, gcol + g0:gcol + g0 + nk].unsqueeze(2).broadcast_to([128, nk, w_]), op=ALU.mult),
                      reads=[sf.name, "gains"], writes=[sbb.name])
            fw.dma(pool, dst_ap, sbb[:], f"d_wsb{b}", reads=[sbb.name], writes=[dst_key])

        prep_units = []
        for q4 in range(4):
            lay = (8, 128, 0) if q4 < 2 else (4, 256, 4 * (q4 - 2))
            prep_units.append(lambda q4=q4, lay=lay: prep_unit(wkv_d[:, q4 * 1024:(q4 + 1) * 1024], wkvbf_d[:, q4 * 1024:(q4 + 1) * 1024], f"wkvbf{q4}", lay, GPRE))
        N_WKV_UNITS = 4
        for s_ in range(NSLOT):
            lay0, gc = gain_slots.get(s_, (None, 0))
            nq = 2 if s_ in (S_POOL, S_WPLE) else 4
            for q4 in range(nq):
                if lay0 == "fm":
                    lay = (8, 128, 0)
                elif lay0 == "tm":
                    lay = (2, 512, 2 * q4)
                else:
                    lay = None
                prep_units.append(lambda s_=s_, q4=q4, lay=lay, gc=gc: prep_unit(wsl_d[s_, :, q4 * 1024:(q4 + 1) * 1024], wbf_d[s_, :, q4 * 1024:(q4 + 1) * 1024],
                                                                               f"wbf{s_}_{q4}", lay, gc))

        def rstd_from(ss_ap, out_ap, scale, eps_col, tmp_ap, rkeys, wkey):
            n_ = ss_ap.shape[0]
            fw.op(act, lambda: S.activation(out=tmp_ap, in_=ss_ap, func=AF.Ln, bias=epsc[0:n_, eps_col:eps_col + 1], scale=scale),
                  reads=rkeys + ["epsc"], writes=[wkey + "_t"])
            fw.op(act, lambda: S.activation(out=out_ap, in_=tmp_ap, func=AF.Exp, scale=-0.5), reads=[wkey + "_t"], writes=[wkey])

        def norm_transpose(x_ap, nrows, junk, hn, dst_fn, key):
            fw.op(act, lambda: S.activation(out=junk[0:nrows, :], in_=x_ap, func=AF.Square, accum_out=stat[0:nrows, 12:13]),
                  reads=[key], writes=["stat12", "junk"])
            rstd_from(stat[0:nrows, 12:13], stat[0:nrows, 14:15], 1.0 / D, 0, stat[0:nrows, 13:14], ["stat12"], "stat14")
            fw.op(dve, lambda: V.tensor_scalar(out=hn[0:nrows, :], in0=x_ap, scalar1=stat[0:nrows, 14:15], scalar2=None, op0=ALU.mult),
                  reads=[key, "stat14"], writes=[hn.name])
            fw.op(pe, [tr(pst[:, kc * 128:kc * 128 + nrows], hn[0:nrows, kc * 128:(kc + 1) * 128]) for kc in range(8)],
                  reads=[hn.name, "cmat"], writes=["ps7"])
            dst_fn()

        def qk_stage1(bank, gcol, tmp, c):
            sq, qg = tmp["sq"], tmp["qg"]
            if stop != 322:
                fw.op(act, lambda: S.activation(out=sq[:], in_=ps[:, bank, :], func=AF.Square), reads=[f"ps{bank}"], writes=[sq.name])
            if stop == 321:
                return
            fw.op(act, lambda: S.activation(out=qg[:], in_=ps[:, bank, :], func=AF.Copy, scale=gains[:, gcol:gcol + 1]),
                  reads=[f"ps{bank}", "gains"], writes=[qg.name])

        def qk_stage2(tmp, dst_ap, dst_key, bss, brq):
            sq, qg, t1, t2, rs = tmp["sq"], tmp["qg"], tmp["t1"], tmp["t2"], tmp["rs"]
            fw.op(pe, [mm(ps[:, bss, :], onesblk, sq[:])], reads=[sq.name, "cmat"], writes=[f"ps{bss}"])
            fw.op(pe, [mm(ps[:, brq, :], rmatT, qg[:])], reads=[qg.name, "cmat"], writes=[f"ps{brq}"])
            fw.op(pool, lambda: G.tensor_tensor(out=t1[:], in0=qg[:], in1=cs_t[:, 0, :], op=ALU.mult), reads=[qg.name, "cs_t"], writes=[t1.name])
            fw.op(dve, lambda: V.tensor_tensor(out=t2[:], in0=ps[:, brq, :], in1=cs_t[:, 1, :], op=ALU.mult), reads=[f"ps{brq}", "cs_t"], writes=[t2.name])
            fw.op(act, lambda: S.activation(out=rs[:], in_=ps[:, bss, :], func=AF.Ln, bias=epsc[:, 1:2], scale=1.0),
                  reads=[f"ps{bss}", "epsc"], writes=[rs.name])
            fw.op(dve, lambda: V.tensor_tensor(out=t1[:], in0=t1[:], in1=t2[:], op=ALU.add), reads=[t1.name, t2.name], writes=[t1.name])
            fw.op(act, lambda: S.activation(out=rs[:], in_=rs[:], func=AF.Exp, scale=-0.5), reads=[rs.name], writes=[rs.name])
            fw.op(dve, lambda: V.tensor_tensor(out=dst_ap, in0=t1[:], in1=rs[:], op=ALU.mult), reads=[t1.name, rs.name], writes=[dst_key])

        def alloc_qk_tmp(ph, tag):
            return dict(sq=sbuf(ph, f"sq{tag}", [128, ST], BF16), qg=sbuf(ph, f"qg{tag}", [128, ST], BF16),
                        t1=sbuf(ph, f"t1{tag}", [128, ST], F32), t2=sbuf(ph, f"t2{tag}", [128, ST], F32),
                        rs=sbuf(ph, f"rs{tag}", [128, ST], F32))

        LIN_BANKS = [0, 1, 2, 3]

        def load_x_tile(xd, t0):
            fw.dma(sp, xt[:], xd[t0:t0 + ST, :].rearrange("(s p) d -> p s d", p=128), "d_xt", writes=["xt"])

        def mh_stats(junk):
            for s in range(4):
                fw.op(act, lambda s=s: S.activation(out=junk[:], in_=xt[:, s, :], func=AF.Square, accum_out=stat[:, s:s + 1]),
                      reads=["xt"], writes=[f"st_ss{s}", "junk"])
            fw.op(act, lambda: S.activation(out=stat[:, 4:8], in_=stat[:, 0:4], func=AF.Ln, bias=epsc[:, 0:1], scale=1.0 / D),
                  reads=[f"st_ss{s}" for s in range(4)] + ["epsc"], writes=["st_ln"])
            fw.op(act, lambda: S.activation(out=stat[:, 8:12], in_=stat[:, 4:8], func=AF.Exp, scale=-0.5), reads=["st_ln"], writes=["st_rs"])

        def mh_scale(s, hn):
            fw.op(dve, lambda: V.tensor_scalar(out=hn[:], in0=xt[:, s, :], scalar1=stat[:, 8 + s:9 + s], scalar2=None, op0=ALU.mult),
                  reads=["xt", "st_rs"], writes=[hn.name])

        def mh_tr_copy(s, hn, tb=None):
            if tb is None:
                tb = 6 + (s % 2)
            pv_ = ps[:, tb, :].bitcast(BF16)
            fw.op(pe, [tr(pv_[:, kc * 128:(kc + 1) * 128], hn[:, kc * 128:(kc + 1) * 128]) for kc in range(8)],
                  reads=[hn.name, "cmat"], writes=[f"ps{tb}"])
            fw.op(act, lambda: S.copy(out=hT[:, :, s * 128:(s + 1) * 128], in_=pv_.rearrange("p (k c) -> p k c", k=8)),
                  reads=[f"ps{tb}"], writes=["hT"])

        def make_hT(junk, hn2):
            mh_stats(junk)
            for s in range(4):
                mh_scale(s, hn2[s % 2])
                mh_tr_copy(s, hn2[s % 2])

        for si, T in enumerate(seqs):
            NK = T // 128
            NST = T // ST
            xd, pd, yd = x_d[si], p_d[si], y_d[si]

            with ExitStack() as ph:
                junk = sbuf(ph, "junk", [128, D], BF16)
                hn2 = [sbuf(ph, f"hn{i}", [128, D], BF16) for i in range(2)]
                tmps = [alloc_qk_tmp(ph, f"a{i}") for i in range(2)]
                wkv = sbuf(ph, "wkv", [128, 4096], BF16)
                if si == 0:
                    prep_state["sf"] = [sbuf(ph, f"w_sf{i}", [128, 1024], F32) for i in range(2)]
                    prep_state["sbb"] = [sbuf(ph, f"w_sb{i}", [128, 1024], BF16) for i in range(2)]
                    for pu in prep_units[:N_WKV_UNITS]:
                        pu()
                    rest = prep_units[N_WKV_UNITS:]
                    per_it = -(-len(rest) // NST)
                fw.dma(sp, wkv[:], wkvbf_d, "d_wkv", reads=[f"wkvbf{q}" for q in range(4)], writes=["wkv"], phase_local=True)
                wk = wkv[:, 0:2048].rearrange("p (j k c) -> p j k c", j=2, k=8)
                wv = wkv[:, 2048:4096].rearrange("p (k c) -> p k c", k=8)
                load_x_tile(xd, 0)
                for it in range(NST):
                    t0 = it * ST
                    fw.dma(sp, cs_t[:], cs_d[:, :, t0:t0 + ST], "d_cs", writes=["cs_t"])
                    if stop == 30:
                        fw.barrier(); return nc
                    make_hT(junk, hn2)
                    if it + 1 < NST:
                        load_x_tile(xd, t0 + ST)
                    if stop == 31:
                        fw.barrier(); return nc
                    banks = []
                    for j in range(2):
                        b = next_bank(LIN_BANKS)
                        banks.append(b)
                        fw.op(pe, [mm(ps[:, b, :], wk[:, j, kc, :], hT[:, kc, :], kc == 0, kc == 7) for kc in range(8)],
                              reads=["hT", "wkv"], writes=[f"ps{b}"])
                        if stop == 320:
                            continue
                        qk_stage1(b, 25, tmps[j], j)
                    if stop in (32, 320, 321, 322):
                        fw.barrier(); return nc
                    for s in range(4):
                        b = next_bank(LIN_BANKS)
                        fw.op(pe, [mm(ps[:, b, 0:256], hT[:, kc, s * 128:(s + 1) * 128], wv[:, kc, :], kc == 0, kc == 7) for kc in range(8)],
                              reads=["hT", "wkv"], writes=[f"ps{b}"])
                        kt = it * 4 + s
                        fw.op(dve, lambda b=b, kt=kt: V.tensor_copy(out=Vt[:, kt, :], in_=ps[:, b, 0:256]), reads=[f"ps{b}"], writes=["Vt"])
                    if stop == 33:
                        fw.barrier(); return nc
                    for j in range(2):
                        qk_stage2(tmps[j], KT[:, j, t0:t0 + ST], "KT", 4 + j, 6)
                    if si == 0:
                        for pu in rest[it * per_it:(it + 1) * per_it]:
                            pu()
                fw.barrier()
                dump("KT", KT, ["KT"], si == 0)
                dump("Vt", Vt, ["Vt"], si == 0)
            if stop == 3:
                return nc

            load_x_tile(xd, 0)
            fw.dma(sp, cs_t[:], cs_d[:, :, 0:ST], "d_cs", writes=["cs_t"])
            for it in range(NST):
                t0 = it * ST
                first, last = (it == 0), (it == NST - 1)
                with ExitStack() as ph:
                    junk = sbuf(ph, "junk", [128, D], BF16)
                    hn2 = [sbuf(ph, f"hn{i}", [128, D], BF16) for i in range(2)]
                    hn = hn2[0]
                    xh = sbuf(ph, "xh", [16, D], F32)
                    poolw_t = sbuf(ph, "poolw", [128, 2048], BF16)
                    pa = ExitStack()
                    uext = [sbuf(pa, f"uext{i}", [128, 2, ST + 16], F32) for i in range(2)]
                    pwa = sbuf(pa, "pwa", [128, 2, ST + 16], F32)
                    pwb = sbuf(pa, "pwb", [128, 2, ST + 16], F32)
                    pooled = [sbuf(pa, f"pooled{i}", [128, 2, ST], BF16) for i in range(2)]
                    sza = [sbuf(pa, f"sza{i}", [128, 2, ST], BF16) for i in range(2)]
                    sma = [sbuf(pa, f"sma{i}", [128, ST], BF16) for i in range(4)]
                    etmp = sbuf(pa, "etmp", [128, 2, 8], F32)
                    aT = szb

                    fw.op(dve, lambda: V.memset(xh[:], 0.0), writes=["xh"])
                    if not first:
                        fw.dma(sp, xh[0:8, :], xd[t0 - 8:t0, :], "d_xh", writes=["xh"], phase_local=True)
                    if not last:
                        fw.dma(sp, xh[8:16, :], xd[t0 + ST:t0 + ST + 8, :], "d_xh", writes=["xh"], phase_local=True)
                    if it == 0:
                        make_hT(junk, hn2)

                    def dsth():
                        fw.op(act, lambda: S.copy(out=hTh[:], in_=pst.rearrange("p (k c) -> p k c", k=8)[:, :, 0:16]), reads=["ps7"], writes=["hTh"])
                    norm_transpose(xh[:], 16, junk, hn, dsth, "xh")
                    dump("hT", hT, ["hT"], si == 0 and it == 0)
                    dump("hTh", hTh, ["hTh"], si == 0 and it == 0)

                    fw.dma(sp, poolw_t[:], wbf_d[S_POOL, :, 0:2048], "d_poolw", reads=[f"wbf{S_POOL}_0", f"wbf{S_POOL}_1"], writes=["poolw"], phase_local=True)
                    poolw = poolw_t[:].rearrange("p (g o k c) -> p g o k c", g=4, o=2, k=2)

                    def proj_a(g):
                        r = load_slot(S_UAZA + g)
                        wsl = ring[r][:].rearrange("p (o k c) -> p o k c", o=4, k=8)
                        ue, sz = uext[g % 2], sza[g % 2]
                        for b2 in range(2):
                            bk = next_bank(LIN_BANKS)
                            fw.op(pe, [mm(ps[:, bk, :], wsl[:, b2, kc, :], hT[:, kc, :], kc == 0, kc == 7) for kc in range(8)],
                                  reads=["hT", f"ring{r}"], writes=[f"ps{bk}"])
                            fw.op(act, lambda: S.copy(out=ue[:, b2, 8:8 + ST], in_=ps[:, bk, :]), reads=[f"ps{bk}"], writes=[ue.name])
                            bh = next_bank([4, 5])
                            fw.op(pe, [mm(ps[:, bh, 0:16], wsl[:, b2, kc, :], hTh[:, kc, :], kc == 0, kc == 7) for kc in range(8)],
                                  reads=["hTh", f"ring{r}"], writes=[f"ps{bh}"])
                            fw.op(dve, lambda: V.tensor_copy(out=ue[:, b2, 0:8], in_=ps[:, bh, 0:8]), reads=[f"ps{bh}"], writes=[ue.name])
                            fw.op(dve, lambda: V.tensor_copy(out=ue[:, b2, 8 + ST:16 + ST], in_=ps[:, bh, 8:16]), reads=[f"ps{bh}"], writes=[ue.name])
                        for b2 in range(2):
                            bk = next_bank(LIN_BANKS)
                            fw.op(pe, [mm(ps[:, bk, :], wsl[:, 2 + b2, kc, :], hT[:, kc, :], kc == 0, kc == 7) for kc in range(8)],
                                  reads=["hT", f"ring{r}"], writes=[f"ps{bk}"])
                            fw.op(act, lambda: S.activation(out=sz[:, b2, :], in_=ps[:, bk, :], func=AF.Silu), reads=[f"ps{bk}"], writes=[sz.name])

                    def pool_mix(g):
                        w = POOL_W[g]
                        ue, pl, sz = uext[g % 2], pooled[g % 2], sza[g % 2]
                        L = ST + 16
                        cur, step, k_ = ue, 1, 0
                        while step < w:
                            nxt = pwa if k_ % 2 == 0 else pwb
                            e, h = (pool, G) if k_ % 2 == 0 else (dve, V)
                            fw.op(e, lambda: h.tensor_tensor(out=nxt[:, :, 0:L - step], in0=cur[:, :, 0:L - step], in1=cur[:, :, step:L], op=ALU.add),
                                  reads=[cur.name], writes=[nxt.name])
                            L -= step
                            cur = nxt
                            step *= 2
                            k_ += 1
                        off = 8 - w // 2
                        fw.op(dve, lambda: V.scalar_tensor_tensor(out=pl[:], in0=cur[:, :, off:off + ST], scalar=1.0 / w, in1=ue[:, :, 8:8 + ST],
                                                                 op0=ALU.mult, op1=ALU.subtract),
                              reads=[cur.name, ue.name], writes=[pl.name])
                        for (is_edge, a_, j0) in ((first, 0, 0), (last, 1, ST - 8)):
                            if not is_edge:
                                continue
                            fw.op(dve, lambda: V.tensor_tensor(out=etmp[:], in0=cur[:, :, off + j0:off + j0 + 8],
                                                               in1=edge[:, a_, g, :].unsqueeze(1).broadcast_to([128, 2, 8]), op=ALU.mult),
                                  reads=[cur.name, "edge"], writes=["etmp"])
                            fw.op(dve, lambda: V.tensor_tensor(out=pl[:, :, j0:j0 + 8], in0=etmp[:], in1=ue[:, :, 8 + j0:16 + j0], op=ALU.subtract),
                                  reads=["etmp", ue.name, pl.name], writes=[pl.name])
                        for o2 in range(2):
                            bk = next_bank(LIN_BANKS)
                            fw.op(pe, [mm(ps[:, bk, :], poolw[:, g, o2, k2, :], pl[:, k2, :], k2 == 0, k2 == 1) for k2 in range(2)],
                                  reads=[pl.name, "poolw"], writes=[f"ps{bk}"])
                            oc = 2 * g + o2
                            fw.op(dve, lambda: V.scalar_tensor_tensor(out=aT[:, oc, :], in0=ps[:, bk, :], scalar=gains[:, 16 + oc:17 + oc], in1=sz[:, o2, :],
                                                                     op0=ALU.mult, op1=ALU.mult),
                                  reads=[f"ps{bk}", "gains", sz.name], writes=["szb"])

                    def ma_proj(w1, r1, o, oc):
                        sm = sma[oc % 4]
                        bk = next_bank(LIN_BANKS)
                        fw.op(pe, [mm(ps[:, bk, :], w1[:, o, kc, :], hT[:, kc, :], kc == 0, kc == 7) for kc in range(8)],
                              reads=["hT", f"ring{r1}"], writes=[f"ps{bk}"])
                        fw.op(act, lambda: S.activation(out=sm[:], in_=ps[:, bk, :], func=AF.Sigmoid), reads=[f"ps{bk}"], writes=[sm.name])

                    def wa_proj(w2, r2, o, oc):
                        sm = sma[oc % 4]
                        bk2 = next_bank(LIN_BANKS)
                        fw.op(pe, [mm(ps[:, bk2, :], w2[:, o, kc, :], aT[:, kc, :], kc == 0, kc == 7) for kc in range(8)],
                              reads=["szb", f"ring{r2}"], writes=[f"ps{bk2}"])
                        fw.op(dve, lambda: V.tensor_tensor(out=tA[:, oc, :], in0=ps[:, bk2, :], in1=sm[:], op=ALU.mult),
                              reads=[f"ps{bk2}", sm.name], writes=["tA"])

                    proj_a(0)
                    for g in range(3):
                        proj_a(g + 1)
                        pool_mix(g)
                    r1 = load_slot(S_MA + 0)
                    w1 = ring[r1][:].rearrange("p (o k c) -> p o k c", o=4, k=8)
                    for o in range(4):
                        ma_proj(w1, r1, o, o)
                    pool_mix(3)
                    r2 = load_slot(S_WA + 0)
                    w2 = ring[r2][:].rearrange("p (o k c) -> p o k c", o=4, k=8)
                    for o in range(4):
                        wa_proj(w2, r2, o, o)
                    r1 = load_slot(S_MA + 1)
                    r2 = load_slot(S_WA + 1)
                    w1 = ring[r1][:].rearrange("p (o k c) -> p o k c", o=4, k=8)
                    w2 = ring[r2][:].rearrange("p (o k c) -> p o k c", o=4, k=8)
                    for o in range(4):
                        ma_proj(w1, r1, o, 4 + o)
                        wa_proj(w2, r2, o, 4 + o)
                    dump("tA", tA, ["tA"], si == 0 and it == 0)
                    dump("aT", szb, ["szb"], si == 0 and it == 0)
                    fw.barrier()
                    pa.close()
                    tmps = [alloc_qk_tmp(ph, f"b{i}") for i in range(2)]
                    pend = None
                    cidx = 0
                    for j in range(2):
                        r = load_slot(S_Q + j)
                        wsl = ring[r][:].rearrange("p (o k c) -> p o k c", o=4, k=8)
                        for i in range(4):
                            c = 4 * j + i
                            bk = next_bank(LIN_BANKS)
                            fw.op(pe, [mm(ps[:, bk, :], wsl[:, i, kc, :], hT[:, kc, :], kc == 0, kc == 7) for kc in range(8)],
                                  reads=["hT", f"ring{r}"], writes=[f"ps{bk}"])
                            tm_ = tmps[cidx % 2]
                            if pend is not None:
                                qk_stage2(*pend)
                            qk_stage1(bk, 24, tm_, c)
                            pend = (tm_, QT[:, c, :], f"QT{c}", 4 + (cidx % 2), 6)
                            cidx += 1
                    qk_stage2(*pend)
                    for (s0, dstt, func) in ((S_ZB, szb, AF.Silu),):
                        for j in range(2):
                            r = load_slot(s0 + j)
                            wsl = ring[r][:].rearrange("p (o k c) -> p o k c", o=4, k=8)
                            for i in range(4):
                                c = 4 * j + i
                                bk = next_bank(LIN_BANKS)
                                fw.op(pe, [mm(ps[:, bk, :], wsl[:, i, kc, :], hT[:, kc, :], kc == 0, kc == 7) for kc in range(8)],
                                      reads=["hT", f"ring{r}"], writes=[f"ps{bk}"])
                                fw.op(act, lambda bk=bk, c=c, dstt=dstt, func=func: S.activation(out=dstt[:, c, :], in_=ps[:, bk, :], func=func),
                                      reads=[f"ps{bk}"], writes=["szb"])
                    fw.barrier()
                    dump("QT", QT, [], si == 0 and it == 0)
                    dump("szb", szb, [], si == 0 and it == 0)
                if stop == 4:
                    return nc

                if it + 1 < NST:
                    load_x_tile(xd, t0 + ST)
                    fw.dma(sp, cs_t[:], cs_d[:, :, t0 + ST:t0 + 2 * ST], "d_cs", writes=["cs_t"])
                with ExitStack() as ph:
                    bT = sbuf(ph, "bT", [128, 8, ST], BF16)
                    rinv = sbuf(ph, "rinv", [128, ST], F32)
                    wgt = sbuf(ph, "wgt", [128, ST], F32)
                    mrg = QT
                    smb = szb
                    tB = sbuf(ph, "tB", [128, ST], F32)
                    steps = [(c, kt) for c in range(8) for kt in range(NK)]
                    ptile4 = [sbuf(ph, f"ptile{i}", [128, PLE], F32) for i in range(4)]
                    ptb4 = [sbuf(ph, f"ptb{i}", [128, PLE], BF16) for i in range(4)]
                    for s4 in range(4):
                        fw.dma(sp, ptile4[s4][:], pd[t0 + s4 * 128:t0 + (s4 + 1) * 128, :], f"d_ptile{s4}", writes=[ptile4[s4].name], phase_local=True)
                        fw.op(pool, lambda s4=s4: G.tensor_copy(out=ptb4[s4][:], in_=ptile4[s4][:]), reads=[ptile4[s4].name], writes=[ptb4[s4].name])
                    pa2 = ExitStack()
                    NPT = 6
                    PT = [sbuf(pa2, f"PT{i}", [128, 2, ST], BF16) for i in range(NPT)]
                    PS2 = [sbuf(pa2, f"PS2_{i}", [128, 2, ST], BF16) for i in range(2)]
                    PS4 = [sbuf(pa2, f"PS4_{i}", [128, 2, ST], BF16) for i in range(2)]
                    RSG = 4 if NK % 4 == 0 else (2 if NK % 2 == 0 else 1)

                    def emit_qk(n):
                        c, kt = steps[n]
                        j = c // 4
                        sb_ = 2 * (n % 2)
                        fw.op(pe, [mm(ps[:, sb_, :], KT[0:64, j, kt * 128:(kt + 1) * 128], QT[0:64, c, :]),
                                   mm(ps[:, sb_ + 1, :], KT[64:128, j, kt * 128:(kt + 1) * 128], QT[64:128, c, :])],
                              reads=["KT", f"QT{c}"], writes=[f"ps{sb_}", f"ps{sb_ + 1}"])

                    def emit_rs(c, kt, src):
                        rb = 5 + 2 * (c % 2)
                        st_, sp_ = (kt == RSG - 1), (kt == NK - 1)
                        fw.op(pe, [mm(ps[0:64, rb, :], ones[:, 0:64], src[:, 0, :], st_, sp_),
                                   mm(ps[64:128, rb, :], ones[:, 64:128], src[:, 1, :], st_, sp_)],
                              reads=[src.name, "cmat"], writes=[f"ps{rb}"])
                        if kt == NK - 1:
                            ob = rb - 1
                            fw.op(dve, lambda: V.reciprocal(out=rinv[:], in_=ps[:, rb, :]), reads=[f"ps{rb}"], writes=["rinv"])
                            fw.op(pool, lambda: G.tensor_tensor(out=wgt[:], in0=rinv[:], in1=szb[:, c, :], op=ALU.mult), reads=["rinv", "szb"], writes=["wgt"])

                            def fin():
                                fw.op(dve, lambda: V.tensor_tensor(out=bT[:, c, :], in0=ps[:, ob, :], in1=wgt[:], op=ALU.mult),
                                      reads=[f"ps{ob}", "wgt"], writes=["bT"])
                            deferred.append((cur_n[0] + 3, c, fin))

                    deferred = []
                    cur_n = [0]
                    npair = [0]
                    nquad = [0]
                    emit_qk(0)
                    emit_qk(1)
                    for n, (c, kt) in enumerate(steps):
                        j = c // 4
                        cur_n[0] = n
                        if kt == 0:
                            while any(d[1] <= c - 2 for d in deferred):
                                deferred.sort(key=lambda d: d[0])
                                i_ = next(i for i, d in enumerate(deferred) if d[1] <= c - 2)
                                deferred.pop(i_)[2]()
                        sb_ = 2 * (n % 2)
                        pt = PT[n % NPT]
                        fw.op(act, lambda sb_=sb_, pt=pt: S.activation(out=pt[:], in_=ps[:, sb_:sb_ + 2, :], func=AF.Exp, scale=8.0),
                              reads=[f"ps{sb_}", f"ps{sb_ + 1}"], writes=[pt.name])
                        if n + 2 < len(steps):
                            emit_qk(n + 2)
                        ob = 4 + 2 * (c % 2)
                        st_, sp_ = (kt == 0), (kt == NK - 1)
                        fw.op(pe, [mm(ps[0:64, ob, :], Vt[:, kt, 128 * j:128 * j + 64], pt[:, 0, :], st_, sp_),
                                   mm(ps[64:128, ob, :], Vt[:, kt, 128 * j + 64:128 * j + 128], pt[:, 1, :], st_, sp_)],
                              reads=[pt.name, "Vt"], writes=[f"ps{ob}"])
                        if RSG == 1:
                            deferred.append((n, c, lambda c=c, kt=kt, pt=pt: emit_rs(c, kt, pt)))
                        else:
                            if kt % 2 == 1:
                                p2 = PS2[npair[0] % 2]
                                npair[0] += 1
                                pprev = PT[(n - 1) % NPT]
                                fw.op(dve, lambda p2=p2, pprev=pprev, pt=pt: V.tensor_tensor(out=p2[:], in0=pprev[:], in1=pt[:], op=ALU.add),
                                      reads=[pprev.name, pt.name], writes=[p2.name])
                                if RSG == 2:
                                    deferred.append((n + 4, c, lambda c=c, kt=kt, p2=p2: emit_rs(c, kt, p2)))
                            if RSG == 4 and kt % 4 == 3:
                                p4 = PS4[nquad[0] % 2]
                                nquad[0] += 1
                                fw.op(dve, lambda p4=p4: V.tensor_tensor(out=p4[:], in0=PS2[0][:], in1=PS2[1][:], op=ALU.add),
                                      reads=[PS2[0].name, PS2[1].name], writes=[p4.name])
                                deferred.append((n + 4, c, lambda c=c, kt=kt, p4=p4: emit_rs(c, kt, p4)))
                        deferred.sort(key=lambda d: d[0])
                        while deferred and deferred[0][0] <= n:
                            deferred.pop(0)[2]()
                            deferred.sort(key=lambda d: d[0])
                    def mb_proj(wsl, r, i):
                        bk = next_bank(LIN_BANKS)
                        fw.op(pe, [mm(ps[:, bk, :], wsl[:, i, kc, :], hT[:, kc, :], kc == 0, kc == 7) for kc in range(8)],
                              reads=["hT", f"ring{r}"], writes=[f"ps{bk}"])
                        return bk

                    def mb_act(bk, c):
                        fw.op(act, lambda: S.activation(out=smb[:, c, :], in_=ps[:, bk, :], func=AF.Sigmoid), reads=[f"ps{bk}"], writes=["szb"])

                    r_mb0 = load_slot(S_MB + 0)
                    wsl0 = ring[r_mb0][:].rearrange("p (o k c) -> p o k c", o=4, k=8)
                    mb_banks = [mb_proj(wsl0, r_mb0, i) for i in range(4)]
                    cur_n[0] = len(steps) + 10
                    while deferred:
                        deferred.sort(key=lambda d: d[0])
                        deferred.pop(0)[2]()
                    fw.barrier()
                    pa2.close()
                    dump("bT", bT, ["bT"], si == 0 and it == 0)
                    if stop == 5:
                        fw.barrier()
                        return nc
                    for i in range(4):
                        mb_act(mb_banks[i], i)
                    for s4 in range(4):
                        pb = 0 + (s4 % 2)
                        pv_ = ps[:, pb, :].bitcast(BF16)
                        fw.op(pe, [tr(pv_[:, k2 * 128:(k2 + 1) * 128], ptb4[s4][:, k2 * 128:(k2 + 1) * 128]) for k2 in range(2)],
                              reads=[ptb4[s4].name, "cmat"], writes=[f"ps{pb}"])
                        fw.op(dve, lambda s4=s4, pv_=pv_: V.tensor_copy(out=pTt[:, :, s4 * 128:(s4 + 1) * 128], in_=pv_[:, 0:256].rearrange("p (k c) -> p k c", k=2)),
                              reads=[f"ps{pb}"], writes=["pTt"])
                    r = load_slot(S_MB + 1)
                    wsl = ring[r][:].rearrange("p (o k c) -> p o k c", o=4, k=8)
                    for i in range(4):
                        mb_act(mb_proj(wsl, r, i), 4 + i)
                    for h_ in range(2):
                        r = load_slot(S_WB + h_)
                        wsl = ring[r][:].rearrange("p (o k c) -> p o k c", o=4, k=8)
                        for o in range(4):
                            oc = 4 * h_ + o
                            bk = next_bank(LIN_BANKS)
                            fw.op(pe, [mm(ps[:, bk, :], wsl[:, o, kc, :], bT[:, kc, :], kc == 0, kc == 7) for kc in range(8)],
                                  reads=["bT", f"ring{r}"], writes=[f"ps{bk}"])
                            fw.op(dve, lambda bk=bk, oc=oc: V.tensor_tensor(out=tB[:], in0=ps[:, bk, :], in1=smb[:, oc, :], op=ALU.mult),
                                  reads=[f"ps{bk}", "szb"], writes=["tB"])
                            fw.op(dve, lambda oc=oc: V.tensor_tensor(out=mrg[:, oc, :], in0=tB[:], in1=tA[:, oc, :], op=ALU.add),
                                  reads=["tB", "tA"], writes=[f"QT{oc}"])
                    dump("mrg", QT, [], si == 0 and it == 0)
                    junk = sbuf(ph, "junk2", [128, D], BF16)
                    xres2 = [sbuf(ph, f"xres{i}", [128, D], F32) for i in range(2)]
                    ty2 = [sbuf(ph, f"ty{i}", [128, D], F32) for i in range(2)]
                    x2n2 = [sbuf(ph, f"x2n{i}", [128, D], BF16) for i in range(2)]
                    x1v = tA[:].rearrange("p a c -> p (a c)").rearrange("p (s d) -> p s d", s=4)
                    rwo = [load_slot(S_WO + hf) for hf in range(2)]

                    def stage_a(s):
                        tok = slice(s * 128, (s + 1) * 128)
                        b = s % 2
                        xres, ty = xres2[b], ty2[b]
                        fw.dma(sp, xres[:], xd[t0 + s * 128:t0 + (s + 1) * 128, :], f"d_xres{b}", writes=[xres.name], phase_local=True)
                        yb = (0, 2, 4)[s % 3]
                        for hf in range(2):
                            wt = ring[rwo[hf]][:].rearrange("p (k c) -> p k c", k=8)
                            fw.op(pe, [mm(ps[:, yb + hf, :], mrg[:, kc, tok], wt[:, kc, :], kc == 0, kc == 7) for kc in range(8)],
                                  reads=[f"QT{c_}" for c_ in range(8)] + [f"ring{rwo[hf]}"], writes=[f"ps{yb + hf}"])
                        fw.op(act, lambda: S.activation(out=junk[:].rearrange("p (a c) -> p a c", a=2), in_=ps[:, yb:yb + 2, :], func=AF.Square,
                                                        accum_out=stat[:, 16 + s:17 + s]),
                              reads=[f"ps{yb}", f"ps{yb + 1}"], writes=[f"t1ss{s}", "junk"])
                        fw.op(act, lambda: S.activation(out=stat[:, 20 + s:21 + s], in_=stat[:, 16 + s:17 + s], func=AF.Ln, bias=epsc[:, 0:1], scale=1.0 / D),
                              reads=[f"t1ss{s}", "epsc"], writes=[f"t1ln{s}"])
                        fw.op(act, lambda: S.activation(out=stat[:, 24 + s:25 + s], in_=stat[:, 20 + s:21 + s], func=AF.Exp, scale=-0.5),
                              reads=[f"t1ln{s}"], writes=[f"t1rs{s}"])
                        fw.op(dve, lambda: V.scalar_tensor_tensor(out=ty[:].rearrange("p (a c) -> p a c", a=2), in0=ps[:, yb:yb + 2, :], scalar=stat[:, 24 + s:25 + s],
                                                                 in1=gpost[:].rearrange("p (a c) -> p a c", a=2), op0=ALU.mult, op1=ALU.mult),
                              reads=[f"ps{yb}", f"ps{yb + 1}", f"t1rs{s}", "gpost"], writes=[ty.name])
                        fw.op(dve, lambda: V.tensor_tensor(out=x1v[:, s, :], in0=ty[:], in1=xres[:], op=ALU.add), reads=[ty.name, xres.name], writes=[f"x1_{s}", "tA"])

                    def stage_b(s):
                        tok = slice(s * 128, (s + 1) * 128)
                        b = s % 2
                        x2n = x2n2[b]
                        tb = 6 + b
                        fw.op(act, lambda: S.activation(out=junk[:], in_=x1v[:, s, :], func=AF.Square, accum_out=stat[:, 28 + s:29 + s]),
                              reads=[f"x1_{s}"], writes=[f"t2ss{s}", "junk"])
                        fw.op(act, lambda: S.activation(out=stat[:, 32 + s:33 + s], in_=stat[:, 28 + s:29 + s], func=AF.Ln, bias=epsc[:, 0:1], scale=1.0 / D),
                              reads=[f"t2ss{s}", "epsc"], writes=[f"t2ln{s}"])
                        fw.op(act, lambda: S.activation(out=stat[:, 36 + s:37 + s], in_=stat[:, 32 + s:33 + s], func=AF.Exp, scale=-0.5),
                              reads=[f"t2ln{s}"], writes=[f"t2rs{s}"])
                        fw.op(dve, lambda: V.tensor_scalar(out=x2n[:], in0=x1v[:, s, :], scalar1=stat[:, 36 + s:37 + s], scalar2=None, op0=ALU.mult),
                              reads=[f"x1_{s}", f"t2rs{s}"], writes=[x2n.name])

                    def stage_btr(s):
                        x2n = x2n2[s % 2]
                        tb = 6 + (s % 2)
                        fw.op(pe, [tr(pstb[tb][:, kc * 128:(kc + 1) * 128], x2n[:, kc * 128:(kc + 1) * 128]) for kc in range(8)],
                              reads=[x2n.name, "cmat"], writes=[f"ps{tb}"])

                    def stage_b2(s):
                        tok = slice(s * 128, (s + 1) * 128)
                        tb = 6 + (s % 2)
                        fw.op(act, lambda: S.copy(out=szb[:, :, tok], in_=pstb[tb].rearrange("p (k c) -> p k c", k=8)), reads=[f"ps{tb}"], writes=["szb"])

                    nxt = it + 1 < NST
                    if nxt:
                        hnb = [sbuf(ph, f"hnb{i}", [128, D], BF16) for i in range(2)]
                        mh_stats(junk)
                        mh_scale(0, hnb[0])
                        mh_scale(1, hnb[1])
                    stage_a(0)
                    stage_a(1)
                    stage_b(0)
                    stage_a(2)
                    if nxt:
                        mh_tr_copy(0, hnb[0])
                        mh_tr_copy(1, hnb[1])
                        mh_scale(2, hnb[0])
                        mh_scale(3, hnb[1])
                    stage_btr(0)
                    stage_b(1)
                    stage_a(3)
                    if nxt:
                        mh_tr_copy(2, hnb[0], 2)
                        mh_tr_copy(3, hnb[1], 3)
                    stage_btr(1)
                    stage_b2(0)
                    stage_b(2)
                    stage_btr(2)
                    stage_b2(1)
                    stage_b(3)
                    stage_btr(3)
                    stage_b2(2)
                    stage_b2(3)
                    fw.barrier()
                    dump("x1", tA, [], si == 0 and it == 0)
                    dump("x2nT", szb, [], si == 0 and it == 0)
                    dump("pTt", pTt, [], si == 0 and it == 0)
                if stop == 6:
                    return nc
                with ExitStack() as ph:
                    sg2 = [sbuf(ph, f"sg{i}", [128, ST], F32) for i in range(2)]
                    tE2 = [sbuf(ph, f"tE{i}", [128, ST], F32) for i in range(2)]
                    outb = [sbuf(ph, f"outb{i}", [128, D], F32) for i in range(2)]
                    x1v = tA[:].rearrange("p a c -> p (a c)").rearrange("p (s d) -> p s d", s=4)
                    rwp = load_slot(S_WPLE, half=True)
                    wpl = ring[rwp][:, 0:2048].rearrange("p (k c) -> p k c", k=2)
                    rwg = [load_slot(S_WG + hf) for hf in range(2)]
                    for s in range(4):
                        tok = slice(s * 128, (s + 1) * 128)
                        ob_ = outb[s % 2]
                        for hf in range(2):
                            sg, tE = sg2[hf], tE2[hf]
                            wt = ring[rwg[hf]][:].rearrange("p (k c) -> p k c", k=8)
                            gb = next_bank([0, 1])
                            eb = next_bank([2, 3])
                            fw.op(pe, [mm(ps[:, gb, :], szb[:, kc, tok], wt[:, kc, :], kc == 0, kc == 7) for kc in range(8)],
                                  reads=["szb", f"ring{rwg[hf]}"], writes=[f"ps{gb}"])
                            fw.op(pe, [mm(ps[:, eb, :], pTt[:, k2, tok], wpl[:, k2, hf * 512:(hf + 1) * 512], k2 == 0, k2 == 1) for k2 in range(2)],
                                  reads=["pTt", f"ring{rwp}"], writes=[f"ps{eb}"])
                            fw.op(act, lambda gb=gb, sg=sg: S.activation(out=sg[:], in_=ps[:, gb, :], func=AF.Sigmoid), reads=[f"ps{gb}"], writes=[sg.name])
                            fw.op(dve, lambda eb=eb, sg=sg, tE=tE: V.tensor_tensor(out=tE[:], in0=ps[:, eb, :], in1=sg[:], op=ALU.mult), reads=[f"ps{eb}", sg.name], writes=[tE.name])
                            fw.op(pool, lambda s=s, hf=hf, ob_=ob_, tE=tE: G.tensor_tensor(out=ob_[:, hf * 512:(hf + 1) * 512], in0=tE[:], in1=x1v[:, s, hf * 512:(hf + 1) * 512], op=ALU.add),
                                  reads=[tE.name, f"x1_{s}"], writes=[ob_.name])
                        fw.dma(sp, yd[t0 + s * 128:t0 + (s + 1) * 128, :], ob_[:], f"d_{ob_.name}", reads=[ob_.name], phase_local=True)
                    fw.barrier()
        fw.barrier(sp=True)
    return nc


_CACHE = {}


def _run(x_list, p_list, inp, n_cores):
    seqs = tuple(int(x.shape[1]) for x in x_list)
    tmax = max(seqs)
    key = (seqs, tmax)
    if key not in _CACHE:
        import os
        _CACHE[key] = build_program(list(seqs), tmax, stop=int(os.environ.get("KSTOP", "0")))
    nc = _CACHE[key]
    slots, wkv, gains, gpost = pack_weights(inp)
    cm, cs, edge = make_consts(tmax)
    in_maps = []
    for c in range(n_cores):
        m = {"wslots": slots, "wkv": wkv, "gains": gains, "gpost": gpost, "cmat": cm, "cs": cs, "edge": edge}
        for i in range(len(seqs)):
            m[f"x{i}"] = np.ascontiguousarray(x_list[i][c], dtype=np.float32)
            m[f"p{i}"] = np.ascontiguousarray(p_list[i][c], dtype=np.float32)
        in_maps.append(m)
    res = run_bass_kernel_spmd(nc, in_maps, core_ids=list(range(n_cores)))
    global LAST_RES
    LAST_RES = res
    outs = []
    for i in range(len(seqs)):
        outs.append(np.stack([np.asarray(res.results[c][f"y{i}"], dtype=np.float32) for c in range(n_cores)], axis=0))
    return outs


def kernel(**inputs):
    xp = np.asarray(inputs["x_prompt"], np.float32)
    xs = np.asarray(inputs["x_sample"], np.float32)
    pp = np.asarray(inputs["p_prompt"], np.float32)[0]
    psm = np.asarray(inputs["p_sample"], np.float32)[0]
    outs = _run([xp, xs], [pp, psm], inputs, 8)
    return (outs[0], outs[1])
```

```python
import numpy as np
from contextlib import ExitStack
import concourse.bass as bass
import concourse.mybir as mybir
from concourse.bass_utils import run_bass_kernel_spmd

F32 = mybir.dt.float32
BF16 = mybir.dt.bfloat16
AF = mybir.ActivationFunctionType
ALU = mybir.AluOpType

D = 1024
PLE = 256
EPS = 1e-6
ST = 512
NSLOT = 22
(S_UAZA, S_POOL, S_MA, S_WA, S_Q, S_ZB, S_MB, S_WB, S_WO, S_WG, S_WPLE) = (0, 4, 5, 7, 9, 11, 13, 15, 17, 19, 21)
OFF_UA, OFF_ZA, OFF_Q, OFF_K, OFF_V, OFF_ZB, OFF_MA, OFF_MB = 0, 1024, 2048, 3072, 3328, 3584, 4608, 5632
POOL_W = (2, 4, 8, 16)


class _Res:
    __slots__ = ("w", "r")

    def __init__(self):
        self.w = None
        self.r = {}


class _Eng:
    def __init__(self, fw, name, h, is_pe=False):
        self.name = name
        self.h = h
        self.is_pe = is_pe
        self.sem = fw.new_sem("e_" + name)
        self.count = 0
        self.waited = {}


class FW:
    def __init__(self, nc, es):
        self.nc = nc
        self.es = es
        self.sems = {}
        self.pe = _Eng(self, "pe", nc.tensor, is_pe=True)
        self.act = _Eng(self, "act", nc.scalar)
        self.dve = _Eng(self, "dve", nc.vector)
        self.pool = _Eng(self, "pool", nc.gpsimd)
        self.sp = _Eng(self, "sp", nc.sync)
        self.engs = [self.pe, self.act, self.dve, self.pool, self.sp]
        self.res = {}
        self.dma_cnt = {}
        self.phase_toks = {}

    def new_sem(self, name):
        self.sems[name] = self.es.enter_context(self.nc.semaphore(name))
        return name

    def R(self, key):
        r = self.res.get(key)
        if r is None:
            r = _Res()
            self.res[key] = r
        return r

    def _wait(self, eng, tok):
        sk, val = tok
        if eng.waited.get(sk, 0) >= val:
            return
        eng.h.wait_ge(self.sems[sk], val)
        eng.waited[sk] = val

    def _deps(self, reads, writes):
        deps = []
        for k in reads:
            r = self.R(k)
            if r.w is not None:
                deps.append(r.w)
        for k in writes:
            r = self.R(k)
            if r.w is not None:
                deps.append(r.w)
            deps.extend(r.r.items())
        return deps

    def _record(self, tok, reads, writes):
        for k in reads:
            r = self.R(k)
            if r.r.get(tok[0], 0) < tok[1]:
                r.r[tok[0]] = tok[1]
        for k in writes:
            r = self.R(k)
            r.w = tok
            r.r = {}

    def op(self, eng, fns, reads=(), writes=()):
        if callable(fns):
            fns = [fns]
        writes = list(writes) + [k for k in reads if k.startswith("ps")]
        reads = [k for k in reads if not k.startswith("ps")]
        for tok in self._deps(reads, writes):
            if eng.is_pe and tok[0] == eng.sem:
                continue
            self._wait(eng, tok)
        inst = None
        for f in fns:
            inst = f()
        eng.count += 1
        inst.then_inc(self.sems[eng.sem], 1)
        tok = (eng.sem, eng.count)
        self._record(tok, reads, writes)
        return tok

    def dma(self, eng, out, in_, sem, reads=(), writes=(), phase_local=False, **kw):
        if sem not in self.sems:
            self.new_sem(sem)
            self.dma_cnt[sem] = 0
        if phase_local:
            for tok in self.phase_toks.items():
                self._wait(eng, tok)
        for tok in self._deps(reads, writes):
            self._wait(eng, tok)
        self.dma_cnt[sem] += 16
        eng.h.dma_start(out=out, in_=in_, **kw).then_inc(self.sems[sem], 16)
        tok = (sem, self.dma_cnt[sem])
        self._record(tok, reads, writes)
        return tok

    def barrier(self, engs=None, sp=False):
        last = {}
        for r in self.res.values():
            toks = list(r.r.items())
            if r.w is not None:
                toks.append(r.w)
            for sk, v in toks:
                if last.get(sk, 0) < v:
                    last[sk] = v
        if engs is None:
            engs = [self.pe, self.act, self.dve, self.pool] + ([self.sp] if sp else [])
        for e in engs:
            for sk, v in last.items():
                self._wait(e, (sk, v))
        self.phase_toks = dict(last)


def _fm_block(W, cols):
    return np.ascontiguousarray(W[:, cols].reshape(8, 128, 128).transpose(1, 0, 2))


def _fm_slot(W, col_lists):
    return np.stack([_fm_block(W, c) for c in col_lists], axis=1).reshape(128, 4096)


def _tm_slot(W, hf):
    return np.ascontiguousarray(W[:, hf * 512:(hf + 1) * 512].reshape(8, 128, 512).transpose(1, 0, 2)).reshape(128, 4096)


def _head_cols(base, j, i):
    hA, hB = 8 * j + i, 8 * j + 4 + i
    return np.concatenate([base + hA * 64 + np.arange(64), base + hB * 64 + np.arange(64)])


def pack_weights(inp):
    w_in = np.asarray(inp["w_in"][0], np.float32)
    slots = np.zeros((NSLOT, 128, 4096), np.float32)
    ar = np.arange(128)
    for g in range(4):
        slots[S_UAZA + g] = _fm_slot(w_in, [OFF_UA + (2 * g) * 128 + ar, OFF_UA + (2 * g + 1) * 128 + ar,
                                            OFF_ZA + (2 * g) * 128 + ar, OFF_ZA + (2 * g + 1) * 128 + ar])
    pw = np.asarray(inp["pool_w"][0], np.float32).reshape(4, 2, 128, 2, 128)
    slots[S_POOL, :, :2048] = pw.transpose(2, 0, 3, 1, 4).reshape(128, 2048)
    w_a = np.asarray(inp["w_branch_a"][0], np.float32)
    w_b = np.asarray(inp["w_branch_b"][0], np.float32)
    rperm = np.concatenate([_head_cols(0, j, i) for j in range(2) for i in range(4)])
    w_bp = w_b[rperm, :]
    w_o = np.asarray(inp["w_out"][0], np.float32)
    w_g = np.asarray(inp["w_ple_gate"][0], np.float32)
    for h in range(2):
        slots[S_MA + h] = _fm_slot(w_in, [OFF_MA + (4 * h + o) * 128 + ar for o in range(4)])
        slots[S_MB + h] = _fm_slot(w_in, [OFF_MB + (4 * h + o) * 128 + ar for o in range(4)])
        slots[S_WA + h] = _fm_slot(w_a, [(4 * h + o) * 128 + ar for o in range(4)])
        slots[S_WB + h] = _fm_slot(w_bp, [(4 * h + o) * 128 + ar for o in range(4)])
        slots[S_Q + h] = _fm_slot(w_in, [_head_cols(OFF_Q, h, i) for i in range(4)])
        slots[S_ZB + h] = _fm_slot(w_in, [_head_cols(OFF_ZB, h, i) for i in range(4)])
        slots[S_WO + h] = _tm_slot(w_o, h)
        slots[S_WG + h] = _tm_slot(w_g, h)
    wp = np.asarray(inp["w_ple_in"][0], np.float32)
    slots[S_WPLE, :, :2048] = wp.reshape(2, 128, 1024).transpose(1, 0, 2).reshape(128, 2048)
    wkv = np.zeros((128, 4096), np.float32)
    wkv[:, 0:2048] = np.stack([_fm_block(w_in, OFF_K + j * 128 + ar) for j in range(2)], axis=1).reshape(128, 2048)
    wkv[:, 2048:4096] = w_in[:, OFF_V:OFF_V + 256].reshape(8, 128, 256).transpose(1, 0, 2).reshape(128, 2048)
    gains = np.zeros((128, 32), np.float32)
    gains[:, 0:8] = np.asarray(inp["norm_pre"][0]).reshape(8, 128).T
    gains[:, 8:16] = np.asarray(inp["ple_norm"][0]).reshape(8, 128).T
    gains[:, 16:24] = np.asarray(inp["pool_scale"][0]).reshape(8, 128).T
    gains[:, 24] = np.tile(np.asarray(inp["q_norm"][0]), 2)
    gains[:, 25] = np.tile(np.asarray(inp["k_norm"][0]), 2)
    return slots, wkv, gains, np.ascontiguousarray(np.asarray(inp["norm_post"][0], np.float32))


def make_consts(tmax):
    cm = np.zeros((128, 4, 128), np.float32)
    cm[:, 0, :] = np.eye(128)
    for h in range(2):
        cm[h * 64:(h + 1) * 64, 1, h * 64:(h + 1) * 64] = 1.0
    Rm = np.zeros((64, 64), np.float32)
    for a in range(2):
        for i in range(16):
            Rm[a * 32 + i, a * 32 + i + 16] = -1.0
            Rm[a * 32 + i + 16, a * 32 + i] = 1.0
    R2 = np.zeros((128, 128), np.float32)
    R2[0:64, 0:64] = Rm
    R2[64:128, 64:128] = Rm
    cm[:, 2, :] = R2.T
    cm[:, 3, :] = 1.0
    t = np.arange(tmax)
    f = np.arange(128) % 64
    idx = (f % 32) % 16
    freq = (10000.0 ** (-(2.0 * idx) / 32.0)).astype(np.float32)
    pos = np.where((f < 32)[:, None], (t // 64)[None, :], (t % 64)[None, :]).astype(np.float32)
    ang = pos * freq[:, None]
    cs = np.stack([np.cos(ang), np.sin(ang)], axis=1).astype(np.float32)
    edge = np.zeros((128, 2, 4, 8), np.float32)
    for g, w in enumerate(POOL_W):
        for i in range(8):
            edge[:, 0, g, i] = 1.0 / (min(i + w // 2, 1 << 30) - max(i - w // 2, 0))
            edge[:, 1, g, i] = 1.0 / (min(w // 2, 8 - i) + w // 2)
    return cm.reshape(128, 512), np.ascontiguousarray(cs), edge.reshape(128, 64)


class _Stop(Exception):
    pass


def build_program(seqs, tmax, stop=0):
    nc = bass.Bass("TRN2", target_bir_lowering=False)
    nseq = len(seqs)
    x_d = [nc.dram_tensor(f"x{i}", [T, D], F32, kind="ExternalInput").ap() for i, T in enumerate(seqs)]
    p_d = [nc.dram_tensor(f"p{i}", [T, PLE], F32, kind="ExternalInput").ap() for i, T in enumerate(seqs)]
    y_d = [nc.dram_tensor(f"y{i}", [T, D], F32, kind="ExternalOutput").ap() for i, T in enumerate(seqs)]
    wsl_d = nc.dram_tensor("wslots", [NSLOT, 128, 4096], F32, kind="ExternalInput").ap()
    wkv_d = nc.dram_tensor("wkv", [128, 4096], F32, kind="ExternalInput").ap()
    gains_d = nc.dram_tensor("gains", [128, 32], F32, kind="ExternalInput").ap()
    gpost_d = nc.dram_tensor("gpost", [D], F32, kind="ExternalInput").ap()
    cmat_d = nc.dram_tensor("cmat", [128, 512], F32, kind="ExternalInput").ap()
    cs_d = nc.dram_tensor("cs", [128, 2, tmax], F32, kind="ExternalInput").ap()
    edge_d = nc.dram_tensor("edge", [128, 64], F32, kind="ExternalInput").ap()
    wbf_d = nc.dram_tensor("wbf", [NSLOT, 128, 4096], BF16, kind="Internal").ap()
    wkvbf_d = nc.dram_tensor("wkvbf", [128, 4096], BF16, kind="Internal").ap()
    TM = max(seqs)
    NKM = TM // 128

    with ExitStack() as es:
        fw = FW(nc, es)
        pe, act, dve, pool, sp = fw.pe, fw.act, fw.dve, fw.pool, fw.sp
        V, S, G = nc.vector, nc.scalar, nc.gpsimd

        uid = [0]

        def sbuf(stack, name, shape, dt):
            uid[0] += 1
            return stack.enter_context(nc.sbuf_tensor(f"s_{name}_{uid[0]}", shape, dt))

        KT = sbuf(es, "KT", [128, 2, TM], BF16)
        Vt = sbuf(es, "Vt", [128, NKM, 256], BF16)
        xt = sbuf(es, "xt", [128, 4, D], F32)
        hT = sbuf(es, "hT", [128, 8, ST], BF16)
        hTh = sbuf(es, "hTh", [128, 8, 16], BF16)
        QT = sbuf(es, "QT", [128, 8, ST], BF16)
        szb = sbuf(es, "szb", [128, 8, ST], BF16)
        pTt = sbuf(es, "pTt", [128, 2, ST], BF16)
        tA = sbuf(es, "tA", [128, 8, ST], F32)
        cs_t = sbuf(es, "cs_t", [128, 2, ST], F32)
        ring = [sbuf(es, f"ring{i}", [128, 4096], BF16) for i in range(3)]
        cmat = sbuf(es, "cmat", [128, 4, 128], BF16)
        gains = sbuf(es, "gains", [128, 32], F32)
        gpost = sbuf(es, "gpost", [128, D], F32)
        edge = sbuf(es, "edge", [128, 2, 4, 8], F32)
        epsc = sbuf(es, "epsc", [128, 2], F32)
        stat = sbuf(es, "stat", [128, 64], F32)
        ps = es.enter_context(nc.psum_tensor("ps", [128, 8, 512], F32))
        ident = cmat[:, 0, :]
        onesblk = cmat[:, 1, :]
        rmatT = cmat[:, 2, :]
        ones = cmat[:, 3, :]
        pst = ps[:, 7, :].bitcast(BF16)
        pstb = {6: ps[:, 6, :].bitcast(BF16), 7: pst}

        def mm(out, lhsT, rhs, start=True, stop=True):
            return lambda: nc.tensor.matmul(out, lhsT=lhsT, rhs=rhs, start=start, stop=stop)

        def tr(out, in_):
            return lambda: nc.tensor.transpose(out=out, in_=in_, identity=ident if in_.shape[0] == 128 else cmat[0:in_.shape[0], 0, 0:in_.shape[0]])

        dbg = {}

        def dump(name, t, key_list, cond=True):
            if not (stop == -1 and cond) or name in dbg:
                return
            shp = list(t.shape)
            d_ = nc.dram_tensor("dbg_" + name, shp, t.dtype, kind="ExternalOutput").ap()
            dbg[name] = d_
            fw.barrier(sp=True)
            fw.dma(sp, d_, t[:], "d_dbg_" + name, reads=key_list)
            fw.barrier(sp=True)

        bank_rr = [0]

        def next_bank(pool_banks):
            b = pool_banks[bank_rr[0] % len(pool_banks)]
            bank_rr[0] += 1
            return b

        ring_rr = [0]

        def load_slot(slot, half=False):
            r = ring_rr[0] % 3
            ring_rr[0] += 1
            n = 2048 if half else 4096
            fw.dma(sp, ring[r][:, 0:n], wbf_d[slot, :, 0:n], f"d_ring{r}", reads=[f"wbf{slot}_{q}" for q in range(2 if half else 4)], writes=[f"ring{r}"])
            return r

        with ExitStack() as ph:
            stg = sbuf(ph, "c_stg", [128, 512], F32)
            fw.dma(sp, stg[:], cmat_d, "d_cstg", writes=["c_stg"])
            fw.dma(sp, gains[:], gains_d, "d_gains", writes=["gains"])
            fw.dma(sp, gpost[:], gpost_d.partition_broadcast(128), "d_gpost", writes=["gpost"])
            fw.dma(sp, edge[:].rearrange("p a g i -> p (a g i)"), edge_d, "d_edge", writes=["edge"])
            fw.op(dve, lambda: V.tensor_copy(out=cmat[:].rearrange("p a c -> p (a c)"), in_=stg[:]), reads=["c_stg"], writes=["cmat"])
            fw.op(dve, lambda: V.memset(epsc[:, 0:1], EPS), writes=["epsc"])
            fw.op(dve, lambda: V.memset(epsc[:, 1:2], 64.0 * EPS), writes=["epsc"])
            fw.barrier()
        if stop == 1:
            return nc

        GPRE, GPLE = 0, 8
        gain_slots = {}
        for g in range(4):
            gain_slots[S_UAZA + g] = ("fm", GPRE)
        for h_ in range(2):
            for s0 in (S_MA, S_Q, S_ZB, S_MB):
                gain_slots[s0 + h_] = ("fm", GPRE)
            gain_slots[S_WG + h_] = ("tm", GPLE)
        prep_state = {"u": 0, "sf": None, "sbb": None}

        def prep_unit(src_ap, dst_ap, dst_key, layout, gcol):
            u = prep_state["u"]
            prep_state["u"] += 1
            b = u % 2
            sf, sbb = prep_state["sf"][b], prep_state["sbb"][b]
            fw.dma(sp, sf[:], src_ap, f"d_wsf{b}", writes=[sf.name], phase_local=True)
            if layout is None:
                fw.op(act, lambda: S.copy(out=sbb[:], in_=sf[:]), reads=[sf.name], writes=[sbb.name])
            else:
                nk, w_, g0 = layout
                fw.op(dve, lambda: V.tensor_tensor(out=sbb[:].rearrange("p (k c) -> p k c", k=nk), in0=sf[:].rearrange("p (k c) -> p k c", k=nk),
                                                   in1=gains[:, gcol + g0:gcol + g0 + nk].unsqueeze(2).broadcast_to([128, nk, w_]), op=ALU.mult),
                      reads=[sf.name, "gains"], writes=[sbb.name])
            fw.dma(pool, dst_ap, sbb[:], f"d_wsb{b}", reads=[sbb.name], writes=[dst_key])

        prep_units = []
        for q4 in range(4):
            lay = (8, 128, 0) if q4 < 2 else (4, 256, 4 * (q4 - 2))
            prep_units.append(lambda q4=q4, lay=lay: prep_unit(wkv_d[:, q4 * 1024:(q4 + 1) * 1024], wkvbf_d[:, q4 * 1024:(q4 + 1) * 1024], f"wkvbf{q4}", lay, GPRE))
        N_WKV_UNITS = 4
        for s_ in range(NSLOT):
            lay0, gc = gain_slots.get(s_, (None, 0))
            nq = 2 if s_ in (S_POOL, S_WPLE) else 4
            for q4 in range(nq):
                if lay0 == "fm":
                    lay = (8, 128, 0)
                elif lay0 == "tm":
                    lay = (2, 512, 2 * q4)
                else:
                    lay = None
                prep_units.append(lambda s_=s_, q4=q4, lay=lay, gc=gc: prep_unit(wsl_d[s_, :, q4 * 1024:(q4 + 1) * 1024], wbf_d[s_, :, q4 * 1024:(q4 + 1) * 1024],
                                                                               f"wbf{s_}_{q4}", lay, gc))

        def rstd_from(ss_ap, out_ap, scale, eps_col, tmp_ap, rkeys, wkey):
            n_ = ss_ap.shape[0]
            fw.op(act, lambda: S.activation(out=tmp_ap, in_=ss_ap, func=AF.Ln, bias=epsc[0:n_, eps_col:eps_col + 1], scale=scale),
                  reads=rkeys + ["epsc"], writes=[wkey + "_t"])
            fw.op(act, lambda: S.activation(out=out_ap, in_=tmp_ap, func=AF.Exp, scale=-0.5), reads=[wkey + "_t"], writes=[wkey])

        def norm_transpose(x_ap, nrows, junk, hn, dst_fn, key):
            fw.op(act, lambda: S.activation(out=junk[0:nrows, :], in_=x_ap, func=AF.Square, accum_out=stat[0:nrows, 12:13]),
                  reads=[key], writes=["stat12", "junk"])
            rstd_from(stat[0:nrows, 12:13], stat[0:nrows, 14:15], 1.0 / D, 0, stat[0:nrows, 13:14], ["stat12"], "stat14")
            fw.op(dve, lambda: V.tensor_scalar(out=hn[0:nrows, :], in0=x_ap, scalar1=stat[0:nrows, 14:15], scalar2=None, op0=ALU.mult),
                  reads=[key, "stat14"], writes=[hn.name])
            fw.op(pe, [tr(pst[:, kc * 128:kc * 128 + nrows], hn[0:nrows, kc * 128:(kc + 1) * 128]) for kc in range(8)],
                  reads=[hn.name, "cmat"], writes=["ps7"])
            dst_fn()

        def qk_stage1(bank, gcol, tmp, c):
            sq, qg = tmp["sq"], tmp["qg"]
            if stop != 322:
                fw.op(act, lambda: S.activation(out=sq[:], in_=ps[:, bank, :], func=AF.Square), reads=[f"ps{bank}"], writes=[sq.name])
            if stop == 321:
                return
            fw.op(act, lambda: S.activation(out=qg[:], in_=ps[:, bank, :], func=AF.Copy, scale=gains[:, gcol:gcol + 1]),
                  reads=[f"ps{bank}", "gains"], writes=[qg.name])

        def qk_stage2(tmp, dst_ap, dst_key, bss, brq):
            sq, qg, t1, t2, rs = tmp["sq"], tmp["qg"], tmp["t1"], tmp["t2"], tmp["rs"]
            fw.op(pe, [mm(ps[:, bss, :], onesblk, sq[:])], reads=[sq.name, "cmat"], writes=[f"ps{bss}"])
            fw.op(pe, [mm(ps[:, brq, :], rmatT, qg[:])], reads=[qg.name, "cmat"], writes=[f"ps{brq}"])
            fw.op(pool, lambda: G.tensor_tensor(out=t1[:], in0=qg[:], in1=cs_t[:, 0, :], op=ALU.mult), reads=[qg.name, "cs_t"], writes=[t1.name])
            fw.op(dve, lambda: V.tensor_tensor(out=t2[:], in0=ps[:, brq, :], in1=cs_t[:, 1, :], op=ALU.mult), reads=[f"ps{brq}", "cs_t"], writes=[t2.name])
            fw.op(act, lambda: S.activation(out=rs[:], in_=ps[:, bss, :], func=AF.Ln, bias=epsc[:, 1:2], scale=1.0),
                  reads=[f"ps{bss}", "epsc"], writes=[rs.name])
            fw.op(dve, lambda: V.tensor_tensor(out=t1[:], in0=t1[:], in1=t2[:], op=ALU.add), reads=[t1.name, t2.name], writes=[t1.name])
            fw.op(act, lambda: S.activation(out=rs[:], in_=rs[:], func=AF.Exp, scale=-0.5), reads=[rs.name], writes=[rs.name])
            fw.op(dve, lambda: V.tensor_tensor(out=dst_ap, in0=t1[:], in1=rs[:], op=ALU.mult), reads=[t1.name, rs.name], writes=[dst_key])

        def alloc_qk_tmp(ph, tag):
            return dict(sq=sbuf(ph, f"sq{tag}", [128, ST], BF16), qg=sbuf(ph, f"qg{tag}", [128, ST], BF16),
                        t1=sbuf(ph, f"t1{tag}", [128, ST], F32), t2=sbuf(ph, f"t2{tag}", [128, ST], F32),
                        rs=sbuf(ph, f"rs{tag}", [128, ST], F32))

        LIN_BANKS = [0, 1, 2, 3]

        def load_x_tile(xd, t0):
            fw.dma(sp, xt[:], xd[t0:t0 + ST, :].rearrange("(s p) d -> p s d", p=128), "d_xt", writes=["xt"])

        def mh_stats(junk):
            for s in range(4):
                fw.op(act, lambda s=s: S.activation(out=junk[:], in_=xt[:, s, :], func=AF.Square, accum_out=stat[:, s:s + 1]),
                      reads=["xt"], writes=[f"st_ss{s}", "junk"])
            fw.op(act, lambda: S.activation(out=stat[:, 4:8], in_=stat[:, 0:4], func=AF.Ln, bias=epsc[:, 0:1], scale=1.0 / D),
                  reads=[f"st_ss{s}" for s in range(4)] + ["epsc"], writes=["st_ln"])
            fw.op(act, lambda: S.activation(out=stat[:, 8:12], in_=stat[:, 4:8], func=AF.Exp, scale=-0.5), reads=["st_ln"], writes=["st_rs"])

        def mh_scale(s, hn):
            fw.op(dve, lambda: V.tensor_scalar(out=hn[:], in0=xt[:, s, :], scalar1=stat[:, 8 + s:9 + s], scalar2=None, op0=ALU.mult),
                  reads=["xt", "st_rs"], writes=[hn.name])

        def mh_tr_copy(s, hn, tb=None, dst=None, dkey="hT"):
            if tb is None:
                tb = 6 + (s % 2)
            if dst is None:
                dst = hT
            pv_ = ps[:, tb, :].bitcast(BF16)
            fw.op(pe, [tr(pv_[:, kc * 128:(kc + 1) * 128], hn[:, kc * 128:(kc + 1) * 128]) for kc in range(8)],
                  reads=[hn.name, "cmat"], writes=[f"ps{tb}"])
            fw.op(act, lambda: S.copy(out=dst[:, :, s * 128:(s + 1) * 128], in_=pv_.rearrange("p (k c) -> p k c", k=8)),
                  reads=[f"ps{tb}"], writes=[dkey])

        def make_hT(junk, hn2, dst=None, dkey="hT"):
            mh_stats(junk)
            for s in range(4):
                mh_scale(s, hn2[s % 2])
                mh_tr_copy(s, hn2[s % 2], None, dst, dkey)

        for si, T in enumerate(seqs):
            NK = T // 128
            NST = T // ST
            xd, pd, yd = x_d[si], p_d[si], y_d[si]

            with ExitStack() as ph:
                junk = sbuf(ph, "junk", [128, D], BF16)
                hn2 = [sbuf(ph, f"hn{i}", [128, D], BF16) for i in range(2)]
                tmps = [alloc_qk_tmp(ph, f"a{i}") for i in range(2)]
                wkv = sbuf(ph, "wkv", [128, 4096], BF16)
                if si == 0:
                    prep_state["sf"] = [sbuf(ph, f"w_sf{i}", [128, 1024], F32) for i in range(2)]
                    prep_state["sbb"] = [sbuf(ph, f"w_sb{i}", [128, 1024], BF16) for i in range(2)]
                    for pu in prep_units[:N_WKV_UNITS]:
                        pu()
                    rest = prep_units[N_WKV_UNITS:]
                    per_it = -(-len(rest) // NST)
                fw.dma(sp, wkv[:], wkvbf_d, "d_wkv", reads=[f"wkvbf{q}" for q in range(4)], writes=["wkv"], phase_local=True)
                wk = wkv[:, 0:2048].rearrange("p (j k c) -> p j k c", j=2, k=8)
                wv = wkv[:, 2048:4096].rearrange("p (k c) -> p k c", k=8)
                hT2 = sbuf(ph, "hT2", [128, 8, ST], BF16)
                hbufs = [(hT, "hT"), (hT2, "hT2")]
                load_x_tile(xd, 0)
                make_hT(junk, hn2, hT, "hT")
                if NST > 1:
                    load_x_tile(xd, ST)
                for it in range(NST):
                    t0 = it * ST
                    hcur, hkey = hbufs[it % 2]
                    fw.dma(sp, cs_t[:], cs_d[:, :, t0:t0 + ST], "d_cs", writes=["cs_t"])
                    for j in range(2):
                        b = next_bank(LIN_BANKS)
                        fw.op(pe, [mm(ps[:, b, :], wk[:, j, kc, :], hcur[:, kc, :], kc == 0, kc == 7) for kc in range(8)],
                              reads=[hkey, "wkv"], writes=[f"ps{b}"])
                        qk_stage1(b, 25, tmps[j], j)
                    for s in range(4):
                        b = next_bank(LIN_BANKS)
                        fw.op(pe, [mm(ps[:, b, 0:256], hcur[:, kc, s * 128:(s + 1) * 128], wv[:, kc, :], kc == 0, kc == 7) for kc in range(8)],
                              reads=[hkey, "wkv"], writes=[f"ps{b}"])
                        kt = it * 4 + s
                        fw.op(dve, lambda b=b, kt=kt: V.tensor_copy(out=Vt[:, kt, :], in_=ps[:, b, 0:256]), reads=[f"ps{b}"], writes=["Vt"])
                    if it + 1 < NST:
                        hnx, hnk = hbufs[(it + 1) % 2]
                        make_hT(junk, hn2, hnx, hnk)
                        if it + 2 < NST:
                            load_x_tile(xd, t0 + 2 * ST)
                    for j in range(2):
                        qk_stage2(tmps[j], KT[:, j, t0:t0 + ST], "KT", 4 + j, 6)
                    if si == 0:
                        for pu in rest[it * per_it:(it + 1) * per_it]:
                            pu()
                fw.barrier()
                dump("KT", KT, ["KT"], si == 0)
                dump("Vt", Vt, ["Vt"], si == 0)
            if stop == 3:
                return nc

            load_x_tile(xd, 0)
            fw.dma(sp, cs_t[:], cs_d[:, :, 0:ST], "d_cs", writes=["cs_t"])
            for it in range(NST):
                t0 = it * ST
                first, last = (it == 0), (it == NST - 1)
                with ExitStack() as ph:
                    junk = sbuf(ph, "junk", [128, D], BF16)
                    hn2 = [sbuf(ph, f"hn{i}", [128, D], BF16) for i in range(2)]
                    hn = hn2[0]
                    xh = sbuf(ph, "xh", [16, D], F32)
                    poolw_t = sbuf(ph, "poolw", [128, 2048], BF16)
                    pa = ExitStack()
                    uext = [sbuf(pa, f"uext{i}", [128, 2, ST + 16], F32) for i in range(2)]
                    pwa = sbuf(pa, "pwa", [128, 2, ST + 16], F32)
                    pwb = sbuf(pa, "pwb", [128, 2, ST + 16], F32)
                    pooled = [sbuf(pa, f"pooled{i}", [128, 2, ST], BF16) for i in range(2)]
                    sza = [sbuf(pa, f"sza{i}", [128, 2, ST], BF16) for i in range(2)]
                    sma = [sbuf(pa, f"sma{i}", [128, ST], BF16) for i in range(4)]
                    etmp = sbuf(pa, "etmp", [128, 2, 8], F32)
                    aT = szb

                    fw.op(dve, lambda: V.memset(xh[:], 0.0), writes=["xh"])
                    if not first:
                        fw.dma(sp, xh[0:8, :], xd[t0 - 8:t0, :], "d_xh", writes=["xh"], phase_local=True)
                    if not last:
                        fw.dma(sp, xh[8:16, :], xd[t0 + ST:t0 + ST + 8, :], "d_xh", writes=["xh"], phase_local=True)
                    if it == 0:
                        make_hT(junk, hn2)

                    def dsth():
                        fw.op(act, lambda: S.copy(out=hTh[:], in_=pst.rearrange("p (k c) -> p k c", k=8)[:, :, 0:16]), reads=["ps7"], writes=["hTh"])
                    norm_transpose(xh[:], 16, junk, hn, dsth, "xh")
                    dump("hT", hT, ["hT"], si == 0 and it == 0)
                    dump("hTh", hTh, ["hTh"], si == 0 and it == 0)

                    fw.dma(sp, poolw_t[:], wbf_d[S_POOL, :, 0:2048], "d_poolw", reads=[f"wbf{S_POOL}_0", f"wbf{S_POOL}_1"], writes=["poolw"], phase_local=True)
                    poolw = poolw_t[:].rearrange("p (g o k c) -> p g o k c", g=4, o=2, k=2)

                    def proj_a(g):
                        r = load_slot(S_UAZA + g)
                        wsl = ring[r][:].rearrange("p (o k c) -> p o k c", o=4, k=8)
                        ue, sz = uext[g % 2], sza[g % 2]
                        for b2 in range(2):
                            bk = next_bank(LIN_BANKS)
                            fw.op(pe, [mm(ps[:, bk, :], wsl[:, b2, kc, :], hT[:, kc, :], kc == 0, kc == 7) for kc in range(8)],
                                  reads=["hT", f"ring{r}"], writes=[f"ps{bk}"])
                            fw.op(act, lambda: S.copy(out=ue[:, b2, 8:8 + ST], in_=ps[:, bk, :]), reads=[f"ps{bk}"], writes=[ue.name])
                            bh = next_bank([4, 5])
                            fw.op(pe, [mm(ps[:, bh, 0:16], wsl[:, b2, kc, :], hTh[:, kc, :], kc == 0, kc == 7) for kc in range(8)],
                                  reads=["hTh", f"ring{r}"], writes=[f"ps{bh}"])
                            fw.op(dve, lambda: V.tensor_copy(out=ue[:, b2, 0:8], in_=ps[:, bh, 0:8]), reads=[f"ps{bh}"], writes=[ue.name])
                            fw.op(dve, lambda: V.tensor_copy(out=ue[:, b2, 8 + ST:16 + ST], in_=ps[:, bh, 8:16]), reads=[f"ps{bh}"], writes=[ue.name])
                        for b2 in range(2):
                            bk = next_bank(LIN_BANKS)
                            fw.op(pe, [mm(ps[:, bk, :], wsl[:, 2 + b2, kc, :], hT[:, kc, :], kc == 0, kc == 7) for kc in range(8)],
                                  reads=["hT", f"ring{r}"], writes=[f"ps{bk}"])
                            fw.op(act, lambda: S.activation(out=sz[:, b2, :], in_=ps[:, bk, :], func=AF.Silu), reads=[f"ps{bk}"], writes=[sz.name])

                    def pool_mix(g):
                        w = POOL_W[g]
                        ue, pl, sz = uext[g % 2], pooled[g % 2], sza[g % 2]
                        L = ST + 16
                        cur, step, k_ = ue, 1, 0
                        while step < w:
                            nxt = pwa if k_ % 2 == 0 else pwb
                            e, h = (pool, G) if k_ % 2 == 0 else (dve, V)
                            fw.op(e, lambda: h.tensor_tensor(out=nxt[:, :, 0:L - step], in0=cur[:, :, 0:L - step], in1=cur[:, :, step:L], op=ALU.add),
                                  reads=[cur.name], writes=[nxt.name])
                            L -= step
                            cur = nxt
                            step *= 2
                            k_ += 1
                        off = 8 - w // 2
                        fw.op(dve, lambda: V.scalar_tensor_tensor(out=pl[:], in0=cur[:, :, off:off + ST], scalar=1.0 / w, in1=ue[:, :, 8:8 + ST],
                                                                 op0=ALU.mult, op1=ALU.subtract),
                              reads=[cur.name, ue.name], writes=[pl.name])
                        for (is_edge, a_, j0) in ((first, 0, 0), (last, 1, ST - 8)):
                            if not is_edge:
                                continue
                            fw.op(dve, lambda: V.tensor_tensor(out=etmp[:], in0=cur[:, :, off + j0:off + j0 + 8],
                                                               in1=edge[:, a_, g, :].unsqueeze(1).broadcast_to([128, 2, 8]), op=ALU.mult),
                                  reads=[cur.name, "edge"], writes=["etmp"])
                            fw.op(dve, lambda: V.tensor_tensor(out=pl[:, :, j0:j0 + 8], in0=etmp[:], in1=ue[:, :, 8 + j0:16 + j0], op=ALU.subtract),
                                  reads=["etmp", ue.name, pl.name], writes=[pl.name])
                        for o2 in range(2):
                            bk = next_bank(LIN_BANKS)
                            fw.op(pe, [mm(ps[:, bk, :], poolw[:, g, o2, k2, :], pl[:, k2, :], k2 == 0, k2 == 1) for k2 in range(2)],
                                  reads=[pl.name, "poolw"], writes=[f"ps{bk}"])
                            oc = 2 * g + o2
                            fw.op(dve, lambda: V.scalar_tensor_tensor(out=aT[:, oc, :], in0=ps[:, bk, :], scalar=gains[:, 16 + oc:17 + oc], in1=sz[:, o2, :],
                                                                     op0=ALU.mult, op1=ALU.mult),
                                  reads=[f"ps{bk}", "gains", sz.name], writes=["szb"])

                    def ma_proj(w1, r1, o, oc):
                        sm = sma[oc % 4]
                        bk = next_bank(LIN_BANKS)
                        fw.op(pe, [mm(ps[:, bk, :], w1[:, o, kc, :], hT[:, kc, :], kc == 0, kc == 7) for kc in range(8)],
                              reads=["hT", f"ring{r1}"], writes=[f"ps{bk}"])
                        fw.op(act, lambda: S.activation(out=sm[:], in_=ps[:, bk, :], func=AF.Sigmoid), reads=[f"ps{bk}"], writes=[sm.name])

                    def wa_proj(w2, r2, o, oc):
                        sm = sma[oc % 4]
                        bk2 = next_bank(LIN_BANKS)
                        fw.op(pe, [mm(ps[:, bk2, :], w2[:, o, kc, :], aT[:, kc, :], kc == 0, kc == 7) for kc in range(8)],
                              reads=["szb", f"ring{r2}"], writes=[f"ps{bk2}"])
                        fw.op(dve, lambda: V.tensor_tensor(out=tA[:, oc, :], in0=ps[:, bk2, :], in1=sm[:], op=ALU.mult),
                              reads=[f"ps{bk2}", sm.name], writes=["tA"])

                    proj_a(0)
                    for g in range(3):
                        proj_a(g + 1)
                        pool_mix(g)
                    r1 = load_slot(S_MA + 0)
                    w1 = ring[r1][:].rearrange("p (o k c) -> p o k c", o=4, k=8)
                    for o in range(4):
                        ma_proj(w1, r1, o, o)
                    pool_mix(3)
                    r2 = load_slot(S_WA + 0)
                    w2 = ring[r2][:].rearrange("p (o k c) -> p o k c", o=4, k=8)
                    for o in range(4):
                        wa_proj(w2, r2, o, o)
                    r1 = load_slot(S_MA + 1)
                    r2 = load_slot(S_WA + 1)
                    w1 = ring[r1][:].rearrange("p (o k c) -> p o k c", o=4, k=8)
                    w2 = ring[r2][:].rearrange("p (o k c) -> p o k c", o=4, k=8)
                    for o in range(4):
                        ma_proj(w1, r1, o, 4 + o)
                        wa_proj(w2, r2, o, 4 + o)
                    dump("tA", tA, ["tA"], si == 0 and it == 0)
                    dump("aT", szb, ["szb"], si == 0 and it == 0)
                    fw.barrier()
                    pa.close()
                    tmps = [alloc_qk_tmp(ph, f"b{i}") for i in range(2)]
                    pend = None
                    cidx = 0
                    for j in range(2):
                        r = load_slot(S_Q + j)
                        wsl = ring[r][:].rearrange("p (o k c) -> p o k c", o=4, k=8)
                        for i in range(4):
                            c = 4 * j + i
                            bk = next_bank(LIN_BANKS)
                            fw.op(pe, [mm(ps[:, bk, :], wsl[:, i, kc, :], hT[:, kc, :], kc == 0, kc == 7) for kc in range(8)],
                                  reads=["hT", f"ring{r}"], writes=[f"ps{bk}"])
                            tm_ = tmps[cidx % 2]
                            if pend is not None:
                                qk_stage2(*pend)
                            qk_stage1(bk, 24, tm_, c)
                            pend = (tm_, QT[:, c, :], f"QT{c}", 4 + (cidx % 2), 6)
                            cidx += 1
                    qk_stage2(*pend)
                    for (s0, dstt, func) in ((S_ZB, szb, AF.Silu),):
                        for j in range(2):
                            r = load_slot(s0 + j)
                            wsl = ring[r][:].rearrange("p (o k c) -> p o k c", o=4, k=8)
                            for i in range(4):
                                c = 4 * j + i
                                bk = next_bank(LIN_BANKS)
                                fw.op(pe, [mm(ps[:, bk, :], wsl[:, i, kc, :], hT[:, kc, :], kc == 0, kc == 7) for kc in range(8)],
                                      reads=["hT", f"ring{r}"], writes=[f"ps{bk}"])
                                fw.op(act, lambda bk=bk, c=c, dstt=dstt, func=func: S.activation(out=dstt[:, c, :], in_=ps[:, bk, :], func=func),
                                      reads=[f"ps{bk}"], writes=["szb"])
                    fw.barrier()
                    dump("QT", QT, [], si == 0 and it == 0)
                    dump("szb", szb, [], si == 0 and it == 0)
                if stop == 4:
                    return nc

                if it + 1 < NST:
                    load_x_tile(xd, t0 + ST)
                    fw.dma(sp, cs_t[:], cs_d[:, :, t0 + ST:t0 + 2 * ST], "d_cs", writes=["cs_t"])
                with ExitStack() as ph:
                    bT = sbuf(ph, "bT", [128, 8, ST], BF16)
                    rinv = sbuf(ph, "rinv", [128, ST], F32)
                    wgt = sbuf(ph, "wgt", [128, ST], F32)
                    mrg = QT
                    smb = szb
                    tB = sbuf(ph, "tB", [128, ST], F32)
                    steps = [(c, kt) for c in range(8) for kt in range(NK)]
                    ptile4 = [sbuf(ph, f"ptile{i}", [128, PLE], F32) for i in range(4)]
                    ptb4 = [sbuf(ph, f"ptb{i}", [128, PLE], BF16) for i in range(4)]
                    for s4 in range(4):
                        fw.dma(sp, ptile4[s4][:], pd[t0 + s4 * 128:t0 + (s4 + 1) * 128, :], f"d_ptile{s4}", writes=[ptile4[s4].name], phase_local=True)
                        fw.op(pool, lambda s4=s4: G.tensor_copy(out=ptb4[s4][:], in_=ptile4[s4][:]), reads=[ptile4[s4].name], writes=[ptb4[s4].name])
                    pa2 = ExitStack()
                    NPT = 6
                    PT = [sbuf(pa2, f"PT{i}", [128, 2, ST], BF16) for i in range(NPT)]
                    PS2 = [sbuf(pa2, f"PS2_{i}", [128, 2, ST], BF16) for i in range(2)]
                    PS4 = [sbuf(pa2, f"PS4_{i}", [128, 2, ST], BF16) for i in range(2)]
                    RSG = 4 if NK % 4 == 0 else (2 if NK % 2 == 0 else 1)

                    def emit_qk(n):
                        c, kt = steps[n]
                        j = c // 4
                        sb_ = 2 * (n % 2)
                        fw.op(pe, [mm(ps[:, sb_, :], KT[0:64, j, kt * 128:(kt + 1) * 128], QT[0:64, c, :]),
                                   mm(ps[:, sb_ + 1, :], KT[64:128, j, kt * 128:(kt + 1) * 128], QT[64:128, c, :])],
                              reads=["KT", f"QT{c}"], writes=[f"ps{sb_}", f"ps{sb_ + 1}"])

                    def emit_rs(c, kt, src):
                        rb = 5 + 2 * (c % 2)
                        st_, sp_ = (kt == RSG - 1), (kt == NK - 1)
                        fw.op(pe, [mm(ps[0:64, rb, :], ones[:, 0:64], src[:, 0, :], st_, sp_),
                                   mm(ps[64:128, rb, :], ones[:, 64:128], src[:, 1, :], st_, sp_)],
                              reads=[src.name, "cmat"], writes=[f"ps{rb}"])
                        if kt == NK - 1:
                            ob = rb - 1
                            fw.op(dve, lambda: V.reciprocal(out=rinv[:], in_=ps[:, rb, :]), reads=[f"ps{rb}"], writes=["rinv"])
                            fw.op(pool, lambda: G.tensor_tensor(out=wgt[:], in0=rinv[:], in1=szb[:, c, :], op=ALU.mult), reads=["rinv", "szb"], writes=["wgt"])

                            def fin():
                                fw.op(dve, lambda: V.tensor_tensor(out=bT[:, c, :], in0=ps[:, ob, :], in1=wgt[:], op=ALU.mult),
                                      reads=[f"ps{ob}", "wgt"], writes=["bT"])
                            deferred.append((cur_n[0] + 3, c, fin))

                    deferred = []
                    cur_n = [0]
                    npair = [0]
                    nquad = [0]
                    emit_qk(0)
                    emit_qk(1)
                    for n, (c, kt) in enumerate(steps):
                        j = c // 4
                        cur_n[0] = n
                        if kt == 0:
                            while any(d[1] <= c - 2 for d in deferred):
                                deferred.sort(key=lambda d: d[0])
                                i_ = next(i for i, d in enumerate(deferred) if d[1] <= c - 2)
                                deferred.pop(i_)[2]()
                        sb_ = 2 * (n % 2)
                        pt = PT[n % NPT]
                        fw.op(act, lambda sb_=sb_, pt=pt: S.activation(out=pt[:], in_=ps[:, sb_:sb_ + 2, :], func=AF.Exp, scale=8.0),
                              reads=[f"ps{sb_}", f"ps{sb_ + 1}"], writes=[pt.name])
                        if n + 2 < len(steps):
                            emit_qk(n + 2)
                        ob = 4 + 2 * (c % 2)
                        st_, sp_ = (kt == 0), (kt == NK - 1)
                        fw.op(pe, [mm(ps[0:64, ob, :], Vt[:, kt, 128 * j:128 * j + 64], pt[:, 0, :], st_, sp_),
                                   mm(ps[64:128, ob, :], Vt[:, kt, 128 * j + 64:128 * j + 128], pt[:, 1, :], st_, sp_)],
                              reads=[pt.name, "Vt"], writes=[f"ps{ob}"])
                        if RSG == 1:
                            deferred.append((n, c, lambda c=c, kt=kt, pt=pt: emit_rs(c, kt, pt)))
                        else:
                            if kt % 2 == 1:
                                p2 = PS2[npair[0] % 2]
                                npair[0] += 1
                                pprev = PT[(n - 1) % NPT]
                                fw.op(dve, lambda p2=p2, pprev=pprev, pt=pt: V.tensor_tensor(out=p2[:], in0=pprev[:], in1=pt[:], op=ALU.add),
                                      reads=[pprev.name, pt.name], writes=[p2.name])
                                if RSG == 2:
                                    deferred.append((n + 4, c, lambda c=c, kt=kt, p2=p2: emit_rs(c, kt, p2)))
                            if RSG == 4 and kt % 4 == 3:
                                p4 = PS4[nquad[0] % 2]
                                nquad[0] += 1
                                fw.op(dve, lambda p4=p4: V.tensor_tensor(out=p4[:], in0=PS2[0][:], in1=PS2[1][:], op=ALU.add),
                                      reads=[PS2[0].name, PS2[1].name], writes=[p4.name])
                                deferred.append((n + 4, c, lambda c=c, kt=kt, p4=p4: emit_rs(c, kt, p4)))
                        deferred.sort(key=lambda d: d[0])
                        while deferred and deferred[0][0] <= n:
                            deferred.pop(0)[2]()
                            deferred.sort(key=lambda d: d[0])
                    def mb_proj(wsl, r, i):
                        bk = next_bank(LIN_BANKS)
                        fw.op(pe, [mm(ps[:, bk, :], wsl[:, i, kc, :], hT[:, kc, :], kc == 0, kc == 7) for kc in range(8)],
                              reads=["hT", f"ring{r}"], writes=[f"ps{bk}"])
                        return bk

                    def mb_act(bk, c):
                        fw.op(act, lambda: S.activation(out=smb[:, c, :], in_=ps[:, bk, :], func=AF.Sigmoid), reads=[f"ps{bk}"], writes=["szb"])

                    r_mb0 = load_slot(S_MB + 0)
                    wsl0 = ring[r_mb0][:].rearrange("p (o k c) -> p o k c", o=4, k=8)
                    mb_banks = [mb_proj(wsl0, r_mb0, i) for i in range(4)]
                    cur_n[0] = len(steps) + 10
                    while deferred:
                        deferred.sort(key=lambda d: d[0])
                        deferred.pop(0)[2]()
                    fw.barrier()
                    pa2.close()
                    dump("bT", bT, ["bT"], si == 0 and it == 0)
                    if stop == 5:
                        fw.barrier()
                        return nc
                    for i in range(4):
                        mb_act(mb_banks[i], i)
                    for s4 in range(4):
                        pb = 0 + (s4 % 2)
                        pv_ = ps[:, pb, :].bitcast(BF16)
                        fw.op(pe, [tr(pv_[:, k2 * 128:(k2 + 1) * 128], ptb4[s4][:, k2 * 128:(k2 + 1) * 128]) for k2 in range(2)],
                              reads=[ptb4[s4].name, "cmat"], writes=[f"ps{pb}"])
                        fw.op(dve, lambda s4=s4, pv_=pv_: V.tensor_copy(out=pTt[:, :, s4 * 128:(s4 + 1) * 128], in_=pv_[:, 0:256].rearrange("p (k c) -> p k c", k=2)),
                              reads=[f"ps{pb}"], writes=["pTt"])
                    r = load_slot(S_MB + 1)
                    wsl = ring[r][:].rearrange("p (o k c) -> p o k c", o=4, k=8)
                    for i in range(4):
                        mb_act(mb_proj(wsl, r, i), 4 + i)
                    for h_ in range(2):
                        r = load_slot(S_WB + h_)
                        wsl = ring[r][:].rearrange("p (o k c) -> p o k c", o=4, k=8)
                        for o in range(4):
                            oc = 4 * h_ + o
                            bk = next_bank(LIN_BANKS)
                            fw.op(pe, [mm(ps[:, bk, :], wsl[:, o, kc, :], bT[:, kc, :], kc == 0, kc == 7) for kc in range(8)],
                                  reads=["bT", f"ring{r}"], writes=[f"ps{bk}"])
                            fw.op(dve, lambda bk=bk, oc=oc: V.tensor_tensor(out=tB[:], in0=ps[:, bk, :], in1=smb[:, oc, :], op=ALU.mult),
                                  reads=[f"ps{bk}", "szb"], writes=["tB"])
                            fw.op(dve, lambda oc=oc: V.tensor_tensor(out=mrg[:, oc, :], in0=tB[:], in1=tA[:, oc, :], op=ALU.add),
                                  reads=["tB", "tA"], writes=[f"QT{oc}"])
                    dump("mrg", QT, [], si == 0 and it == 0)
                    junk = sbuf(ph, "junk2", [128, D], BF16)
                    xres2 = [sbuf(ph, f"xres{i}", [128, D], F32) for i in range(2)]
                    ty2 = [sbuf(ph, f"ty{i}", [128, D], F32) for i in range(2)]
                    x2n2 = [sbuf(ph, f"x2n{i}", [128, D], BF16) for i in range(2)]
                    x1v = tA[:].rearrange("p a c -> p (a c)").rearrange("p (s d) -> p s d", s=4)
                    rwo = [load_slot(S_WO + hf) for hf in range(2)]

                    def stage_a(s):
                        tok = slice(s * 128, (s + 1) * 128)
                        b = s % 2
                        xres, ty = xres2[b], ty2[b]
                        fw.dma(sp, xres[:], xd[t0 + s * 128:t0 + (s + 1) * 128, :], f"d_xres{b}", writes=[xres.name], phase_local=True)
                        yb = (0, 2, 4)[s % 3]
                        for hf in range(2):
                            wt = ring[rwo[hf]][:].rearrange("p (k c) -> p k c", k=8)
                            fw.op(pe, [mm(ps[:, yb + hf, :], mrg[:, kc, tok], wt[:, kc, :], kc == 0, kc == 7) for kc in range(8)],
                                  reads=[f"QT{c_}" for c_ in range(8)] + [f"ring{rwo[hf]}"], writes=[f"ps{yb + hf}"])
                        fw.op(act, lambda: S.activation(out=junk[:].rearrange("p (a c) -> p a c", a=2), in_=ps[:, yb:yb + 2, :], func=AF.Square,
                                                        accum_out=stat[:, 16 + s:17 + s]),
                              reads=[f"ps{yb}", f"ps{yb + 1}"], writes=[f"t1ss{s}", "junk"])
                        fw.op(act, lambda: S.activation(out=stat[:, 20 + s:21 + s], in_=stat[:, 16 + s:17 + s], func=AF.Ln, bias=epsc[:, 0:1], scale=1.0 / D),
                              reads=[f"t1ss{s}", "epsc"], writes=[f"t1ln{s}"])
                        fw.op(act, lambda: S.activation(out=stat[:, 24 + s:25 + s], in_=stat[:, 20 + s:21 + s], func=AF.Exp, scale=-0.5),
                              reads=[f"t1ln{s}"], writes=[f"t1rs{s}"])
                        fw.op(dve, lambda: V.scalar_tensor_tensor(out=ty[:].rearrange("p (a c) -> p a c", a=2), in0=ps[:, yb:yb + 2, :], scalar=stat[:, 24 + s:25 + s],
                                                                 in1=gpost[:].rearrange("p (a c) -> p a c", a=2), op0=ALU.mult, op1=ALU.mult),
                              reads=[f"ps{yb}", f"ps{yb + 1}", f"t1rs{s}", "gpost"], writes=[ty.name])
                        fw.op(dve, lambda: V.tensor_tensor(out=x1v[:, s, :], in0=ty[:], in1=xres[:], op=ALU.add), reads=[ty.name, xres.name], writes=[f"x1_{s}", "tA"])

                    def stage_b(s):
                        tok = slice(s * 128, (s + 1) * 128)
                        b = s % 2
                        x2n = x2n2[b]
                        tb = 6 + b
                        fw.op(act, lambda: S.activation(out=junk[:], in_=x1v[:, s, :], func=AF.Square, accum_out=stat[:, 28 + s:29 + s]),
                              reads=[f"x1_{s}"], writes=[f"t2ss{s}", "junk"])
                        fw.op(act, lambda: S.activation(out=stat[:, 32 + s:33 + s], in_=stat[:, 28 + s:29 + s], func=AF.Ln, bias=epsc[:, 0:1], scale=1.0 / D),
                              reads=[f"t2ss{s}", "epsc"], writes=[f"t2ln{s}"])
                        fw.op(act, lambda: S.activation(out=stat[:, 36 + s:37 + s], in_=stat[:, 32 + s:33 + s], func=AF.Exp, scale=-0.5),
                              reads=[f"t2ln{s}"], writes=[f"t2rs{s}"])
                        fw.op(dve, lambda: V.tensor_scalar(out=x2n[:], in0=x1v[:, s, :], scalar1=stat[:, 36 + s:37 + s], scalar2=None, op0=ALU.mult),
                              reads=[f"x1_{s}", f"t2rs{s}"], writes=[x2n.name])

                    def stage_btr(s):
                        x2n = x2n2[s % 2]
                        tb = 6 + (s % 2)
                        fw.op(pe, [tr(pstb[tb][:, kc * 128:(kc + 1) * 128], x2n[:, kc * 128:(kc + 1) * 128]) for kc in range(8)],
                              reads=[x2n.name, "cmat"], writes=[f"ps{tb}"])

                    def stage_b2(s):
                        tok = slice(s * 128, (s + 1) * 128)
                        tb = 6 + (s % 2)
                        fw.op(act, lambda: S.copy(out=szb[:, :, tok], in_=pstb[tb].rearrange("p (k c) -> p k c", k=8)), reads=[f"ps{tb}"], writes=["szb"])

                    nxt = it + 1 < NST
                    if nxt:
                        hnb = [sbuf(ph, f"hnb{i}", [128, D], BF16) for i in range(2)]
                        mh_stats(junk)
                        mh_scale(0, hnb[0])
                        mh_scale(1, hnb[1])
                    stage_a(0)
                    stage_a(1)
                    stage_b(0)
                    stage_a(2)
                    if nxt:
                        mh_tr_copy(0, hnb[0])
                        mh_tr_copy(1, hnb[1])
                        mh_scale(2, hnb[0])
                        mh_scale(3, hnb[1])
                    stage_btr(0)
                    stage_b(1)
                    stage_a(3)
                    if nxt:
                        mh_tr_copy(2, hnb[0], 2)
                        mh_tr_copy(3, hnb[1], 3)
                    stage_btr(1)
                    stage_b2(0)
                    stage_b(2)
                    stage_btr(2)
                    stage_b2(1)
                    stage_b(3)
                    stage_btr(3)
                    stage_b2(2)
                    stage_b2(3)
                    fw.barrier()
                    dump("x1", tA, [], si == 0 and it == 0)
                    dump("x2nT", szb, [], si == 0 and it == 0)
                    dump("pTt", pTt, [], si == 0 and it == 0)
                if stop == 6:
                    return nc
                with ExitStack() as ph:
                    sg2 = [sbuf(ph, f"sg{i}", [128, ST], F32) for i in range(2)]
                    tE2 = [sbuf(ph, f"tE{i}", [128, ST], F32) for i in range(2)]
                    outb = [sbuf(ph, f"outb{i}", [128, D], F32) for i in range(2)]
                    x1v = tA[:].rearrange("p a c -> p (a c)").rearrange("p (s d) -> p s d", s=4)
                    rwp = load_slot(S_WPLE, half=True)
                    wpl = ring[rwp][:, 0:2048].rearrange("p (k c) -> p k c", k=2)
                    rwg = [load_slot(S_WG + hf) for hf in range(2)]
                    for s in range(4):
                        tok = slice(s * 128, (s + 1) * 128)
                        ob_ = outb[s % 2]
                        for hf in range(2):
                            sg, tE = sg2[hf], tE2[hf]
                            wt = ring[rwg[hf]][:].rearrange("p (k c) -> p k c", k=8)
                            gb = next_bank([0, 1])
                            eb = next_bank([2, 3])
                            fw.op(pe, [mm(ps[:, gb, :], szb[:, kc, tok], wt[:, kc, :], kc == 0, kc == 7) for kc in range(8)],
                                  reads=["szb", f"ring{rwg[hf]}"], writes=[f"ps{gb}"])
                            fw.op(pe, [mm(ps[:, eb, :], pTt[:, k2, tok], wpl[:, k2, hf * 512:(hf + 1) * 512], k2 == 0, k2 == 1) for k2 in range(2)],
                                  reads=["pTt", f"ring{rwp}"], writes=[f"ps{eb}"])
                            fw.op(act, lambda gb=gb, sg=sg: S.activation(out=sg[:], in_=ps[:, gb, :], func=AF.Sigmoid), reads=[f"ps{gb}"], writes=[sg.name])
                            fw.op(dve, lambda eb=eb, sg=sg, tE=tE: V.tensor_tensor(out=tE[:], in0=ps[:, eb, :], in1=sg[:], op=ALU.mult), reads=[f"ps{eb}", sg.name], writes=[tE.name])
                            fw.op(pool, lambda s=s, hf=hf, ob_=ob_, tE=tE: G.tensor_tensor(out=ob_[:, hf * 512:(hf + 1) * 512], in0=tE[:], in1=x1v[:, s, hf * 512:(hf + 1) * 512], op=ALU.add),
                                  reads=[tE.name, f"x1_{s}"], writes=[ob_.name])
                        fw.dma(sp, yd[t0 + s * 128:t0 + (s + 1) * 128, :], ob_[:], f"d_{ob_.name}", reads=[ob_.name], phase_local=True)
                    fw.barrier()
        fw.barrier(sp=True)
    return nc


_CACHE = {}


def _run(x_list, p_list, inp, n_cores):
    seqs = tuple(int(x.shape[1]) for x in x_list)
    tmax = max(seqs)
    key = (seqs, tmax)
    if key not in _CACHE:
        import os
        _CACHE[key] = build_program(list(seqs), tmax, stop=int(os.environ.get("KSTOP", "0")))
    nc = _CACHE[key]
    slots, wkv, gains, gpost = pack_weights(inp)
    cm, cs, edge = make_consts(tmax)
    in_maps = []
    for c in range(n_cores):
        m = {"wslots": slots, "wkv": wkv, "gains": gains, "gpost": gpost, "cmat": cm, "cs": cs, "edge": edge}
        for i in range(len(seqs)):
            m[f"x{i}"] = np.ascontiguousarray(x_list[i][c], dtype=np.float32)
            m[f"p{i}"] = np.ascontiguousarray(p_list[i][c], dtype=np.float32)
        in_maps.append(m)
    res = run_bass_kernel_spmd(nc, in_maps, core_ids=list(range(n_cores)))
    global LAST_RES
    LAST_RES = res
    outs = []
    for i in range(len(seqs)):
        outs.append(np.stack([np.asarray(res.results[c][f"y{i}"], dtype=np.float32) for c in range(n_cores)], axis=0))
    return outs


def kernel(**inputs):
    xp = np.asarray(inputs["x_prompt"], np.float32)
    xs = np.asarray(inputs["x_sample"], np.float32)
    pp = np.asarray(inputs["p_prompt"], np.float32)[0]
    psm = np.asarray(inputs["p_sample"], np.float32)[0]
    outs = _run([xp, xs], [pp, psm], inputs, 8)
    return (outs[0], outs[1])
```

```python
import numpy as np
from contextlib import ExitStack
import concourse.bass as bass
import concourse.mybir as mybir
from concourse.bass_utils import run_bass_kernel_spmd

F32 = mybir.dt.float32
BF16 = mybir.dt.bfloat16
AF = mybir.ActivationFunctionType
ALU = mybir.AluOpType

D = 1024
PLE = 256
EPS = 1e-6
ST = 512
NSLOT = 22
(S_UAZA, S_POOL, S_MA, S_WA, S_Q, S_ZB, S_MB, S_WB, S_WO, S_WG, S_WPLE) = (0, 4, 5, 7, 9, 11, 13, 15, 17, 19, 21)
OFF_UA, OFF_ZA, OFF_Q, OFF_K, OFF_V, OFF_ZB, OFF_MA, OFF_MB = 0, 1024, 2048, 3072, 3328, 3584, 4608, 5632
POOL_W = (2, 4, 8, 16)


class _Res:
    __slots__ = ("w", "r")

    def __init__(self):
        self.w = None
        self.r = {}


class _Eng:
    def __init__(self, fw, name, h, is_pe=False):
        self.name = name
        self.h = h
        self.is_pe = is_pe
        self.sem = fw.new_sem("e_" + name)
        self.count = 0
        self.waited = {}


class FW:
    def __init__(self, nc, es):
        self.nc = nc
        self.es = es
        self.sems = {}
        self.pe = _Eng(self, "pe", nc.tensor, is_pe=True)
        self.act = _Eng(self, "act", nc.scalar)
        self.dve = _Eng(self, "dve", nc.vector)
        self.pool = _Eng(self, "pool", nc.gpsimd)
        self.sp = _Eng(self, "sp", nc.sync)
        self.engs = [self.pe, self.act, self.dve, self.pool, self.sp]
        self.res = {}
        self.dma_cnt = {}
        self.phase_toks = {}

    def new_sem(self, name):
        self.sems[name] = self.es.enter_context(self.nc.semaphore(name))
        return name

    def R(self, key):
        r = self.res.get(key)
        if r is None:
            r = _Res()
            self.res[key] = r
        return r

    def _wait(self, eng, tok):
        sk, val = tok
        if eng.waited.get(sk, 0) >= val:
            return
        eng.h.wait_ge(self.sems[sk], val)
        eng.waited[sk] = val

    def _deps(self, reads, writes):
        deps = []
        for k in reads:
            r = self.R(k)
            if r.w is not None:
                deps.append(r.w)
        for k in writes:
            r = self.R(k)
            if r.w is not None:
                deps.append(r.w)
            deps.extend(r.r.items())
        return deps

    def _record(self, tok, reads, writes):
        for k in reads:
            r = self.R(k)
            if r.r.get(tok[0], 0) < tok[1]:
                r.r[tok[0]] = tok[1]
        for k in writes:
            r = self.R(k)
            r.w = tok
            r.r = {}

    def op(self, eng, fns, reads=(), writes=()):
        if callable(fns):
            fns = [fns]
        writes = list(writes) + [k for k in reads if k.startswith("ps")]
        reads = [k for k in reads if not k.startswith("ps")]
        for tok in self._deps(reads, writes):
            if eng.is_pe and tok[0] == eng.sem:
                continue
            self._wait(eng, tok)
        inst = None
        for f in fns:
            inst = f()
        eng.count += 1
        inst.then_inc(self.sems[eng.sem], 1)
        tok = (eng.sem, eng.count)
        self._record(tok, reads, writes)
        return tok

    def dma(self, eng, out, in_, sem, reads=(), writes=(), phase_local=False, **kw):
        if sem not in self.sems:
            self.new_sem(sem)
            self.dma_cnt[sem] = 0
        if phase_local:
            for tok in self.phase_toks.items():
                self._wait(eng, tok)
        for tok in self._deps(reads, writes):
            self._wait(eng, tok)
        self.dma_cnt[sem] += 16
        eng.h.dma_start(out=out, in_=in_, **kw).then_inc(self.sems[sem], 16)
        tok = (sem, self.dma_cnt[sem])
        self._record(tok, reads, writes)
        return tok

    def barrier(self, engs=None, sp=False):
        last = {}
        for r in self.res.values():
            toks = list(r.r.items())
            if r.w is not None:
                toks.append(r.w)
            for sk, v in toks:
                if last.get(sk, 0) < v:
                    last[sk] = v
        if engs is None:
            engs = [self.pe, self.act, self.dve, self.pool] + ([self.sp] if sp else [])
        for e in engs:
            for sk, v in last.items():
                self._wait(e, (sk, v))
        self.phase_toks = dict(last)


def _fm_block(W, cols):
    return np.ascontiguousarray(W[:, cols].reshape(8, 128, 128).transpose(1, 0, 2))


def _fm_slot(W, col_lists):
    return np.stack([_fm_block(W, c) for c in col_lists], axis=1).reshape(128, 4096)


def _tm_slot(W, hf):
    return np.ascontiguousarray(W[:, hf * 512:(hf + 1) * 512].reshape(8, 128, 512).transpose(1, 0, 2)).reshape(128, 4096)


def _head_cols(base, j, i):
    hA, hB = 8 * j + i, 8 * j + 4 + i
    return np.concatenate([base + hA * 64 + np.arange(64), base + hB * 64 + np.arange(64)])


def pack_weights(inp):
    w_in = np.asarray(inp["w_in"][0], np.float32)
    slots = np.zeros((NSLOT, 128, 4096), np.float32)
    ar = np.arange(128)
    for g in range(4):
        slots[S_UAZA + g] = _fm_slot(w_in, [OFF_UA + (2 * g) * 128 + ar, OFF_UA + (2 * g + 1) * 128 + ar,
                                            OFF_ZA + (2 * g) * 128 + ar, OFF_ZA + (2 * g + 1) * 128 + ar])
    pw = np.asarray(inp["pool_w"][0], np.float32).reshape(4, 2, 128, 2, 128)
    slots[S_POOL, :, :2048] = pw.transpose(2, 0, 3, 1, 4).reshape(128, 2048)
    w_a = np.asarray(inp["w_branch_a"][0], np.float32)
    w_b = np.asarray(inp["w_branch_b"][0], np.float32)
    rperm = np.concatenate([_head_cols(0, j, i) for j in range(2) for i in range(4)])
    w_bp = w_b[rperm, :]
    w_o = np.asarray(inp["w_out"][0], np.float32)
    w_g = np.asarray(inp["w_ple_gate"][0], np.float32)
    for h in range(2):
        slots[S_MA + h] = _fm_slot(w_in, [OFF_MA + (4 * h + o) * 128 + ar for o in range(4)])
        slots[S_MB + h] = _fm_slot(w_in, [OFF_MB + (4 * h + o) * 128 + ar for o in range(4)])
        slots[S_WA + h] = _fm_slot(w_a, [(4 * h + o) * 128 + ar for o in range(4)])
        slots[S_WB + h] = _fm_slot(w_bp, [(4 * h + o) * 128 + ar for o in range(4)])
        slots[S_Q + h] = _fm_slot(w_in, [_head_cols(OFF_Q, h, i) for i in range(4)])
        slots[S_ZB + h] = _fm_slot(w_in, [_head_cols(OFF_ZB, h, i) for i in range(4)])
        slots[S_WO + h] = _tm_slot(w_o, h)
        slots[S_WG + h] = _tm_slot(w_g, h)
    wp = np.asarray(inp["w_ple_in"][0], np.float32)
    slots[S_WPLE, :, :2048] = wp.reshape(2, 128, 1024).transpose(1, 0, 2).reshape(128, 2048)
    wkv = np.zeros((128, 4096), np.float32)
    wkv[:, 0:2048] = np.stack([_fm_block(w_in, OFF_K + j * 128 + ar) for j in range(2)], axis=1).reshape(128, 2048)
    wkv[:, 2048:4096] = w_in[:, OFF_V:OFF_V + 256].reshape(8, 128, 256).transpose(1, 0, 2).reshape(128, 2048)
    gains = np.zeros((128, 32), np.float32)
    gains[:, 0:8] = np.asarray(inp["norm_pre"][0]).reshape(8, 128).T
    gains[:, 8:16] = np.asarray(inp["ple_norm"][0]).reshape(8, 128).T
    gains[:, 16:24] = np.asarray(inp["pool_scale"][0]).reshape(8, 128).T
    gains[:, 24] = np.tile(np.asarray(inp["q_norm"][0]), 2)
    gains[:, 25] = np.tile(np.asarray(inp["k_norm"][0]), 2)
    return slots, wkv, gains, np.ascontiguousarray(np.asarray(inp["norm_post"][0], np.float32))


def make_consts(tmax):
    cm = np.zeros((128, 4, 128), np.float32)
    cm[:, 0, :] = np.eye(128)
    for h in range(2):
        cm[h * 64:(h + 1) * 64, 1, h * 64:(h + 1) * 64] = 1.0
    Rm = np.zeros((64, 64), np.float32)
    for a in range(2):
        for i in range(16):
            Rm[a * 32 + i, a * 32 + i + 16] = -1.0
            Rm[a * 32 + i + 16, a * 32 + i] = 1.0
    R2 = np.zeros((128, 128), np.float32)
    R2[0:64, 0:64] = Rm
    R2[64:128, 64:128] = Rm
    cm[:, 2, :] = R2.T
    cm[:, 3, :] = 1.0
    t = np.arange(tmax)
    f = np.arange(128) % 64
    idx = (f % 32) % 16
    freq = (10000.0 ** (-(2.0 * idx) / 32.0)).astype(np.float32)
    pos = np.where((f < 32)[:, None], (t // 64)[None, :], (t % 64)[None, :]).astype(np.float32)
    ang = pos * freq[:, None]
    cs = np.stack([np.cos(ang), np.sin(ang)], axis=1).astype(np.float32)
    edge = np.zeros((128, 2, 4, 8), np.float32)
    for g, w in enumerate(POOL_W):
        for i in range(8):
            edge[:, 0, g, i] = 1.0 / (min(i + w // 2, 1 << 30) - max(i - w // 2, 0))
            edge[:, 1, g, i] = 1.0 / (min(w // 2, 8 - i) + w // 2)
    return cm.reshape(128, 512), np.ascontiguousarray(cs), edge.reshape(128, 64)


class _Stop(Exception):
    pass


def build_program(seqs, tmax, stop=0):
    nc = bass.Bass("TRN2", target_bir_lowering=False)
    nseq = len(seqs)
    x_d = [nc.dram_tensor(f"x{i}", [T, D], F32, kind="ExternalInput").ap() for i, T in enumerate(seqs)]
    p_d = [nc.dram_tensor(f"p{i}", [T, PLE], F32, kind="ExternalInput").ap() for i, T in enumerate(seqs)]
    y_d = [nc.dram_tensor(f"y{i}", [T, D], F32, kind="ExternalOutput").ap() for i, T in enumerate(seqs)]
    wsl_d = nc.dram_tensor("wslots", [NSLOT, 128, 4096], F32, kind="ExternalInput").ap()
    wkv_d = nc.dram_tensor("wkv", [128, 4096], F32, kind="ExternalInput").ap()
    gains_d = nc.dram_tensor("gains", [128, 32], F32, kind="ExternalInput").ap()
    gpost_d = nc.dram_tensor("gpost", [D], F32, kind="ExternalInput").ap()
    cmat_d = nc.dram_tensor("cmat", [128, 512], F32, kind="ExternalInput").ap()
    cs_d = nc.dram_tensor("cs", [128, 2, tmax], F32, kind="ExternalInput").ap()
    edge_d = nc.dram_tensor("edge", [128, 64], F32, kind="ExternalInput").ap()
    wbf_d = nc.dram_tensor("wbf", [NSLOT, 128, 4096], BF16, kind="Internal").ap()
    wkvbf_d = nc.dram_tensor("wkvbf", [128, 4096], BF16, kind="Internal").ap()
    TM = max(seqs)
    NKM = TM // 128

    with ExitStack() as es:
        fw = FW(nc, es)
        pe, act, dve, pool, sp = fw.pe, fw.act, fw.dve, fw.pool, fw.sp
        V, S, G = nc.vector, nc.scalar, nc.gpsimd

        uid = [0]

        def sbuf(stack, name, shape, dt):
            uid[0] += 1
            return stack.enter_context(nc.sbuf_tensor(f"s_{name}_{uid[0]}", shape, dt))

        KT = sbuf(es, "KT", [128, 2, TM], BF16)
        Vt = sbuf(es, "Vt", [128, NKM, 256], BF16)
        xt = sbuf(es, "xt", [128, 4, D], F32)
        hT = sbuf(es, "hT", [128, 8, ST], BF16)
        hTh = sbuf(es, "hTh", [128, 8, 16], BF16)
        QT = sbuf(es, "QT", [128, 8, ST], BF16)
        szb = sbuf(es, "szb", [128, 8, ST], BF16)
        pTt = sbuf(es, "pTt", [128, 2, ST], BF16)
        tA = sbuf(es, "tA", [128, 8, ST], F32)
        cs_t = sbuf(es, "cs_t", [128, 2, ST], F32)
        ring = [sbuf(es, f"ring{i}", [128, 4096], BF16) for i in range(3)]
        cmat = sbuf(es, "cmat", [128, 4, 128], BF16)
        gains = sbuf(es, "gains", [128, 32], F32)
        gpost = sbuf(es, "gpost", [128, D], F32)
        edge = sbuf(es, "edge", [128, 2, 4, 8], F32)
        epsc = sbuf(es, "epsc", [128, 2], F32)
        stat = sbuf(es, "stat", [128, 64], F32)
        ps = es.enter_context(nc.psum_tensor("ps", [128, 8, 512], F32))
        ident = cmat[:, 0, :]
        onesblk = cmat[:, 1, :]
        rmatT = cmat[:, 2, :]
        ones = cmat[:, 3, :]
        pst = ps[:, 7, :].bitcast(BF16)
        pstb = {6: ps[:, 6, :].bitcast(BF16), 7: pst}

        def mm(out, lhsT, rhs, start=True, stop=True):
            return lambda: nc.tensor.matmul(out, lhsT=lhsT, rhs=rhs, start=start, stop=stop)

        def tr(out, in_):
            return lambda: nc.tensor.transpose(out=out, in_=in_, identity=ident if in_.shape[0] == 128 else cmat[0:in_.shape[0], 0, 0:in_.shape[0]])

        dbg = {}

        def dump(name, t, key_list, cond=True):
            if not (stop == -1 and cond) or name in dbg:
                return
            shp = list(t.shape)
            d_ = nc.dram_tensor("dbg_" + name, shp, t.dtype, kind="ExternalOutput").ap()
            dbg[name] = d_
            fw.barrier(sp=True)
            fw.dma(sp, d_, t[:], "d_dbg_" + name, reads=key_list)
            fw.barrier(sp=True)

        bank_rr = [0]

        def next_bank(pool_banks):
            b = pool_banks[bank_rr[0] % len(pool_banks)]
            bank_rr[0] += 1
            return b

        ring_rr = [0]

        def load_slot(slot, half=False):
            r = ring_rr[0] % 3
            ring_rr[0] += 1
            n = 2048 if half else 4096
            fw.dma(sp, ring[r][:, 0:n], wbf_d[slot, :, 0:n], f"d_ring{r}", reads=[f"wbf{slot}_{q}" for q in range(2 if half else 4)], writes=[f"ring{r}"])
            return r

        with ExitStack() as ph:
            stg = sbuf(ph, "c_stg", [128, 512], F32)
            fw.dma(sp, stg[:], cmat_d, "d_cstg", writes=["c_stg"])
            fw.dma(sp, gains[:], gains_d, "d_gains", writes=["gains"])
            fw.dma(sp, gpost[:], gpost_d.partition_broadcast(128), "d_gpost", writes=["gpost"])
            fw.dma(sp, edge[:].rearrange("p a g i -> p (a g i)"), edge_d, "d_edge", writes=["edge"])
            fw.op(dve, lambda: V.tensor_copy(out=cmat[:].rearrange("p a c -> p (a c)"), in_=stg[:]), reads=["c_stg"], writes=["cmat"])
            fw.op(dve, lambda: V.memset(epsc[:, 0:1], EPS), writes=["epsc"])
            fw.op(dve, lambda: V.memset(epsc[:, 1:2], 64.0 * EPS), writes=["epsc"])
            fw.barrier()
        if stop == 1:
            return nc

        GPRE, GPLE = 0, 8
        gain_slots = {}
        for g in range(4):
            gain_slots[S_UAZA + g] = ("fm", GPRE)
        for h_ in range(2):
            for s0 in (S_MA, S_Q, S_ZB, S_MB):
                gain_slots[s0 + h_] = ("fm", GPRE)
            gain_slots[S_WG + h_] = ("tm", GPLE)
        prep_state = {"u": 0, "sf": None, "sbb": None}

        def prep_unit(src_ap, dst_ap, dst_key, layout, gcol):
            u = prep_state["u"]
            prep_state["u"] += 1
            b = u % 2
            sf, sbb = prep_state["sf"][b], prep_state["sbb"][b]
            fw.dma(sp, sf[:], src_ap, f"d_wsf{b}", writes=[sf.name], phase_local=True)
            if layout is None:
                fw.op(act, lambda: S.copy(out=sbb[:], in_=sf[:]), reads=[sf.name], writes=[sbb.name])
            else:
                nk, w_, g0 = layout
                fw.op(dve, lambda: V.tensor_tensor(out=sbb[:].rearrange("p (k c) -> p k c", k=nk), in0=sf[:].rearrange("p (k c) -> p k c", k=nk),
                                                   in1=gains[:, gcol + g0:gcol + g0 + nk].unsqueeze(2).broadcast_to([128, nk, w_]), op=ALU.mult),
                      reads=[sf.name, "gains"], writes=[sbb.name])
            fw.dma(pool, dst_ap, sbb[:], f"d_wsb{b}", reads=[sbb.name], writes=[dst_key])

        prep_units = []
        for q4 in range(4):
            lay = (8, 128, 0) if q4 < 2 else (4, 256, 4 * (q4 - 2))
            prep_units.append(lambda q4=q4, lay=lay: prep_unit(wkv_d[:, q4 * 1024:(q4 + 1) * 1024], wkvbf_d[:, q4 * 1024:(q4 + 1) * 1024], f"wkvbf{q4}", lay, GPRE))
        N_WKV_UNITS = 4
        for s_ in range(NSLOT):
            lay0, gc = gain_slots.get(s_, (None, 0))
            nq = 2 if s_ in (S_POOL, S_WPLE) else 4
            for q4 in range(nq):
                if lay0 == "fm":
                    lay = (8, 128, 0)
                elif lay0 == "tm":
                    lay = (2, 512, 2 * q4)
                else:
                    lay = None
                prep_units.append(lambda s_=s_, q4=q4, lay=lay, gc=gc: prep_unit(wsl_d[s_, :, q4 * 1024:(q4 + 1) * 1024], wbf_d[s_, :, q4 * 1024:(q4 + 1) * 1024],
                                                                               f"wbf{s_}_{q4}", lay, gc))

        def rstd_from(ss_ap, out_ap, scale, eps_col, tmp_ap, rkeys, wkey):
            n_ = ss_ap.shape[0]
            fw.op(act, lambda: S.activation(out=tmp_ap, in_=ss_ap, func=AF.Ln, bias=epsc[0:n_, eps_col:eps_col + 1], scale=scale),
                  reads=rkeys + ["epsc"], writes=[wkey + "_t"])
            fw.op(act, lambda: S.activation(out=out_ap, in_=tmp_ap, func=AF.Exp, scale=-0.5), reads=[wkey + "_t"], writes=[wkey])

        def norm_transpose(x_ap, nrows, junk, hn, dst_fn, key):
            fw.op(act, lambda: S.activation(out=junk[0:nrows, :], in_=x_ap, func=AF.Square, accum_out=stat[0:nrows, 12:13]),
                  reads=[key], writes=["stat12", "junk"])
            rstd_from(stat[0:nrows, 12:13], stat[0:nrows, 14:15], 1.0 / D, 0, stat[0:nrows, 13:14], ["stat12"], "stat14")
            fw.op(dve, lambda: V.tensor_scalar(out=hn[0:nrows, :], in0=x_ap, scalar1=stat[0:nrows, 14:15], scalar2=None, op0=ALU.mult),
                  reads=[key, "stat14"], writes=[hn.name])
            fw.op(pe, [tr(pst[:, kc * 128:kc * 128 + nrows], hn[0:nrows, kc * 128:(kc + 1) * 128]) for kc in range(8)],
                  reads=[hn.name, "cmat"], writes=["ps7"])
            dst_fn()

        def qk_stage1(bank, gcol, tmp, c):
            sq, qg = tmp["sq"], tmp["qg"]
            if stop != 322:
                fw.op(act, lambda: S.activation(out=sq[:], in_=ps[:, bank, :], func=AF.Square), reads=[f"ps{bank}"], writes=[sq.name])
            if stop == 321:
                return
            fw.op(act, lambda: S.activation(out=qg[:], in_=ps[:, bank, :], func=AF.Copy, scale=gains[:, gcol:gcol + 1]),
                  reads=[f"ps{bank}", "gains"], writes=[qg.name])

        def qk_stage2(tmp, dst_ap, dst_key, bss, brq):
            sq, qg, t1, t2, rs = tmp["sq"], tmp["qg"], tmp["t1"], tmp["t2"], tmp["rs"]
            fw.op(pe, [mm(ps[:, bss, :], onesblk, sq[:])], reads=[sq.name, "cmat"], writes=[f"ps{bss}"])
            fw.op(pe, [mm(ps[:, brq, :], rmatT, qg[:])], reads=[qg.name, "cmat"], writes=[f"ps{brq}"])
            fw.op(pool, lambda: G.tensor_tensor(out=t1[:], in0=qg[:], in1=cs_t[:, 0, :], op=ALU.mult), reads=[qg.name, "cs_t"], writes=[t1.name])
            fw.op(dve, lambda: V.tensor_tensor(out=t2[:], in0=ps[:, brq, :], in1=cs_t[:, 1, :], op=ALU.mult), reads=[f"ps{brq}", "cs_t"], writes=[t2.name])
            fw.op(act, lambda: S.activation(out=rs[:], in_=ps[:, bss, :], func=AF.Ln, bias=epsc[:, 1:2], scale=1.0),
                  reads=[f"ps{bss}", "epsc"], writes=[rs.name])
            fw.op(dve, lambda: V.tensor_tensor(out=t1[:], in0=t1[:], in1=t2[:], op=ALU.add), reads=[t1.name, t2.name], writes=[t1.name])
            fw.op(act, lambda: S.activation(out=rs[:], in_=rs[:], func=AF.Exp, scale=-0.5), reads=[rs.name], writes=[rs.name])
            fw.op(dve, lambda: V.tensor_tensor(out=dst_ap, in0=t1[:], in1=rs[:], op=ALU.mult), reads=[t1.name, rs.name], writes=[dst_key])

        def alloc_qk_tmp(ph, tag):
            return dict(sq=sbuf(ph, f"sq{tag}", [128, ST], BF16), qg=sbuf(ph, f"qg{tag}", [128, ST], BF16),
                        t1=sbuf(ph, f"t1{tag}", [128, ST], F32), t2=sbuf(ph, f"t2{tag}", [128, ST], F32),
                        rs=sbuf(ph, f"rs{tag}", [128, ST], F32))

        LIN_BANKS = [0, 1, 2, 3]

        def load_x_tile(xd, t0):
            fw.dma(sp, xt[:], xd[t0:t0 + ST, :].rearrange("(s p) d -> p s d", p=128), "d_xt", writes=["xt"])

        def mh_stats(junk):
            for s in range(4):
                fw.op(act, lambda s=s: S.activation(out=junk[:], in_=xt[:, s, :], func=AF.Square, accum_out=stat[:, s:s + 1]),
                      reads=["xt"], writes=[f"st_ss{s}", "junk"])
            fw.op(act, lambda: S.activation(out=stat[:, 4:8], in_=stat[:, 0:4], func=AF.Ln, bias=epsc[:, 0:1], scale=1.0 / D),
                  reads=[f"st_ss{s}" for s in range(4)] + ["epsc"], writes=["st_ln"])
            fw.op(act, lambda: S.activation(out=stat[:, 8:12], in_=stat[:, 4:8], func=AF.Exp, scale=-0.5), reads=["st_ln"], writes=["st_rs"])

        def mh_scale(s, hn):
            fw.op(dve, lambda: V.tensor_scalar(out=hn[:], in0=xt[:, s, :], scalar1=stat[:, 8 + s:9 + s], scalar2=None, op0=ALU.mult),
                  reads=["xt", "st_rs"], writes=[hn.name])

        def mh_tr_copy(s, hn, tb=None, dst=None, dkey="hT"):
            if tb is None:
                tb = 6 + (s % 2)
            if dst is None:
                dst = hT
            pv_ = ps[:, tb, :].bitcast(BF16)
            fw.op(pe, [tr(pv_[:, kc * 128:(kc + 1) * 128], hn[:, kc * 128:(kc + 1) * 128]) for kc in range(8)],
                  reads=[hn.name, "cmat"], writes=[f"ps{tb}"])
            fw.op(act, lambda: S.copy(out=dst[:, :, s * 128:(s + 1) * 128], in_=pv_.rearrange("p (k c) -> p k c", k=8)),
                  reads=[f"ps{tb}"], writes=[dkey])

        def make_hT(junk, hn2, dst=None, dkey="hT"):
            mh_stats(junk)
            for s in range(4):
                mh_scale(s, hn2[s % 2])
                mh_tr_copy(s, hn2[s % 2], None, dst, dkey)

        for si, T in enumerate(seqs):
            NK = T // 128
            NST = T // ST
            xd, pd, yd = x_d[si], p_d[si], y_d[si]

            with ExitStack() as ph:
                junk = sbuf(ph, "junk", [128, D], BF16)
                hn2 = [sbuf(ph, f"hn{i}", [128, D], BF16) for i in range(2)]
                tmps = [alloc_qk_tmp(ph, f"a{i}") for i in range(2)]
                wkv = sbuf(ph, "wkv", [128, 4096], BF16)
                if si == 0:
                    prep_state["sf"] = [sbuf(ph, f"w_sf{i}", [128, 1024], F32) for i in range(2)]
                    prep_state["sbb"] = [sbuf(ph, f"w_sb{i}", [128, 1024], BF16) for i in range(2)]
                    for pu in prep_units[:N_WKV_UNITS]:
                        pu()
                    rest = prep_units[N_WKV_UNITS:]
                    per_it = -(-len(rest) // NST)
                fw.dma(sp, wkv[:], wkvbf_d, "d_wkv", reads=[f"wkvbf{q}" for q in range(4)], writes=["wkv"], phase_local=True)
                wk = wkv[:, 0:2048].rearrange("p (j k c) -> p j k c", j=2, k=8)
                wv = wkv[:, 2048:4096].rearrange("p (k c) -> p k c", k=8)
                hT2 = sbuf(ph, "hT2", [128, 8, ST], BF16)
                hoisted_first = [False]
                hbufs = [(hT, "hT"), (hT2, "hT2")]
                load_x_tile(xd, 0)
                make_hT(junk, hn2, hT, "hT")
                if NST > 1:
                    load_x_tile(xd, ST)
                for it in range(NST):
                    t0 = it * ST
                    hcur, hkey = hbufs[it % 2]
                    fw.dma(sp, cs_t[:], cs_d[:, :, t0:t0 + ST], "d_cs", writes=["cs_t"])
                    for j in range(2):
                        b = next_bank(LIN_BANKS)
                        fw.op(pe, [mm(ps[:, b, :], wk[:, j, kc, :], hcur[:, kc, :], kc == 0, kc == 7) for kc in range(8)],
                              reads=[hkey, "wkv"], writes=[f"ps{b}"])
                        qk_stage1(b, 25, tmps[j], j)
                    for s in range(4):
                        b = next_bank(LIN_BANKS)
                        fw.op(pe, [mm(ps[:, b, 0:256], hcur[:, kc, s * 128:(s + 1) * 128], wv[:, kc, :], kc == 0, kc == 7) for kc in range(8)],
                              reads=[hkey, "wkv"], writes=[f"ps{b}"])
                        kt = it * 4 + s
                        fw.op(dve, lambda b=b, kt=kt: V.tensor_copy(out=Vt[:, kt, :], in_=ps[:, b, 0:256]), reads=[f"ps{b}"], writes=["Vt"])
                    if it + 1 < NST:
                        hnx, hnk = hbufs[(it + 1) % 2]
                        make_hT(junk, hn2, hnx, hnk)
                        if it + 2 < NST:
                            load_x_tile(xd, t0 + 2 * ST)
                        elif NST % 2 == 0:
                            load_x_tile(xd, 0)
                    elif NST % 2 == 0 and NST > 1:
                        make_hT(junk, hn2, hT, "hT")
                        hoisted_first[0] = True
                    for j in range(2):
                        qk_stage2(tmps[j], KT[:, j, t0:t0 + ST], "KT", 4 + j, 6)
                    if si == 0:
                        for pu in rest[it * per_it:(it + 1) * per_it]:
                            pu()
                fw.barrier()
                dump("KT", KT, ["KT"], si == 0)
                dump("Vt", Vt, ["Vt"], si == 0)
            if stop == 3:
                return nc

            if not hoisted_first[0]:
                load_x_tile(xd, 0)
            fw.dma(sp, cs_t[:], cs_d[:, :, 0:ST], "d_cs", writes=["cs_t"])
            for it in range(NST):
                t0 = it * ST
                first, last = (it == 0), (it == NST - 1)
                with ExitStack() as ph:
                    junk = sbuf(ph, "junk", [128, D], BF16)
                    hn2 = [sbuf(ph, f"hn{i}", [128, D], BF16) for i in range(2)]
                    hn = hn2[0]
                    xh = sbuf(ph, "xh", [16, D], F32)
                    poolw_t = sbuf(ph, "poolw", [128, 2048], BF16)
                    pa = ExitStack()
                    uext = [sbuf(pa, f"uext{i}", [128, 2, ST + 16], F32) for i in range(2)]
                    pwa = sbuf(pa, "pwa", [128, 2, ST + 16], F32)
                    pwb = sbuf(pa, "pwb", [128, 2, ST + 16], F32)
                    pooled = [sbuf(pa, f"pooled{i}", [128, 2, ST], BF16) for i in range(2)]
                    sza = [sbuf(pa, f"sza{i}", [128, 2, ST], BF16) for i in range(2)]
                    sma = [sbuf(pa, f"sma{i}", [128, ST], BF16) for i in range(4)]
                    etmp = sbuf(pa, "etmp", [128, 2, 8], F32)
                    aT = szb

                    fw.op(dve, lambda: V.memset(xh[:], 0.0), writes=["xh"])
                    if not first:
                        fw.dma(sp, xh[0:8, :], xd[t0 - 8:t0, :], "d_xh", writes=["xh"], phase_local=True)
                    if not last:
                        fw.dma(sp, xh[8:16, :], xd[t0 + ST:t0 + ST + 8, :], "d_xh", writes=["xh"], phase_local=True)
                    if it == 0 and not hoisted_first[0]:
                        make_hT(junk, hn2)

                    def dsth():
                        fw.op(act, lambda: S.copy(out=hTh[:], in_=pst.rearrange("p (k c) -> p k c", k=8)[:, :, 0:16]), reads=["ps7"], writes=["hTh"])
                    norm_transpose(xh[:], 16, junk, hn, dsth, "xh")
                    dump("hT", hT, ["hT"], si == 0 and it == 0)
                    dump("hTh", hTh, ["hTh"], si == 0 and it == 0)

                    fw.dma(sp, poolw_t[:], wbf_d[S_POOL, :, 0:2048], "d_poolw", reads=[f"wbf{S_POOL}_0", f"wbf{S_POOL}_1"], writes=["poolw"], phase_local=True)
                    poolw = poolw_t[:].rearrange("p (g o k c) -> p g o k c", g=4, o=2, k=2)

                    def proj_a(g):
                        r = load_slot(S_UAZA + g)
                        wsl = ring[r][:].rearrange("p (o k c) -> p o k c", o=4, k=8)
                        ue, sz = uext[g % 2], sza[g % 2]
                        for b2 in range(2):
                            bk = next_bank(LIN_BANKS)
                            fw.op(pe, [mm(ps[:, bk, :], wsl[:, b2, kc, :], hT[:, kc, :], kc == 0, kc == 7) for kc in range(8)],
                                  reads=["hT", f"ring{r}"], writes=[f"ps{bk}"])
                            fw.op(act, lambda: S.copy(out=ue[:, b2, 8:8 + ST], in_=ps[:, bk, :]), reads=[f"ps{bk}"], writes=[ue.name])
                            bh = next_bank([4, 5])
                            fw.op(pe, [mm(ps[:, bh, 0:16], wsl[:, b2, kc, :], hTh[:, kc, :], kc == 0, kc == 7) for kc in range(8)],
                                  reads=["hTh", f"ring{r}"], writes=[f"ps{bh}"])
                            fw.op(dve, lambda: V.tensor_copy(out=ue[:, b2, 0:8], in_=ps[:, bh, 0:8]), reads=[f"ps{bh}"], writes=[ue.name])
                            fw.op(dve, lambda: V.tensor_copy(out=ue[:, b2, 8 + ST:16 + ST], in_=ps[:, bh, 8:16]), reads=[f"ps{bh}"], writes=[ue.name])
                        for b2 in range(2):
                            bk = next_bank(LIN_BANKS)
                            fw.op(pe, [mm(ps[:, bk, :], wsl[:, 2 + b2, kc, :], hT[:, kc, :], kc == 0, kc == 7) for kc in range(8)],
                                  reads=["hT", f"ring{r}"], writes=[f"ps{bk}"])
                            fw.op(act, lambda: S.activation(out=sz[:, b2, :], in_=ps[:, bk, :], func=AF.Silu), reads=[f"ps{bk}"], writes=[sz.name])

                    def pool_mix(g):
                        w = POOL_W[g]
                        ue, pl, sz = uext[g % 2], pooled[g % 2], sza[g % 2]
                        L = ST + 16
                        cur, step, k_ = ue, 1, 0
                        while step < w:
                            nxt = pwa if k_ % 2 == 0 else pwb
                            e, h = (pool, G) if k_ % 2 == 0 else (dve, V)
                            fw.op(e, lambda: h.tensor_tensor(out=nxt[:, :, 0:L - step], in0=cur[:, :, 0:L - step], in1=cur[:, :, step:L], op=ALU.add),
                                  reads=[cur.name], writes=[nxt.name])
                            L -= step
                            cur = nxt
                            step *= 2
                            k_ += 1
                        off = 8 - w // 2
                        fw.op(dve, lambda: V.scalar_tensor_tensor(out=pl[:], in0=cur[:, :, off:off + ST], scalar=1.0 / w, in1=ue[:, :, 8:8 + ST],
                                                                 op0=ALU.mult, op1=ALU.subtract),
                              reads=[cur.name, ue.name], writes=[pl.name])
                        for (is_edge, a_, j0) in ((first, 0, 0), (last, 1, ST - 8)):
                            if not is_edge:
                                continue
                            fw.op(dve, lambda: V.tensor_tensor(out=etmp[:], in0=cur[:, :, off + j0:off + j0 + 8],
                                                               in1=edge[:, a_, g, :].unsqueeze(1).broadcast_to([128, 2, 8]), op=ALU.mult),
                                  reads=[cur.name, "edge"], writes=["etmp"])
                            fw.op(dve, lambda: V.tensor_tensor(out=pl[:, :, j0:j0 + 8], in0=etmp[:], in1=ue[:, :, 8 + j0:16 + j0], op=ALU.subtract),
                                  reads=["etmp", ue.name, pl.name], writes=[pl.name])
                        for o2 in range(2):
                            bk = next_bank(LIN_BANKS)
                            fw.op(pe, [mm(ps[:, bk, :], poolw[:, g, o2, k2, :], pl[:, k2, :], k2 == 0, k2 == 1) for k2 in range(2)],
                                  reads=[pl.name, "poolw"], writes=[f"ps{bk}"])
                            oc = 2 * g + o2
                            fw.op(dve, lambda: V.scalar_tensor_tensor(out=aT[:, oc, :], in0=ps[:, bk, :], scalar=gains[:, 16 + oc:17 + oc], in1=sz[:, o2, :],
                                                                     op0=ALU.mult, op1=ALU.mult),
                                  reads=[f"ps{bk}", "gains", sz.name], writes=["szb"])

                    def ma_proj(w1, r1, o, oc):
                        sm = sma[oc % 4]
                        bk = next_bank(LIN_BANKS)
                        fw.op(pe, [mm(ps[:, bk, :], w1[:, o, kc, :], hT[:, kc, :], kc == 0, kc == 7) for kc in range(8)],
                              reads=["hT", f"ring{r1}"], writes=[f"ps{bk}"])
                        fw.op(act, lambda: S.activation(out=sm[:], in_=ps[:, bk, :], func=AF.Sigmoid), reads=[f"ps{bk}"], writes=[sm.name])

                    def wa_proj(w2, r2, o, oc):
                        sm = sma[oc % 4]
                        bk2 = next_bank(LIN_BANKS)
                        fw.op(pe, [mm(ps[:, bk2, :], w2[:, o, kc, :], aT[:, kc, :], kc == 0, kc == 7) for kc in range(8)],
                              reads=["szb", f"ring{r2}"], writes=[f"ps{bk2}"])
                        fw.op(dve, lambda: V.tensor_tensor(out=tA[:, oc, :], in0=ps[:, bk2, :], in1=sm[:], op=ALU.mult),
                              reads=[f"ps{bk2}", sm.name], writes=["tA"])

                    proj_a(0)
                    for g in range(3):
                        proj_a(g + 1)
                        pool_mix(g)
                    r1 = load_slot(S_MA + 0)
                    w1 = ring[r1][:].rearrange("p (o k c) -> p o k c", o=4, k=8)
                    for o in range(4):
                        ma_proj(w1, r1, o, o)
                    pool_mix(3)
                    r2 = load_slot(S_WA + 0)
                    w2 = ring[r2][:].rearrange("p (o k c) -> p o k c", o=4, k=8)
                    for o in range(4):
                        wa_proj(w2, r2, o, o)
                    r1 = load_slot(S_MA + 1)
                    r2 = load_slot(S_WA + 1)
                    w1 = ring[r1][:].rearrange("p (o k c) -> p o k c", o=4, k=8)
                    w2 = ring[r2][:].rearrange("p (o k c) -> p o k c", o=4, k=8)
                    for o in range(4):
                        ma_proj(w1, r1, o, 4 + o)
                        wa_proj(w2, r2, o, 4 + o)
                    dump("tA", tA, ["tA"], si == 0 and it == 0)
                    dump("aT", szb, ["szb"], si == 0 and it == 0)
                    fw.barrier()
                    pa.close()
                    tmps = [alloc_qk_tmp(ph, f"b{i}") for i in range(2)]
                    pend = None
                    cidx = 0
                    for j in range(2):
                        r = load_slot(S_Q + j)
                        wsl = ring[r][:].rearrange("p (o k c) -> p o k c", o=4, k=8)
                        for i in range(4):
                            c = 4 * j + i
                            bk = next_bank(LIN_BANKS)
                            fw.op(pe, [mm(ps[:, bk, :], wsl[:, i, kc, :], hT[:, kc, :], kc == 0, kc == 7) for kc in range(8)],
                                  reads=["hT", f"ring{r}"], writes=[f"ps{bk}"])
                            tm_ = tmps[cidx % 2]
                            if pend is not None:
                                qk_stage2(*pend)
                            qk_stage1(bk, 24, tm_, c)
                            pend = (tm_, QT[:, c, :], f"QT{c}", 4 + (cidx % 2), 6)
                            cidx += 1
                    qk_stage2(*pend)
                    for (s0, dstt, func) in ((S_ZB, szb, AF.Silu),):
                        for j in range(2):
                            r = load_slot(s0 + j)
                            wsl = ring[r][:].rearrange("p (o k c) -> p o k c", o=4, k=8)
                            for i in range(4):
                                c = 4 * j + i
                                bk = next_bank(LIN_BANKS)
                                fw.op(pe, [mm(ps[:, bk, :], wsl[:, i, kc, :], hT[:, kc, :], kc == 0, kc == 7) for kc in range(8)],
                                      reads=["hT", f"ring{r}"], writes=[f"ps{bk}"])
                                fw.op(act, lambda bk=bk, c=c, dstt=dstt, func=func: S.activation(out=dstt[:, c, :], in_=ps[:, bk, :], func=func),
                                      reads=[f"ps{bk}"], writes=["szb"])
                    fw.barrier()
                    dump("QT", QT, [], si == 0 and it == 0)
                    dump("szb", szb, [], si == 0 and it == 0)
                if stop == 4:
                    return nc

                if it + 1 < NST:
                    load_x_tile(xd, t0 + ST)
                    fw.dma(sp, cs_t[:], cs_d[:, :, t0 + ST:t0 + 2 * ST], "d_cs", writes=["cs_t"])
                with ExitStack() as ph:
                    bT = sbuf(ph, "bT", [128, 8, ST], BF16)
                    rinv = sbuf(ph, "rinv", [128, ST], F32)
                    wgt = sbuf(ph, "wgt", [128, ST], F32)
                    mrg = QT
                    smb = szb
                    tB = sbuf(ph, "tB", [128, ST], F32)
                    steps = [(c, kt) for c in range(8) for kt in range(NK)]
                    ptile4 = [sbuf(ph, f"ptile{i}", [128, PLE], F32) for i in range(4)]
                    ptb4 = [sbuf(ph, f"ptb{i}", [128, PLE], BF16) for i in range(4)]
                    for s4 in range(4):
                        fw.dma(sp, ptile4[s4][:], pd[t0 + s4 * 128:t0 + (s4 + 1) * 128, :], f"d_ptile{s4}", writes=[ptile4[s4].name], phase_local=True)
                        fw.op(pool, lambda s4=s4: G.tensor_copy(out=ptb4[s4][:], in_=ptile4[s4][:]), reads=[ptile4[s4].name], writes=[ptb4[s4].name])
                    pa2 = ExitStack()
                    NPT = 6
                    PT = [sbuf(pa2, f"PT{i}", [128, 2, ST], BF16) for i in range(NPT)]
                    PS2 = [sbuf(pa2, f"PS2_{i}", [128, 2, ST], BF16) for i in range(2)]
                    PS4 = [sbuf(pa2, f"PS4_{i}", [128, 2, ST], BF16) for i in range(2)]
                    RSG = 4 if NK % 4 == 0 else (2 if NK % 2 == 0 else 1)

                    def emit_qk(n):
                        c, kt = steps[n]
                        j = c // 4
                        sb_ = 2 * (n % 2)
                        fw.op(pe, [mm(ps[:, sb_, :], KT[0:64, j, kt * 128:(kt + 1) * 128], QT[0:64, c, :]),
                                   mm(ps[:, sb_ + 1, :], KT[64:128, j, kt * 128:(kt + 1) * 128], QT[64:128, c, :])],
                              reads=["KT", f"QT{c}"], writes=[f"ps{sb_}", f"ps{sb_ + 1}"])

                    def emit_rs(c, kt, src):
                        rb = 5 + 2 * (c % 2)
                        st_, sp_ = (kt == RSG - 1), (kt == NK - 1)
                        fw.op(pe, [mm(ps[0:64, rb, :], ones[:, 0:64], src[:, 0, :], st_, sp_),
                                   mm(ps[64:128, rb, :], ones[:, 64:128], src[:, 1, :], st_, sp_)],
                              reads=[src.name, "cmat"], writes=[f"ps{rb}"])
                        if kt == NK - 1:
                            ob = rb - 1
                            fw.op(dve, lambda: V.reciprocal(out=rinv[:], in_=ps[:, rb, :]), reads=[f"ps{rb}"], writes=["rinv"])
                            fw.op(pool, lambda: G.tensor_tensor(out=wgt[:], in0=rinv[:], in1=szb[:, c, :], op=ALU.mult), reads=["rinv", "szb"], writes=["wgt"])

                            def fin():
                                fw.op(dve, lambda: V.tensor_tensor(out=bT[:, c, :], in0=ps[:, ob, :], in1=wgt[:], op=ALU.mult),
                                      reads=[f"ps{ob}", "wgt"], writes=["bT"])
                            deferred.append((cur_n[0] + 3, c, fin))

                    deferred = []
                    cur_n = [0]
                    npair = [0]
                    nquad = [0]
                    emit_qk(0)
                    emit_qk(1)
                    for n, (c, kt) in enumerate(steps):
                        j = c // 4
                        cur_n[0] = n
                        if kt == 0:
                            while any(d[1] <= c - 2 for d in deferred):
                                deferred.sort(key=lambda d: d[0])
                                i_ = next(i for i, d in enumerate(deferred) if d[1] <= c - 2)
                                deferred.pop(i_)[2]()
                        sb_ = 2 * (n % 2)
                        pt = PT[n % NPT]
                        fw.op(act, lambda sb_=sb_, pt=pt: S.activation(out=pt[:], in_=ps[:, sb_:sb_ + 2, :], func=AF.Exp, scale=8.0),
                              reads=[f"ps{sb_}", f"ps{sb_ + 1}"], writes=[pt.name])
                        if n + 2 < len(steps):
                            emit_qk(n + 2)
                        ob = 4 + 2 * (c % 2)
                        st_, sp_ = (kt == 0), (kt == NK - 1)
                        fw.op(pe, [mm(ps[0:64, ob, :], Vt[:, kt, 128 * j:128 * j + 64], pt[:, 0, :], st_, sp_),
                                   mm(ps[64:128, ob, :], Vt[:, kt, 128 * j + 64:128 * j + 128], pt[:, 1, :], st_, sp_)],
                              reads=[pt.name, "Vt"], writes=[f"ps{ob}"])
                        if RSG == 1:
                            deferred.append((n, c, lambda c=c, kt=kt, pt=pt: emit_rs(c, kt, pt)))
                        else:
                            if kt % 2 == 1:
                                p2 = PS2[npair[0] % 2]
                                npair[0] += 1
                                pprev = PT[(n - 1) % NPT]
                                fw.op(dve, lambda p2=p2, pprev=pprev, pt=pt: V.tensor_tensor(out=p2[:], in0=pprev[:], in1=pt[:], op=ALU.add),
                                      reads=[pprev.name, pt.name], writes=[p2.name])
                                if RSG == 2:
                                    deferred.append((n + 4, c, lambda c=c, kt=kt, p2=p2: emit_rs(c, kt, p2)))
                            if RSG == 4 and kt % 4 == 3:
                                p4 = PS4[nquad[0] % 2]
                                nquad[0] += 1
                                fw.op(dve, lambda p4=p4: V.tensor_tensor(out=p4[:], in0=PS2[0][:], in1=PS2[1][:], op=ALU.add),
                                      reads=[PS2[0].name, PS2[1].name], writes=[p4.name])
                                deferred.append((n + 4, c, lambda c=c, kt=kt, p4=p4: emit_rs(c, kt, p4)))
                        deferred.sort(key=lambda d: d[0])
                        while deferred and deferred[0][0] <= n:
                            deferred.pop(0)[2]()
                            deferred.sort(key=lambda d: d[0])
                    def mb_proj(wsl, r, i):
                        bk = next_bank(LIN_BANKS)
                        fw.op(pe, [mm(ps[:, bk, :], wsl[:, i, kc, :], hT[:, kc, :], kc == 0, kc == 7) for kc in range(8)],
                              reads=["hT", f"ring{r}"], writes=[f"ps{bk}"])
                        return bk

                    def mb_act(bk, c):
                        fw.op(act, lambda: S.activation(out=smb[:, c, :], in_=ps[:, bk, :], func=AF.Sigmoid), reads=[f"ps{bk}"], writes=["szb"])

                    r_mb0 = load_slot(S_MB + 0)
                    wsl0 = ring[r_mb0][:].rearrange("p (o k c) -> p o k c", o=4, k=8)
                    mb_banks = [mb_proj(wsl0, r_mb0, i) for i in range(4)]
                    cur_n[0] = len(steps) + 10
                    while deferred:
                        deferred.sort(key=lambda d: d[0])
                        deferred.pop(0)[2]()
                    fw.barrier()
                    pa2.close()
                    dump("bT", bT, ["bT"], si == 0 and it == 0)
                    if stop == 5:
                        fw.barrier()
                        return nc
                    for i in range(4):
                        mb_act(mb_banks[i], i)
                    for s4 in range(4):
                        pb = 0 + (s4 % 2)
                        pv_ = ps[:, pb, :].bitcast(BF16)
                        fw.op(pe, [tr(pv_[:, k2 * 128:(k2 + 1) * 128], ptb4[s4][:, k2 * 128:(k2 + 1) * 128]) for k2 in range(2)],
                              reads=[ptb4[s4].name, "cmat"], writes=[f"ps{pb}"])
                        fw.op(dve, lambda s4=s4, pv_=pv_: V.tensor_copy(out=pTt[:, :, s4 * 128:(s4 + 1) * 128], in_=pv_[:, 0:256].rearrange("p (k c) -> p k c", k=2)),
                              reads=[f"ps{pb}"], writes=["pTt"])
                    r = load_slot(S_MB + 1)
                    wsl = ring[r][:].rearrange("p (o k c) -> p o k c", o=4, k=8)
                    for i in range(4):
                        mb_act(mb_proj(wsl, r, i), 4 + i)
                    for h_ in range(2):
                        r = load_slot(S_WB + h_)
                        wsl = ring[r][:].rearrange("p (o k c) -> p o k c", o=4, k=8)
                        for o in range(4):
                            oc = 4 * h_ + o
                            bk = next_bank(LIN_BANKS)
                            fw.op(pe, [mm(ps[:, bk, :], wsl[:, o, kc, :], bT[:, kc, :], kc == 0, kc == 7) for kc in range(8)],
                                  reads=["bT", f"ring{r}"], writes=[f"ps{bk}"])
                            fw.op(dve, lambda bk=bk, oc=oc: V.tensor_tensor(out=tB[:], in0=ps[:, bk, :], in1=smb[:, oc, :], op=ALU.mult),
                                  reads=[f"ps{bk}", "szb"], writes=["tB"])
                            fw.op(dve, lambda oc=oc: V.tensor_tensor(out=mrg[:, oc, :], in0=tB[:], in1=tA[:, oc, :], op=ALU.add),
                                  reads=["tB", "tA"], writes=[f"QT{oc}"])
                    dump("mrg", QT, [], si == 0 and it == 0)
                    junk = sbuf(ph, "junk2", [128, D], BF16)
                    xres2 = [sbuf(ph, f"xres{i}", [128, D], F32) for i in range(2)]
                    ty2 = [sbuf(ph, f"ty{i}", [128, D], F32) for i in range(2)]
                    x2n2 = [sbuf(ph, f"x2n{i}", [128, D], BF16) for i in range(2)]
                    x1v = tA[:].rearrange("p a c -> p (a c)").rearrange("p (s d) -> p s d", s=4)
                    rwo = [load_slot(S_WO + hf) for hf in range(2)]

                    def stage_a(s):
                        tok = slice(s * 128, (s + 1) * 128)
                        b = s % 2
                        xres, ty = xres2[b], ty2[b]
                        fw.dma(sp, xres[:], xd[t0 + s * 128:t0 + (s + 1) * 128, :], f"d_xres{b}", writes=[xres.name], phase_local=True)
                        yb = (0, 2, 4)[s % 3]
                        for hf in range(2):
                            wt = ring[rwo[hf]][:].rearrange("p (k c) -> p k c", k=8)
                            fw.op(pe, [mm(ps[:, yb + hf, :], mrg[:, kc, tok], wt[:, kc, :], kc == 0, kc == 7) for kc in range(8)],
                                  reads=[f"QT{c_}" for c_ in range(8)] + [f"ring{rwo[hf]}"], writes=[f"ps{yb + hf}"])
                        fw.op(act, lambda: S.activation(out=junk[:].rearrange("p (a c) -> p a c", a=2), in_=ps[:, yb:yb + 2, :], func=AF.Square,
                                                        accum_out=stat[:, 16 + s:17 + s]),
                              reads=[f"ps{yb}", f"ps{yb + 1}"], writes=[f"t1ss{s}", "junk"])
                        fw.op(act, lambda: S.activation(out=stat[:, 20 + s:21 + s], in_=stat[:, 16 + s:17 + s], func=AF.Ln, bias=epsc[:, 0:1], scale=1.0 / D),
                              reads=[f"t1ss{s}", "epsc"], writes=[f"t1ln{s}"])
                        fw.op(act, lambda: S.activation(out=stat[:, 24 + s:25 + s], in_=stat[:, 20 + s:21 + s], func=AF.Exp, scale=-0.5),
                              reads=[f"t1ln{s}"], writes=[f"t1rs{s}"])
                        fw.op(dve, lambda: V.scalar_tensor_tensor(out=ty[:].rearrange("p (a c) -> p a c", a=2), in0=ps[:, yb:yb + 2, :], scalar=stat[:, 24 + s:25 + s],
                                                                 in1=gpost[:].rearrange("p (a c) -> p a c", a=2), op0=ALU.mult, op1=ALU.mult),
                              reads=[f"ps{yb}", f"ps{yb + 1}", f"t1rs{s}", "gpost"], writes=[ty.name])
                        fw.op(dve, lambda: V.tensor_tensor(out=x1v[:, s, :], in0=ty[:], in1=xres[:], op=ALU.add), reads=[ty.name, xres.name], writes=[f"x1_{s}", "tA"])

                    def stage_b(s):
                        tok = slice(s * 128, (s + 1) * 128)
                        b = s % 2
                        x2n = x2n2[b]
                        tb = 6 + b
                        fw.op(act, lambda: S.activation(out=junk[:], in_=x1v[:, s, :], func=AF.Square, accum_out=stat[:, 28 + s:29 + s]),
                              reads=[f"x1_{s}"], writes=[f"t2ss{s}", "junk"])
                        fw.op(act, lambda: S.activation(out=stat[:, 32 + s:33 + s], in_=stat[:, 28 + s:29 + s], func=AF.Ln, bias=epsc[:, 0:1], scale=1.0 / D),
                              reads=[f"t2ss{s}", "epsc"], writes=[f"t2ln{s}"])
                        fw.op(act, lambda: S.activation(out=stat[:, 36 + s:37 + s], in_=stat[:, 32 + s:33 + s], func=AF.Exp, scale=-0.5),
                              reads=[f"t2ln{s}"], writes=[f"t2rs{s}"])
                        fw.op(dve, lambda: V.tensor_scalar(out=x2n[:], in0=x1v[:, s, :], scalar1=stat[:, 36 + s:37 + s], scalar2=None, op0=ALU.mult),
                              reads=[f"x1_{s}", f"t2rs{s}"], writes=[x2n.name])

                    def stage_btr(s):
                        x2n = x2n2[s % 2]
                        tb = 6 + (s % 2)
                        fw.op(pe, [tr(pstb[tb][:, kc * 128:(kc + 1) * 128], x2n[:, kc * 128:(kc + 1) * 128]) for kc in range(8)],
                              reads=[x2n.name, "cmat"], writes=[f"ps{tb}"])

                    def stage_b2(s):
                        tok = slice(s * 128, (s + 1) * 128)
                        tb = 6 + (s % 2)
                        fw.op(act, lambda: S.copy(out=szb[:, :, tok], in_=pstb[tb].rearrange("p (k c) -> p k c", k=8)), reads=[f"ps{tb}"], writes=["szb"])

                    nxt = it + 1 < NST
                    if nxt:
                        hnb = [sbuf(ph, f"hnb{i}", [128, D], BF16) for i in range(2)]
                        mh_stats(junk)
                        mh_scale(0, hnb[0])
                        mh_scale(1, hnb[1])
                    stage_a(0)
                    stage_a(1)
                    stage_b(0)
                    stage_a(2)
                    if nxt:
                        mh_tr_copy(0, hnb[0])
                        mh_tr_copy(1, hnb[1])
                        mh_scale(2, hnb[0])
                        mh_scale(3, hnb[1])
                    stage_btr(0)
                    stage_b(1)
                    stage_a(3)
                    if nxt:
                        mh_tr_copy(2, hnb[0], 2)
                        mh_tr_copy(3, hnb[1], 3)
                    stage_btr(1)
                    stage_b2(0)
                    stage_b(2)
                    stage_btr(2)
                    stage_b2(1)
                    stage_b(3)
                    stage_btr(3)
                    stage_b2(2)
                    stage_b2(3)
                    fw.barrier()
                    dump("x1", tA, [], si == 0 and it == 0)
                    dump("x2nT", szb, [], si == 0 and it == 0)
                    dump("pTt", pTt, [], si == 0 and it == 0)
                if stop == 6:
                    return nc
                with ExitStack() as ph:
                    sg2 = [sbuf(ph, f"sg{i}", [128, ST], F32) for i in range(2)]
                    tE2 = [sbuf(ph, f"tE{i}", [128, ST], F32) for i in range(2)]
                    outb = [sbuf(ph, f"outb{i}", [128, D], F32) for i in range(2)]
                    x1v = tA[:].rearrange("p a c -> p (a c)").rearrange("p (s d) -> p s d", s=4)
                    rwp = load_slot(S_WPLE, half=True)
                    wpl = ring[rwp][:, 0:2048].rearrange("p (k c) -> p k c", k=2)
                    rwg = [load_slot(S_WG + hf) for hf in range(2)]
                    for s in range(4):
                        tok = slice(s * 128, (s + 1) * 128)
                        ob_ = outb[s % 2]
                        for hf in range(2):
                            sg, tE = sg2[hf], tE2[hf]
                            wt = ring[rwg[hf]][:].rearrange("p (k c) -> p k c", k=8)
                            gb = next_bank([0, 1])
                            eb = next_bank([2, 3])
                            fw.op(pe, [mm(ps[:, gb, :], szb[:, kc, tok], wt[:, kc, :], kc == 0, kc == 7) for kc in range(8)],
                                  reads=["szb", f"ring{rwg[hf]}"], writes=[f"ps{gb}"])
                            fw.op(pe, [mm(ps[:, eb, :], pTt[:, k2, tok], wpl[:, k2, hf * 512:(hf + 1) * 512], k2 == 0, k2 == 1) for k2 in range(2)],
                                  reads=["pTt", f"ring{rwp}"], writes=[f"ps{eb}"])
                            fw.op(act, lambda gb=gb, sg=sg: S.activation(out=sg[:], in_=ps[:, gb, :], func=AF.Sigmoid), reads=[f"ps{gb}"], writes=[sg.name])
                            fw.op(dve, lambda eb=eb, sg=sg, tE=tE: V.tensor_tensor(out=tE[:], in0=ps[:, eb, :], in1=sg[:], op=ALU.mult), reads=[f"ps{eb}", sg.name], writes=[tE.name])
                            fw.op(pool, lambda s=s, hf=hf, ob_=ob_, tE=tE: G.tensor_tensor(out=ob_[:, hf * 512:(hf + 1) * 512], in0=tE[:], in1=x1v[:, s, hf * 512:(hf + 1) * 512], op=ALU.add),
                                  reads=[tE.name, f"x1_{s}"], writes=[ob_.name])
                        fw.dma(sp, yd[t0 + s * 128:t0 + (s + 1) * 128, :], ob_[:], f"d_{ob_.name}", reads=[ob_.name], phase_local=True)
                    fw.barrier()
        fw.barrier(sp=True)
    return nc


_CACHE = {}


def _run(x_list, p_list, inp, n_cores):
    seqs = tuple(int(x.shape[1]) for x in x_list)
    tmax = max(seqs)
    key = (seqs, tmax)
    if key not in _CACHE:
        import os
        _CACHE[key] = build_program(list(seqs), tmax, stop=int(os.environ.get("KSTOP", "0")))
    nc = _CACHE[key]
    slots, wkv, gains, gpost = pack_weights(inp)
    cm, cs, edge = make_consts(tmax)
    in_maps = []
    for c in range(n_cores):
        m = {"wslots": slots, "wkv": wkv, "gains": gains, "gpost": gpost, "cmat": cm, "cs": cs, "edge": edge}
        for i in range(len(seqs)):
            m[f"x{i}"] = np.ascontiguousarray(x_list[i][c], dtype=np.float32)
            m[f"p{i}"] = np.ascontiguousarray(p_list[i][c], dtype=np.float32)
        in_maps.append(m)
    res = run_bass_kernel_spmd(nc, in_maps, core_ids=list(range(n_cores)))
    global LAST_RES
    LAST_RES = res
    outs = []
    for i in range(len(seqs)):
        outs.append(np.stack([np.asarray(res.results[c][f"y{i}"], dtype=np.float32) for c in range(n_cores)], axis=0))
    return outs


def kernel(**inputs):
    xp = np.asarray(inputs["x_prompt"], np.float32)
    xs = np.asarray(inputs["x_sample"], np.float32)
    pp = np.asarray(inputs["p_prompt"], np.float32)[0]
    psm = np.asarray(inputs["p_sample"], np.float32)[0]
    outs = _run([xp, xs], [pp, psm], inputs, 8)
    return (outs[0], outs[1])
```
